# Optimizing a Trainium2 kernel written in Bass

```python
import jax
import jax.numpy as jnp
from jax import lax
import numpy as np


D_MODEL = 1024
BATCH = 32
SEQ = 2048
DEPTH = 4

MEM_LEN = 256
NORM_EPS = 1e-6
D_FF = 4 * D_MODEL
MASK_VALUE = -1e30
LOG_FLOOR = 1e-30

FOX_HEADS = 8
FOX_HEAD_DIM = 64
FOX_WIDTH = FOX_HEADS * FOX_HEAD_DIM
FOX_BLOCK = 128
FOX_FGATE_BIAS_CENTER = 2.0

HGRN_HEADS = 4
HGRN_KEY_DIM = 128
HGRN_VAL_DIM = 128
HGRN_KEY_WIDTH = HGRN_HEADS * HGRN_KEY_DIM
HGRN_VAL_WIDTH = HGRN_HEADS * HGRN_VAL_DIM
HGRN_CHUNK = 64

EVEN_PROJ = 3 * FOX_WIDTH + FOX_HEADS + 2 * HGRN_KEY_WIDTH + 2 * HGRN_VAL_WIDTH
EVEN_MIX_WIDTH = FOX_WIDTH + HGRN_VAL_WIDTH

SSM_INNER = 2 * D_MODEL
SSM_HEAD_DIM = 64
SSM_HEADS = SSM_INNER // SSM_HEAD_DIM
SSM_GROUPS = 8
SSM_HEADS_PER_GROUP = SSM_HEADS // SSM_GROUPS
SSM_STATE = 128
SSM_CONV = 4
SSM_CHUNK = 128
SSM_CONV_CH = SSM_INNER + 2 * SSM_GROUPS * SSM_STATE
SSM_PROJ = SSM_INNER + SSM_CONV_CH + SSM_HEADS
SSM_DT_MIN = 1e-3
SSM_DT_MAX = 1e-1

XATTN_HEADS = 4
XATTN_HEAD_DIM = 128
XATTN_WIDTH = XATTN_HEADS * XATTN_HEAD_DIM

N_EVEN = (DEPTH + 1) // 2
N_ODD = DEPTH // 2

kernel_name = 'hybrid_fox_hgrn2_mamba2_xattn_block'


def rmsnorm(x, gain):
    xf = x.astype(jnp.float32)
    y = xf * lax.rsqrt(jnp.mean(xf * xf, axis=-1, keepdims=True) + NORM_EPS)
    return (y * gain.astype(jnp.float32)).astype(x.dtype)


def group_rmsnorm(x, gain, groups):
    shp = x.shape
    xf = x.astype(jnp.float32).reshape(shp[:-1] + (groups, shp[-1] // groups))
    y = xf * lax.rsqrt(jnp.mean(xf * xf, axis=-1, keepdims=True) + NORM_EPS)
    return (y.reshape(shp) * gain.astype(jnp.float32)).astype(x.dtype)


def split_cols(t, sizes):
    offs = np.cumsum(sizes)[:-1].tolist()
    return jnp.split(t, offs, axis=-1)


def masked_exp(logd, mask):
    return jnp.where(mask, jnp.exp(jnp.where(mask, logd, 0.0)), 0.0)


def forgetting_attention(q, k, v, log_f):
    bsz, seq, heads, hd = q.shape
    c = jnp.cumsum(log_f, axis=1).transpose(0, 2, 1)
    scale = hd ** -0.5
    outs = []
    for blk in range(seq // FOX_BLOCK):
        q0 = blk * FOX_BLOCK
        q1 = q0 + FOX_BLOCK
        s = jnp.einsum('bqhd,bkhd->bhqk', q[:, q0:q1], k[:, :q1]).astype(jnp.float32) * scale
        s = s + c[:, :, q0:q1, None] - c[:, :, None, :q1]
        mask = (q0 + jnp.arange(FOX_BLOCK))[:, None] >= jnp.arange(q1)[None, :]
        s = jnp.where(mask, s, MASK_VALUE)
        p = jax.nn.softmax(s, axis=-1).astype(v.dtype)
        outs.append(jnp.einsum('bhqk,bkhd->bqhd', p, v[:, :q1]))
    return jnp.concatenate(outs, axis=1)


def hgrn2_recurrence(q, k, v, log_f):
    bsz, seq, heads, dk = q.shape
    dv = v.shape[-1]
    n_chunks = seq // HGRN_CHUNK

    def chunks(t):
        t = t.astype(jnp.float32).reshape(bsz, n_chunks, HGRN_CHUNK, heads, t.shape[-1])
        return t.transpose(1, 0, 3, 2, 4)

    causal = jnp.tril(jnp.ones((HGRN_CHUNK, HGRN_CHUNK), dtype=bool))[:, :, None]

    def step(state, inp):
        qc, kc, vc, gc = inp
        b = jnp.cumsum(gc, axis=2)
        diff = b[:, :, :, None, :] - b[:, :, None, :, :]
        decay = masked_exp(diff, causal)
        scores = jnp.einsum('bhtk,bhtsk,bhsk->bhts', qc, decay, kc)
        out = (jnp.einsum('bhts,bhsv->bhtv', scores, vc)
               + jnp.einsum('bhtk,bhkv->bhtv', qc * jnp.exp(b), state))
        b_last = b[:, :, -1:, :]
        state = (state * jnp.exp(b_last[:, :, 0, :, None])
                 + jnp.einsum('bhsk,bhsv->bhkv', kc * jnp.exp(b_last - b), vc))
        return state, out

    state0 = jnp.zeros((bsz, heads, dk, dv), jnp.float32)
    _, out = lax.scan(step, state0, (chunks(q), chunks(k), chunks(v), chunks(log_f)))
    return out.transpose(1, 0, 3, 2, 4).reshape(bsz, seq, heads, dv).astype(v.dtype)


def ssd_chunked_scan(x, dt, a, bm, cm):
    bsz, seq, heads, hd = x.shape
    n_chunks = seq // SSM_CHUNK

    def chunks(t):
        t = t.astype(jnp.float32).reshape((bsz, n_chunks, SSM_CHUNK) + t.shape[2:])
        return jnp.moveaxis(t, 1, 0)

    causal = jnp.tril(jnp.ones((SSM_CHUNK, SSM_CHUNK), dtype=bool))[None, :, :, None]

    def step(state, inp):
        xc, dtc, bc, cc = inp
        cum = jnp.cumsum(dtc * a, axis=1)
        seg = cum[:, :, None, :] - cum[:, None, :, :]
        decay = masked_exp(seg, causal)
        cb = jnp.repeat(jnp.einsum('btgn,bsgn->btsg', cc, bc), SSM_HEADS_PER_GROUP, axis=-1)
        y = jnp.einsum('btsh,bsh,bshp->bthp', cb * decay, dtc, xc)
        ch = jnp.repeat(cc, SSM_HEADS_PER_GROUP, axis=2)
        bh = jnp.repeat(bc, SSM_HEADS_PER_GROUP, axis=2)
        y = y + jnp.einsum('bthn,bhpn->bthp', ch, state) * jnp.exp(cum)[..., None]
        w = jnp.exp(cum[:, -1:, :] - cum) * dtc
        state = (state * jnp.exp(cum[:, -1, :])[:, :, None, None]
                 + jnp.einsum('bsh,bshp,bshn->bhpn', w, xc, bh))
        return state, y

    state0 = jnp.zeros((bsz, heads, hd, SSM_STATE), jnp.float32)
    _, y = lax.scan(step, state0, (chunks(x), chunks(dt), chunks(bm), chunks(cm)))
    return jnp.moveaxis(y, 0, 1).reshape(bsz, seq, heads, hd).astype(x.dtype)


def fox_hgrn2_mixer(hn, w_in, fgate_bias, lb, out_norm_gain, w_out):
    bsz, seq, _ = hn.shape
    fq, fk, fv, fg, hq, hf, hi, hg = split_cols(
        hn @ w_in,
        (FOX_WIDTH, FOX_WIDTH, FOX_WIDTH, FOX_HEADS,
         HGRN_KEY_WIDTH, HGRN_KEY_WIDTH, HGRN_VAL_WIDTH, HGRN_VAL_WIDTH))
    log_fox_f = jax.nn.log_sigmoid(fg.astype(jnp.float32) + fgate_bias.astype(jnp.float32))
    a_out = forgetting_attention(
        fq.reshape(bsz, seq, FOX_HEADS, FOX_HEAD_DIM),
        fk.reshape(bsz, seq, FOX_HEADS, FOX_HEAD_DIM),
        fv.reshape(bsz, seq, FOX_HEADS, FOX_HEAD_DIM),
        log_fox_f).reshape(bsz, seq, FOX_WIDTH)
    hf32 = hf.astype(jnp.float32)
    lb = lb.astype(jnp.float32)
    log_lb = jnp.log(jnp.maximum(lb, LOG_FLOOR))
    log_gate = jnp.log1p(-lb) + jax.nn.log_sigmoid(hf32)
    m = jnp.maximum(log_lb, log_gate)
    log_f = m + jnp.log1p(jnp.exp(-jnp.abs(log_lb - log_gate)))
    k_in = (1.0 - lb) * jax.nn.sigmoid(-hf32)
    q_h = jax.nn.silu(hq)
    b_out = hgrn2_recurrence(
        q_h.reshape(bsz, seq, HGRN_HEADS, HGRN_KEY_DIM),
        k_in.reshape(bsz, seq, HGRN_HEADS, HGRN_KEY_DIM),
        hi.reshape(bsz, seq, HGRN_HEADS, HGRN_VAL_DIM),
        log_f.reshape(bsz, seq, HGRN_HEADS, HGRN_KEY_DIM)).reshape(bsz, seq, HGRN_VAL_WIDTH)
    b_out = group_rmsnorm(b_out, out_norm_gain, HGRN_HEADS) * jax.nn.silu(hg)
    mixed = jnp.concatenate([a_out, b_out.astype(a_out.dtype)], axis=-1)
    return mixed @ w_out


def mamba2_mixer(hn, w_in, conv_w, conv_b, dt_bias, a_log, d_skip, norm_gain, w_out):
    bsz, seq, _ = hn.shape
    z, xbc, dt_raw = split_cols(hn @ w_in, (SSM_INNER, SSM_CONV_CH, SSM_HEADS))
    xbc = lax.conv_general_dilated(
        xbc, conv_w[:, None, :], window_strides=(1,), padding=[(SSM_CONV - 1, 0)],
        dimension_numbers=('NWC', 'WIO', 'NWC'), feature_group_count=SSM_CONV_CH)
    xbc = jax.nn.silu(xbc + conv_b)
    xs, bm, cm = split_cols(xbc, (SSM_INNER, SSM_GROUPS * SSM_STATE, SSM_GROUPS * SSM_STATE))
    xs = xs.reshape(bsz, seq, SSM_HEADS, SSM_HEAD_DIM)
    bm = bm.reshape(bsz, seq, SSM_GROUPS, SSM_STATE)
    cm = cm.reshape(bsz, seq, SSM_GROUPS, SSM_STATE)
    dt = jax.nn.softplus(dt_raw.astype(jnp.float32) + dt_bias.astype(jnp.float32))
    a = -jnp.exp(a_log.astype(jnp.float32))
    y = ssd_chunked_scan(xs, dt, a, bm, cm)
    y = y + (d_skip[:, None] * xs).astype(y.dtype)
    y = y.reshape(bsz, seq, SSM_INNER) * jax.nn.silu(z)
    y = group_rmsnorm(y, norm_gain, SSM_GROUPS)
    return y @ w_out


def memory_cross_attention(hn, mem_n, w_q, w_kv, w_o):
    bsz, seq, _ = hn.shape
    q = (hn @ w_q).reshape(bsz, seq, XATTN_HEADS, XATTN_HEAD_DIM)
    k, v = split_cols(mem_n @ w_kv, (XATTN_WIDTH, XATTN_WIDTH))
    k = k.reshape(bsz, -1, XATTN_HEADS, XATTN_HEAD_DIM)
    v = v.reshape(bsz, -1, XATTN_HEADS, XATTN_HEAD_DIM)
    s = jnp.einsum('bqhd,bkhd->bhqk', q, k).astype(jnp.float32) * (XATTN_HEAD_DIM ** -0.5)
    p = jax.nn.softmax(s, axis=-1).astype(v.dtype)
    o = jnp.einsum('bhqk,bkhd->bqhd', p, v).reshape(bsz, seq, XATTN_WIDTH)
    return o @ w_o


def squared_relu_mlp(hn, w_up, w_down):
    return jnp.square(jax.nn.relu(hn @ w_up)) @ w_down


def setup_inputs(seed: int = 0) -> dict:
    key = jax.random.key(seed)
    ks = iter(jax.random.split(key, 32))
    f32 = jnp.float32

    def w(shape, fan_in):
        return jax.random.normal(next(ks), shape, f32) * fan_in ** -0.5

    def gain(shape):
        return 1.0 + 0.02 * jax.random.normal(next(ks), shape, f32)

    x = jax.random.normal(next(ks), (BATCH, SEQ, D_MODEL), f32)
    mem = jax.random.normal(next(ks), (BATCH, MEM_LEN, D_MODEL), f32)
    mem_norm = gain((D_MODEL,))
    norm_mix = gain((DEPTH, D_MODEL))
    norm_xattn = gain((DEPTH, D_MODEL))
    norm_mlp = gain((DEPTH, D_MODEL))
    norm_final = gain((D_MODEL,))
    ev_in_proj = w((N_EVEN, D_MODEL, EVEN_PROJ), D_MODEL)
    fox_fgate_bias = FOX_FGATE_BIAS_CENTER + 0.5 * jax.random.normal(next(ks), (N_EVEN, FOX_HEADS), f32)
    hgrn_lb_logits = 0.5 * jax.random.normal(next(ks), (N_EVEN, HGRN_KEY_WIDTH), f32)
    hgrn_out_norm = gain((N_EVEN, HGRN_VAL_WIDTH))
    ev_out_proj = w((N_EVEN, EVEN_MIX_WIDTH, D_MODEL), EVEN_MIX_WIDTH)
    ssm_in_proj = w((N_ODD, D_MODEL, SSM_PROJ), D_MODEL)
    ssm_conv_w = w((N_ODD, SSM_CONV, SSM_CONV_CH), SSM_CONV)
    ssm_conv_b = 0.02 * jax.random.normal(next(ks), (N_ODD, SSM_CONV_CH), f32)
    u = jax.random.uniform(next(ks), (N_ODD, SSM_HEADS), f32)
    dt0 = jnp.exp(u * (np.log(SSM_DT_MAX) - np.log(SSM_DT_MIN)) + np.log(SSM_DT_MIN))
    ssm_dt_bias = dt0 + jnp.log(-jnp.expm1(-dt0))
    ssm_A_log = jnp.log(jax.random.uniform(next(ks), (N_ODD, SSM_HEADS), f32, 1.0, 16.0))
    ssm_D = 1.0 + 0.1 * jax.random.normal(next(ks), (N_ODD, SSM_HEADS), f32)
    ssm_norm = gain((N_ODD, SSM_INNER))
    ssm_out_proj = w((N_ODD, SSM_INNER, D_MODEL), SSM_INNER)
    xa_q = w((DEPTH, D_MODEL, XATTN_WIDTH), D_MODEL)
    xa_kv = w((DEPTH, D_MODEL, 2 * XATTN_WIDTH), D_MODEL)
    xa_o = w((DEPTH, XATTN_WIDTH, D_MODEL), XATTN_WIDTH)
    mlp_up = w((DEPTH, D_MODEL, D_FF), D_MODEL)
    mlp_down = w((DEPTH, D_FF, D_MODEL), D_FF)
    return {'x': x, 'mem': mem, 'mem_norm': mem_norm, 'norm_mix': norm_mix,
            'norm_xattn': norm_xattn, 'norm_mlp': norm_mlp, 'norm_final': norm_final,
            'ev_in_proj': ev_in_proj, 'fox_fgate_bias': fox_fgate_bias,
            'hgrn_lb_logits': hgrn_lb_logits, 'hgrn_out_norm': hgrn_out_norm,
            'ev_out_proj': ev_out_proj, 'ssm_in_proj': ssm_in_proj, 'ssm_conv_w': ssm_conv_w,
            'ssm_conv_b': ssm_conv_b, 'ssm_dt_bias': ssm_dt_bias, 'ssm_A_log': ssm_A_log,
            'ssm_D': ssm_D, 'ssm_norm': ssm_norm, 'ssm_out_proj': ssm_out_proj,
            'xa_q': xa_q, 'xa_kv': xa_kv, 'xa_o': xa_o, 'mlp_up': mlp_up, 'mlp_down': mlp_down}


def reference(x, mem, mem_norm, norm_mix, norm_xattn, norm_mlp, norm_final,
              ev_in_proj, fox_fgate_bias, hgrn_lb_logits, hgrn_out_norm, ev_out_proj,
              ssm_in_proj, ssm_conv_w, ssm_conv_b, ssm_dt_bias, ssm_A_log, ssm_D, ssm_norm,
              ssm_out_proj, xa_q, xa_kv, xa_o, mlp_up, mlp_down):
    mem_n = rmsnorm(mem, mem_norm)
    lb_w = jax.nn.softmax(hgrn_lb_logits.astype(jnp.float32), axis=0)
    lb_all = jnp.cumsum(lb_w, axis=0) - lb_w[0:1]
    h = x
    for layer in range(DEPTH):
        hn = rmsnorm(h, norm_mix[layer])
        if layer % 2 == 0:
            e = layer // 2
            mix = fox_hgrn2_mixer(hn, ev_in_proj[e], fox_fgate_bias[e], lb_all[e],
                                  hgrn_out_norm[e], ev_out_proj[e])
        else:
            o = layer // 2
            mix = mamba2_mixer(hn, ssm_in_proj[o], ssm_conv_w[o], ssm_conv_b[o], ssm_dt_bias[o],
                               ssm_A_log[o], ssm_D[o], ssm_norm[o], ssm_out_proj[o])
        h = h + mix
        h = h + memory_cross_attention(rmsnorm(h, norm_xattn[layer]), mem_n,
                                       xa_q[layer], xa_kv[layer], xa_o[layer])
        h = h + squared_relu_mlp(rmsnorm(h, norm_mlp[layer]), mlp_up[layer], mlp_down[layer])
    return rmsnorm(h, norm_final)
```

```python
import numpy as np
from contextlib import contextmanager, ExitStack
import concourse.bass as bass
import concourse.mybir as mybir
from concourse.bass_utils import run_bass_kernel_spmd

F32 = mybir.dt.float32
BF16 = mybir.dt.bfloat16
AF = mybir.ActivationFunctionType
ALU = mybir.AluOpType
D = 1024
EPS = 1e-6
MEM = 256
NDS = 24


class Tl:
    __slots__ = ("ap", "w", "r")

    def __init__(self, ap=None):
        self.ap = ap
        self.w = None
        self.r = {}


class Prog:
    def __init__(self, nc):
        self.nc = nc
        self.eng = dict(pe=nc.tensor, act=nc.scalar, dve=nc.vector, pool=nc.gpsimd, sp=nc.sync)
        self.sem = {k: nc.alloc_semaphore("s_" + k) for k in self.eng}
        self.cnt = {k: 0 for k in self.eng}
        self.seen = {k: {} for k in self.eng}
        for i in range(NDS):
            self.sem[("d", i)] = nc.alloc_semaphore("d%d" % i)
        self.dval = [0] * NDS
        self.dnext = {"sp": 0, "pool": 0}
        self.uid = 0
        self.nins = 0

    def need(self, e, key, val):
        if e == "pe" and key == "pe":
            return
        if self.seen[e].get(key, 0) >= val:
            return
        self.eng[e].wait_ge(self.sem[key], val)
        self.seen[e][key] = val
        self.nins += 1

    def deps(self, e, rd, wr):
        for t in rd:
            if t.w is not None:
                self.need(e, t.w[0], t.w[1])
        for t in wr:
            if t.w is not None:
                self.need(e, t.w[0], t.w[1])
            for k, v in t.r.items():
                self.need(e, k, v)

    def mark(self, key, val, rd, wr):
        for t in rd:
            t.r[key] = val
        for t in wr:
            t.w = (key, val)
            t.r = {}

    def op(self, e, fn, rd=(), wr=()):
        self.deps(e, rd, wr)
        ins = fn(self.eng[e])
        self.cnt[e] += 1
        ins.then_inc(self.sem[e], 1)
        self.mark(e, self.cnt[e], rd, wr)
        self.nins += 1
        return ins

    def dma(self, q, out, in_, rd=(), wr=()):
        half = NDS // 2
        i = self.dnext[q] + (0 if q == "sp" else half)
        self.dnext[q] = (self.dnext[q] + 1) % half
        key = ("d", i)
        if self.dval[i] > 0:
            self.need(q, key, self.dval[i])
        self.deps(q, rd, wr)
        ins = self.eng[q].dma_start(out=out, in_=in_)
        self.dval[i] += 16
        ins.then_inc(self.sem[key], 16)
        self.mark(key, self.dval[i], rd, wr)
        self.nins += 1
        return key, self.dval[i]

    def barrier(self):
        es = ("pe", "act", "dve")
        for e in es:
            for x in es:
                if x != e and self.cnt[x] > 0:
                    self.need(e, x, self.cnt[x])

    @contextmanager
    def phase(self):
        es = ExitStack()

        def sb(name, shape, dt):
            self.uid += 1
            t = es.enter_context(self.nc.sbuf_tensor("%s_%d" % (name, self.uid), list(shape), dt))
            return t.ap()

        yield sb
        self.barrier()
        es.close()


def const_layout():
    off = {}
    c = 0
    for name, n in (("ident", 128), ("U", 128), ("SL", 128), ("ones", 128), ("sel64", 128),
                    ("maskbd", 128), ("rowA", 1), ("rowB", 1), ("colA", 128), ("colB", 128),
                    ("scanm", 512)):
        off[name] = (c, n)
        c += n
    return off, c


def make_consts():
    off, n = const_layout()
    a = np.zeros((128, n), np.float32)
    r = np.arange(128)[:, None]
    t = np.arange(128)[None, :]

    def put(name, v):
        o, w = off[name]
        a[:, o:o + w] = v

    put("ident", (r == t))
    put("U", (r <= t))
    put("SL", (r > t))
    put("ones", 1.0)
    put("sel64", np.broadcast_to(r == 64, (128, 128)))
    put("maskbd", (r <= t) & ((r // 64) == (t // 64)))
    put("rowA", (r < 64))
    put("rowB", (r >= 64))
    t5 = np.arange(512)[None, :]
    put("colA", np.broadcast_to(t < 64, (128, 128)))
    put("colB", np.broadcast_to(t >= 64, (128, 128)))
    put("scanm", np.broadcast_to((t5 % 64) != 0, (128, 512)))
    return a


def param_layout(kinds):
    L = len(kinds)
    ne = sum(1 for k in kinds if k == "e")
    no = L - ne
    off = {}
    c = [0]

    def add(name, n):
        off[name] = (c[0], n)
        c[0] += n

    add("mem_norm", 8)
    add("norm_final", 8)
    for l in range(L):
        add("nmix%d" % l, 8)
        add("nxa%d" % l, 8)
        add("nmlp%d" % l, 8)
    for e in range(ne):
        add("fgb%d" % e, 8)
        add("hon%d" % e, 512)
    for e in range(max(ne, 1)):
        add("lbl%d" % e, 4)
    for o in range(no):
        add("cw%d" % o, 128)
        add("cb%d" % o, 32)
        add("dtb%d" % o, 32)
        add("alog%d" % o, 32)
        add("dsk%d" % o, 32)
        add("snorm%d" % o, 16)
    return off, c[0]


def fm(v, n):
    return np.ascontiguousarray(np.asarray(v, np.float32).reshape(n, 128).T)


def make_params(kinds, inp):
    off, n = param_layout(kinds)
    a = np.zeros((128, n), np.float32)

    def put(name, v):
        o, w = off[name]
        a[:, o:o + w] = v

    put("mem_norm", fm(inp["mem_norm"], 8))
    put("norm_final", fm(inp["norm_final"], 8))
    e = o = 0
    for l, k in enumerate(kinds):
        put("nmix%d" % l, fm(inp["norm_mix"][l], 8))
        put("nxa%d" % l, fm(inp["norm_xattn"][l], 8))
        put("nmlp%d" % l, fm(inp["norm_mlp"][l], 8))
        if k == "e":
            put("fgb%d" % e, np.broadcast_to(np.asarray(inp["fox_fgate_bias"][e])[None, :], (128, 8)))
            put("hon%d" % e, np.broadcast_to(np.asarray(inp["hgrn_out_norm"][e])[None, :], (128, 512)))
            put("lbl%d" % e, fm(inp["hgrn_lb_logits"][e], 4))
            e += 1
        else:
            cw = np.asarray(inp["ssm_conv_w"][o], np.float32)
            put("cw%d" % o, np.ascontiguousarray(cw.reshape(4, 32, 128).transpose(2, 1, 0)).reshape(128, 128))
            put("cb%d" % o, fm(inp["ssm_conv_b"][o], 32))
            put("dtb%d" % o, np.broadcast_to(np.asarray(inp["ssm_dt_bias"][o])[None, :], (128, 32)))
            put("alog%d" % o, np.broadcast_to(np.asarray(inp["ssm_A_log"][o])[None, :], (128, 32)))
            put("dsk%d" % o, np.broadcast_to(np.asarray(inp["ssm_D"][o])[None, :], (128, 32)))
            put("snorm%d" % o, fm(inp["ssm_norm"][o], 16))
            o += 1
    return a


class Ctx:
    pass


def build(S, NSEQ, kinds, flags=("mix", "xa", "mlp")):
    nc = bass.Bass("TRN2", target_bir_lowering=False)
    p = Prog(nc)
    C = Ctx()
    C.S, C.NT, C.NG = S, S // 128, S // 512
    NT, NG = C.NT, C.NG
    L = len(kinds)
    ne = sum(1 for k in kinds if k == "e")
    no = L - ne
    coff, cn = const_layout()
    poff, pn = param_layout(kinds)

    def din(name, shape):
        return nc.dram_tensor(name, list(shape), F32, kind="ExternalInput").ap()

    x_d = din("x", [NSEQ, S, D])
    mem_d = din("mem", [NSEQ, MEM, D])
    cst_d = din("cst", [128, cn])
    prm_d = din("prm", [128, pn])
    W = {}
    if ne:
        W["ev_in"] = din("ev_in_proj", [ne, D, 3592])
        W["ev_out"] = din("ev_out_proj", [ne, D, D])
    if no:
        W["ssm_in"] = din("ssm_in_proj", [no, D, 6176])
        W["ssm_out"] = din("ssm_out_proj", [no, 2048, D])
    W["xa_q"] = din("xa_q", [L, D, 512])
    W["xa_kv"] = din("xa_kv", [L, D, 1024])
    W["xa_o"] = din("xa_o", [L, 512, D])
    W["mlp_up"] = din("mlp_up", [L, D, 4096])
    W["mlp_down"] = din("mlp_down", [L, 4096, D])
    y_d = nc.dram_tensor("y", [NSEQ, S, D], F32, kind="ExternalOutput").ap()

    def sba(name, shape, dt):
        return nc.alloc_sbuf_tensor(name, list(shape), dt).ap()

    h_sb = sba("h", [128, NT, D], F32)
    hT = [[Tl(h_sb[:, tt, hf * 512:(hf + 1) * 512]) for hf in range(2)] for tt in range(NT)]
    hn_sb = sba("hnT", [128, 8, S], BF16)
    hnT = [Tl(hn_sb[:, :, g * 512:(g + 1) * 512]) for g in range(NG)]
    mem_sb = sba("memT", [128, 8, MEM], BF16)
    memT = Tl(mem_sb)
    cst = sba("cst_sb", [128, cn], F32)
    cstT = Tl(cst)
    prm = sba("prm_sb", [128, pn], F32)
    prmT = Tl(prm)
    cb_sb = sba("cstb", [128, 512], BF16)
    cbT = Tl(cb_sb)
    WA = Tl(sba("WA", [128, 8192], BF16))
    WB = Tl(sba("WB", [128, 8192], BF16))
    ws_ap = sba("WS", [128, 4096], BF16)
    WS0 = Tl(ws_ap[:, 0:2048])
    WS1 = Tl(ws_ap[:, 2048:4096])
    WT = Tl(sba("WT", [128, 8 * 32], BF16))
    banks = [Tl(nc.alloc_psum_tensor("ps%d" % i, [128, 512], F32).ap()) for i in range(8)]

    def cs(name):
        o, w = coff[name]
        return cst[:, o:o + w]

    def pr(name):
        o, w = poff[name]
        return prm[:, o:o + w]

    identb = cb_sb[:, 0:128]
    Ub = cb_sb[:, 128:256]
    onesb128 = cb_sb[:, 256:384]
    onesb = cb_sb[:, 256:320]

    p.dma("sp", cst, cst_d, wr=[cstT])
    p.dma("sp", prm, prm_d, wr=[prmT])
    p.op("dve", lambda e: e.tensor_copy(out=identb, in_=cs("ident")), rd=[cstT], wr=[cbT])
    p.op("dve", lambda e: e.tensor_copy(out=Ub, in_=cs("U")), rd=[cstT], wr=[cbT])
    p.op("dve", lambda e: e.tensor_copy(out=onesb128, in_=cs("ones")), rd=[cstT], wr=[cbT])
    epsc = sba("epsc", [128, 1], F32)
    epsT = Tl()
    p.op("dve", lambda e: e.memset(epsc, EPS), wr=[epsT])

    def rstd_chain(ss, lnv, rs, n, tl):
        p.op("act", lambda e: e.activation(out=lnv, in_=ss, func=AF.Ln, scale=1.0 / n, bias=epsc), rd=[tl, epsT], wr=[tl])
        p.op("act", lambda e: e.activation(out=rs, in_=lnv, func=AF.Exp, scale=-0.5), rd=[tl], wr=[tl])

    def wload(tiles, dst2d, src, kc, ncols, col0=0, pdim=128):
        dst = dst2d[0:pdim, col0:col0 + kc * ncols].rearrange("p (k n) -> p k n", k=kc)
        p.dma("pool", dst, src.rearrange("(k p) n -> p k n", p=pdim), wr=tiles)
        return dst

    def rmsnorm_T(sb, ntt, src, gcol, dst, psb):
        xn = [sb("xn", [128, D], BF16) for _ in range(2)]
        xnT = [Tl() for _ in range(2)]
        junk = sb("junk", [128, D], BF16)
        junkT = Tl()
        st = sb("st", [128, 8], F32)
        stT = [Tl(), Tl()]
        for tt in range(ntt):
            tiles, ap = src(tt)
            k = tt % 2
            ss, sd, rs = st[:, 4 * k:4 * k + 1], st[:, 4 * k + 1:4 * k + 2], st[:, 4 * k + 2:4 * k + 3]
            p.op("dve", lambda e: e.scalar_tensor_tensor(out=junk, in0=ap, scalar=1.0, in1=ap, op0=ALU.mult,
                                                         op1=ALU.mult, accum_out=ss),
                 rd=tiles, wr=[junkT, stT[k]])
            rstd_chain(ss, sd, rs, D, stT[k])
            p.op("act", lambda e: e.activation(out=xn[k], in_=ap, func=AF.Copy, scale=rs),
                 rd=list(tiles) + [stT[k]], wr=[xnT[k]])
            pst = banks[psb[tt % len(psb)]]
            psv = pst.ap.bitcast(BF16)[:, 0:1024].rearrange("p (k n) -> p k n", k=8)
            for kc in range(8):
                p.op("pe", lambda e: e.transpose(out=psv[:, kc, :], in_=xn[k][:, kc * 128:(kc + 1) * 128],
                                                 identity=identb), rd=[xnT[k], cbT], wr=[pst])
            dt_, dap = dst(tt)
            p.op("dve", lambda e: e.tensor_tensor(out=dap, in0=psv,
                                                  in1=gcol.unsqueeze(2).broadcast_to([128, 8, 128]),
                                                  op=ALU.mult), rd=[pst, prmT], wr=[dt_])

    def h_src(tt):
        return hT[tt], h_sb[:, tt, :]

    def hn_dst(tt):
        return hnT[tt // 4], hn_sb[:, :, tt * 128:(tt + 1) * 128]

    def add_h(tt, hf, ps_t, ps_ap):
        p.op("dve", lambda e: e.tensor_tensor(out=hT[tt][hf].ap, in0=hT[tt][hf].ap, in1=ps_ap, op=ALU.add),
             rd=[ps_t, hT[tt][hf]], wr=[hT[tt][hf]])

    def mlp(l):
        wu0 = wload([WA], WA.ap, W["mlp_up"][l, :, 0:1024], 8, 1024)
        wd0 = wload([WB], WB.ap, W["mlp_down"][l, 0:1024, :], 8, 1024)
        with p.phase() as sb:
            rmsnorm_T(sb, NT, h_src, pr("nmlp%d" % l), hn_dst, (6, 7))
        with p.phase() as sb:
            aT = sb("aT", [128, 8, S], BF16)
            aTt = [Tl() for _ in range(NG)]
            rl = [sb("rl", [128, 512], F32) for _ in range(2)]
            rlT = [Tl(), Tl()]
            n = 0
            for fb in range(4):
                if fb == 0:
                    wu, wd = wu0, wd0
                else:
                    wu = wload([WA], WA.ap, W["mlp_up"][l, :, fb * 1024:(fb + 1) * 1024], 8, 1024)
                    wd = wload([WB], WB.ap, W["mlp_down"][l, fb * 1024:(fb + 1) * 1024, :], 8, 1024)
                for g in range(NG):
                    for fc in range(8):
                        pst = banks[n % 4]
                        for kc in range(8):
                            p.op("pe", lambda e: e.matmul(pst.ap, lhsT=wu[:, kc, fc * 128:(fc + 1) * 128],
                                                          rhs=hn_sb[:, kc, g * 512:(g + 1) * 512],
                                                          start=(kc == 0), stop=(kc == 7)),
                                 rd=[WA, hnT[g]], wr=[pst])
                        k = n % 2
                        p.op("act", lambda e: e.activation(out=rl[k], in_=pst.ap, func=AF.Relu),
                             rd=[pst], wr=[rlT[k]])
                        p.op("dve", lambda e: e.tensor_tensor(out=aT[:, fc, g * 512:(g + 1) * 512], in0=rl[k],
                                                              in1=rl[k], op=ALU.mult), rd=[rlT[k]], wr=[aTt[g]])
                        n += 1
                for tt in range(NT):
                    for hf in range(2):
                        pst = banks[4 + (n % 4)]
                        n += 1
                        for fc in range(8):
                            p.op("pe", lambda e: e.matmul(pst.ap, lhsT=aT[:, fc, tt * 128:(tt + 1) * 128],
                                                          rhs=wd[:, fc, hf * 512:(hf + 1) * 512],
                                                          start=(fc == 0), stop=(fc == 7)),
                                 rd=[WB, aTt[tt // 4]], wr=[pst])
                        add_h(tt, hf, pst, pst.ap)

    def xattn(l):
        wkv = wload([WA], WA.ap, W["xa_kv"][l], 8, 1024)
        wq = wload([WS0, WS1], ws_ap, W["xa_q"][l], 8, 512)
        wo = wload([WB], WB.ap, W["xa_o"][l], 4, 1024)
        with p.phase() as sb:
            rmsnorm_T(sb, NT, h_src, pr("nxa%d" % l), hn_dst, (6, 7))
        with p.phase() as sb:
            kT = sb("kT", [128, 4, MEM], BF16)
            kTt = Tl()
            va = sb("va", [128, 2, 4, 128], BF16)
            vat = Tl()
            qT = sb("qT", [128, 4, S], BF16)
            qTt = [Tl() for _ in range(NG)]
            n = 0
            for hh in range(4):
                pst = banks[n % 2]
                n += 1
                for kc in range(8):
                    p.op("pe", lambda e: e.matmul(pst.ap[:, 0:MEM], lhsT=wkv[:, kc, hh * 128:(hh + 1) * 128],
                                                  rhs=mem_sb[:, kc, :], start=(kc == 0), stop=(kc == 7)),
                         rd=[WA, memT], wr=[pst])
                p.op("act", lambda e: e.activation(out=kT[:, hh, :], in_=pst.ap[:, 0:MEM], func=AF.Copy),
                     rd=[pst], wr=[kTt])
            for kb in range(2):
                pst = banks[n % 2]
                n += 1
                for kc in range(8):
                    p.op("pe", lambda e: e.matmul(pst.ap, lhsT=mem_sb[:, kc, kb * 128:(kb + 1) * 128],
                                                  rhs=wkv[:, kc, 512:1024], start=(kc == 0), stop=(kc == 7)),
                         rd=[WA, memT], wr=[pst])
                p.op("act", lambda e: e.activation(out=va[:, kb, :, :],
                                                   in_=pst.ap.rearrange("p (h d) -> p h d", h=4), func=AF.Copy),
                     rd=[pst], wr=[vat])
            for g in range(NG):
                for hh in range(4):
                    pst = banks[n % 2]
                    n += 1
                    for kc in range(8):
                        p.op("pe", lambda e: e.matmul(pst.ap, lhsT=wq[:, kc, hh * 128:(hh + 1) * 128],
                                                      rhs=hn_sb[:, kc, g * 512:(g + 1) * 512],
                                                      start=(kc == 0), stop=(kc == 7)),
                             rd=[WS0, WS1, hnT[g]], wr=[pst])
                    p.op("act", lambda e: e.activation(out=qT[:, hh, g * 512:(g + 1) * 512], in_=pst.ap,
                                                       func=AF.Copy, scale=float(128 ** -0.5)),
                         rd=[pst], wr=[qTt[g]])
            PT = [sb("PT", [128, 512], BF16) for _ in range(4)]
            PTt = [Tl() for _ in range(4)]
            rs_ = [sb("rs", [128, 512], F32) for _ in range(2)]
            rst = [Tl(), Tl()]
            oT = [sb("oT", [128, 4, 512], BF16) for _ in range(2)]
            oTt = [Tl(), Tl()]
            items = [(g, hh) for g in range(NG) for hh in range(4)]

            def sbanks(idx):
                return (banks[0], banks[1]) if idx % 2 == 0 else (banks[6], banks[7])

            def xX(idx):
                g, hh = items[idx]
                for kb in range(2):
                    pst = sbanks(idx)[kb]
                    p.op("pe", lambda e: e.matmul(pst.ap, lhsT=kT[:, hh, kb * 128:(kb + 1) * 128],
                                                  rhs=qT[:, hh, g * 512:(g + 1) * 512], start=True, stop=True),
                         rd=[kTt, qTt[g]], wr=[pst])

            def xY(idx):
                nonlocal n
                g, hh = items[idx]
                og = g % 2
                pts = []
                for kb in range(2):
                    pst = sbanks(idx)[kb]
                    k = (2 * idx + kb) % 4
                    p.op("act", lambda e: e.activation(out=PT[k], in_=pst.ap, func=AF.Exp),
                         rd=[pst], wr=[PTt[k]])
                    pts.append(k)
                po, psum_ = banks[2 + (hh % 2) * 2], banks[3 + (hh % 2) * 2]
                for i, k in enumerate(pts):
                    p.op("pe", lambda e: e.matmul(po.ap, lhsT=va[:, i, hh, :], rhs=PT[k],
                                                  start=(i == 0), stop=(i == 1)), rd=[vat, PTt[k]], wr=[po])
                for i, k in enumerate(pts):
                    p.op("pe", lambda e: e.matmul(psum_.ap, lhsT=onesb128, rhs=PT[k],
                                                  start=(i == 0), stop=(i == 1)), rd=[cbT, PTt[k]], wr=[psum_])
                r = hh % 2
                p.op("dve", lambda e: e.reciprocal(out=rs_[r], in_=psum_.ap), rd=[psum_], wr=[rst[r]])
                p.op("dve", lambda e: e.tensor_tensor(out=oT[og][:, hh, :], in0=po.ap, in1=rs_[r], op=ALU.mult),
                     rd=[po, rst[r]], wr=[oTt[og]])
                if hh == 3:
                    for j in range(4):
                        tt = g * 4 + j
                        for hf in range(2):
                            pst = sbanks(idx)[n % 2]
                            n += 1
                            for kc in range(4):
                                p.op("pe", lambda e: e.matmul(pst.ap, lhsT=oT[og][:, kc, j * 128:(j + 1) * 128],
                                                              rhs=wo[:, kc, hf * 512:(hf + 1) * 512],
                                                              start=(kc == 0), stop=(kc == 3)),
                                     rd=[oTt[og], WB], wr=[pst])
                            add_h(tt, hf, pst, pst.ap)

            xX(0)
            for idx in range(len(items)):
                if idx + 1 < len(items):
                    xX(idx + 1)
                xY(idx)

    def even_mixer(l, e):
        Win = W["ev_in"][e]
        Wout = W["ev_out"][e]
        fw = (wload([WA], WA.ap, Win[:, 0:1024], 8, 1024),
              wload([WS0, WS1], ws_ap, Win[:, 1024:1536], 8, 512),
              wload([WT], WT.ap, Win[:, 1536:1544], 8, 8),
              wload([WB], WB.ap, Wout[0:512, :], 8, 1024, pdim=64))
        with p.phase() as sb:
            rmsnorm_T(sb, NT, h_src, pr("nmix%d" % l), hn_dst, (6, 7))
        fox(l, e, Win, Wout, fw)
        hgrn(l, e, Win, Wout)

    def fox(l, e, Win, Wout, fw):
        with p.phase() as sb:
            wqk, wv, wg, woF = fw
            NH = NT * 8
            lf = sb("lf", [128, NT, 8], F32)
            lfT = Tl()
            lf2 = lf.rearrange("p t h -> p (t h)")
            pg = banks[0]
            for tt in range(NT):
                for kc in range(8):
                    p.op("pe", lambda e_: e_.matmul(pg.ap[:, tt * 8:(tt + 1) * 8],
                                                    lhsT=hn_sb[:, kc, tt * 128:(tt + 1) * 128], rhs=wg[:, kc, :],
                                                    start=(kc == 0), stop=(kc == 7)), rd=[hnT[tt // 4], WT], wr=[pg])
            pgv = pg.ap[:, 0:NH].rearrange("p (t h) -> p t h", h=8)
            p.op("dve", lambda e_: e_.tensor_tensor(out=lf, in0=pgv,
                                                    in1=pr("fgb%d" % e).unsqueeze(1).broadcast_to([128, NT, 8]),
                                                    op=ALU.add), rd=[pg, prmT], wr=[lfT])
            p.op("act", lambda e_: e_.activation(out=lf, in_=lf, func=AF.Exp, scale=-1.0), rd=[lfT], wr=[lfT])
            p.op("dve", lambda e_: e_.tensor_scalar(out=lf, in0=lf, scalar1=1.0, scalar2=None, op0=ALU.add),
                 rd=[lfT], wr=[lfT])
            p.op("act", lambda e_: e_.activation(out=lf, in_=lf, func=AF.Ln), rd=[lfT], wr=[lfT])
            p.op("dve", lambda e_: e_.tensor_scalar(out=lf, in0=lf, scalar1=-1.0, scalar2=None, op0=ALU.mult),
                 rd=[lfT], wr=[lfT])
            pc, ptot, pm = banks[1], banks[2], banks[3]
            p.op("pe", lambda e_: e_.matmul(pc.ap[:, 0:NH], lhsT=cs("U"), rhs=lf2, start=True, stop=True),
                 rd=[cstT, lfT], wr=[pc])
            p.op("pe", lambda e_: e_.matmul(ptot.ap[:, 0:NH], lhsT=cs("ones"), rhs=lf2, start=True, stop=True),
                 rd=[cstT, lfT], wr=[ptot])
            tot = sb("tot", [128, NT, 8], F32)
            totT = Tl()
            offs = sb("offs", [128, NT, 8], F32)
            offT = Tl()
            c = sb("c", [128, NT, 8], F32)
            cT = Tl()
            cmb = sb("cmb", [128, NT, 8], F32)
            cmbT = Tl()
            p.op("act", lambda e_: e_.activation(out=tot.rearrange("p t h -> p (t h)"), in_=ptot.ap[:, 0:NH],
                                                 func=AF.Copy), rd=[ptot], wr=[totT])
            p.op("dve", lambda e_: e_.memset(offs[:, 0, :], 0.0), wr=[offT])
            for i in range(1, NT):
                p.op("dve", lambda e_: e_.tensor_tensor(out=offs[:, i, :], in0=offs[:, i - 1, :], in1=tot[:, i - 1, :],
                                                        op=ALU.add), rd=[totT, offT], wr=[offT])
            p.op("dve", lambda e_: e_.tensor_tensor(out=c.rearrange("p t h -> p (t h)"), in0=pc.ap[:, 0:NH],
                                                    in1=offs.rearrange("p t h -> p (t h)"), op=ALU.add),
                 rd=[pc, offT], wr=[cT])
            p.op("pe", lambda e_: e_.matmul(pm.ap[:, 0:NH], lhsT=cs("sel64"), rhs=c.rearrange("p t h -> p (t h)"),
                                            start=True, stop=True), rd=[cstT, cT], wr=[pm])
            p.op("act", lambda e_: e_.activation(out=cmb.rearrange("p t h -> p (t h)"), in_=pm.ap[:, 0:NH],
                                                 func=AF.Copy), rd=[pm], wr=[cmbT])
            negc = sb("negc", [128, NT, 8], F32)
            negcT = Tl()
            p.op("dve", lambda e_: e_.tensor_scalar(out=negc, in0=c, scalar1=-1.0, scalar2=None, op0=ALU.mult),
                 rd=[cT], wr=[negcT])
            hib = sb("hib", [128, NT], BF16)
            lo32 = sb("lo32", [128, NT], F32)
            hlT = Tl()
            fq = [sb("fq", [97, S], BF16) for _ in range(2)]
            fk = [sb("fk", [97, S], BF16) for _ in range(2)]
            fqT = [Tl(), Tl()]
            fkT = [Tl(), Tl()]
            for b2 in range(2):
                p.op("dve", lambda e_: e_.memset(fq[b2][64:97, :], 0.0), wr=[fqT[b2]])
                p.op("dve", lambda e_: e_.memset(fk[b2][64:97, :], 0.0), wr=[fkT[b2]])
                p.op("dve", lambda e_: e_.memset(fk[b2][64:65, :], 1.0), wr=[fkT[b2]])
                p.op("dve", lambda e_: e_.memset(fk[b2][96:97, :], 1.0), wr=[fkT[b2]])
            vh = [sb("vh", [128, NT, 64], BF16) for _ in range(2)]
            vhT = [Tl(), Tl()]
            oT = [sb("foT", [64, S], BF16) for _ in range(2)]
            oTt = [Tl(), Tl()]
            PT = [sb("fPT", [128, 512], BF16) for _ in range(3)]
            PTt = [[Tl() for _ in range(4)] for _ in range(3)]
            rs = [sb("frs", [64, 512], F32) for _ in range(2)]
            rsT = [Tl(), Tl()]
            n = 0
            m = 0
            def f_proj(hd):
                nonlocal n
                b_ = hd % 2
                p.op("dve", lambda e_: e_.tensor_copy(
                    out=fq[b_][64:65, :].rearrange("p (t c) -> p t c", c=128),
                    in_=cmb[64:65, :, hd].unsqueeze(2).broadcast_to([1, NT, 128])), rd=[cmbT], wr=[fqT[b_]])
                p.op("dve", lambda e_: e_.tensor_copy(out=hib[96:97, :], in_=cmb[96:97, :, hd]), rd=[cmbT], wr=[hlT])
                p.op("dve", lambda e_: e_.tensor_tensor(out=lo32[96:97, :], in0=cmb[96:97, :, hd], in1=hib[96:97, :],
                                                        op=ALU.subtract), rd=[cmbT, hlT], wr=[hlT])
                p.op("dve", lambda e_: e_.tensor_copy(
                    out=fq[b_][96:97, :].rearrange("p (t c) -> p t c", c=128),
                    in_=lo32[96:97, :].unsqueeze(2).broadcast_to([1, NT, 128])), rd=[hlT], wr=[fqT[b_]])
                for g in range(NG):
                    for dst, dT, col0, sc in ((fq[b_], fqT[b_], hd * 64, 0.125), (fk[b_], fkT[b_], 512 + hd * 64, 1.0)):
                        pst = banks[n % 2]
                        n += 1
                        for kc in range(8):
                            p.op("pe", lambda e_: e_.matmul(pst.ap[0:64, :], lhsT=wqk[:, kc, col0:col0 + 64],
                                                            rhs=hn_sb[:, kc, g * 512:(g + 1) * 512],
                                                            start=(kc == 0), stop=(kc == 7)),
                                 rd=[WA, hnT[g]], wr=[pst])
                        p.op("dve", lambda e_: e_.tensor_scalar(out=dst[0:64, g * 512:(g + 1) * 512], in0=pst.ap[0:64, :],
                                                                scalar1=sc, scalar2=None, op0=ALU.mult), rd=[pst], wr=[dT])
                nv = min(8, NT)
                for t0 in range(0, NT, nv):
                    pst = banks[n % 2]
                    n += 1
                    for j in range(nv):
                        tt = t0 + j
                        for kc in range(8):
                            p.op("pe", lambda e_: e_.matmul(pst.ap[:, j * 64:(j + 1) * 64],
                                                            lhsT=hn_sb[:, kc, tt * 128:(tt + 1) * 128],
                                                            rhs=wv[:, kc, hd * 64:(hd + 1) * 64],
                                                            start=(kc == 0), stop=(kc == 7)),
                                 rd=[WS0, WS1, hnT[tt // 4]], wr=[pst])
                    p.op("act", lambda e_: e_.activation(out=vh[b_][:, t0:t0 + nv, :],
                                                         in_=pst.ap[:, 0:nv * 64].rearrange("p (t d) -> p t d", d=64),
                                                         func=AF.Copy), rd=[pst], wr=[vhT[b_]])

            def f_attn(hd):
                nonlocal m
                b_ = hd % 2
                items = [(G, j) for G in range(NG) for j in range(4 * G + 4)]

                def stX(idx):
                    G, j = items[idx]
                    i0 = max(j, 4 * G)
                    c0 = (i0 - 4 * G) * 128
                    pst = banks[2 + ((m + idx) % 2)]
                    p.op("pe", lambda e_: e_.matmul(pst.ap[:, c0:512], lhsT=fk[b_][:, j * 128:(j + 1) * 128],
                                                    rhs=fq[b_][:, G * 512 + c0:(G + 1) * 512], start=True, stop=True),
                         rd=[fkT[b_], fqT[b_]], wr=[pst])

                def stY(idx):
                    G, j = items[idx]
                    jmax = 4 * G + 3
                    i0 = max(j, 4 * G)
                    c0 = (i0 - 4 * G) * 128
                    pst = banks[2 + ((m + idx) % 2)]
                    po, psm = (banks[4], banks[5]) if G % 2 == 0 else (banks[6], banks[7])
                    k = (m + idx) % 3
                    blks = list(range(i0 - 4 * G, 4))
                    p.op("act", lambda e_: e_.activation(out=PT[k][:, c0:512], in_=pst.ap[:, c0:512],
                                                         func=AF.Exp, bias=negc[:, j, hd:hd + 1]),
                         rd=[pst, negcT], wr=[PTt[k][bi] for bi in blks])
                    if j >= 4 * G:
                        bi = j - 4 * G
                        cc = bi * 128
                        p.op("dve", lambda e_: e_.tensor_tensor(out=PT[k][:, cc:cc + 128], in0=PT[k][:, cc:cc + 128],
                                                                in1=Ub, op=ALU.mult),
                             rd=[PTt[k][bi], cbT], wr=[PTt[k][bi]])
                    rdt = [PTt[k][bi] for bi in blks]
                    p.op("pe", lambda e_: e_.matmul(po.ap[0:64, c0:512], lhsT=vh[b_][:, j, :], rhs=PT[k][:, c0:512],
                                                    start=(j == 0), stop=(j == jmax)), rd=rdt + [vhT[b_]], wr=[po])
                    p.op("pe", lambda e_: e_.matmul(psm.ap[0:64, c0:512], lhsT=onesb, rhs=PT[k][:, c0:512],
                                                    start=(j == 0), stop=(j == jmax)), rd=rdt + [cbT], wr=[psm])
                    if j == jmax:
                        r_ = G % 2
                        p.op("dve", lambda e_: e_.reciprocal(out=rs[r_], in_=psm.ap[0:64, :]), rd=[psm], wr=[rsT[r_]])
                        p.op("dve", lambda e_: e_.tensor_tensor(out=oT[b_][:, G * 512:(G + 1) * 512], in0=po.ap[0:64, :],
                                                                in1=rs[r_], op=ALU.mult), rd=[po, rsT[r_]], wr=[oTt[b_]])

                stX(0)
                for idx in range(len(items)):
                    if idx + 1 < len(items):
                        stX(idx + 1)
                    stY(idx)
                m += len(items)

            def f_out(hd):
                nonlocal n
                b_ = hd % 2
                for tt in range(NT):
                    for hf in range(2):
                        pst = banks[n % 2]
                        n += 1
                        p.op("pe", lambda e_: e_.matmul(pst.ap, lhsT=oT[b_][:, tt * 128:(tt + 1) * 128],
                                                        rhs=woF[:, hd, hf * 512:(hf + 1) * 512], start=True, stop=True),
                             rd=[oTt[b_], WB], wr=[pst])
                        add_h(tt, hf, pst, pst.ap)


            f_proj(0)
            for hd in range(8):
                if hd + 1 < 8:
                    f_proj(hd + 1)
                f_attn(hd)
                if hd >= 1:
                    f_out(hd - 1)
            f_out(7)
    def hgrn(l, e, Win, Wout):
        assert ne <= 2
        with p.phase() as sb:
            wqf = wload([WA], WA.ap, Win[:, 1544:2568], 8, 1024)
            wig = wload([WB], WB.ap, Win[:, 2568:3592], 8, 1024)
            woH = wload([WS0, WS1], ws_ap, Wout[512:1024, :], 4, 1024)
            lbs = sb("lbs", [128, 12], F32)
            lbT = Tl()
            lb, oml, noml = lbs[:, 0:4], lbs[:, 4:8], lbs[:, 8:12]
            if e == 0:
                p.op("dve", lambda e_: e_.memset(lb, 0.0), wr=[lbT])
            else:
                p.op("dve", lambda e_: e_.tensor_tensor(out=lb, in0=pr("lbl1"), in1=pr("lbl0"), op=ALU.subtract),
                     rd=[prmT], wr=[lbT])
                p.op("act", lambda e_: e_.activation(out=lb, in_=lb, func=AF.Sigmoid), rd=[lbT], wr=[lbT])
            p.op("dve", lambda e_: e_.tensor_scalar(out=oml, in0=lb, scalar1=-1.0, scalar2=1.0, op0=ALU.mult,
                                                    op1=ALU.add), rd=[lbT], wr=[lbT])
            p.op("dve", lambda e_: e_.tensor_scalar(out=noml, in0=oml, scalar1=-1.0, scalar2=None, op0=ALU.mult),
                 rd=[lbT], wr=[lbT])
            Sst = sb("Sst", [128, 4, 128], F32)
            SstT = [Tl() for _ in range(4)]
            p.op("dve", lambda e_: e_.memset(Sst, 0.0), wr=SstT)
            vtok = sb("vtok", [128, 4, 512], BF16)
            vtT = [Tl() for _ in range(4)]
            gsg = sb("gsg", [128, 4, 512], F32)
            gsT = [Tl() for _ in range(4)]
            sg = sb("sg", [128, 512], F32)
            sgT = Tl()

            def f32t(name):
                return sb(name, [128, 512], F32), Tl()

            def b16t(name):
                return sb(name, [128, 512], BF16), Tl()

            qs, qsT = f32t("qs")
            sig, sigT = f32t("sig")
            lg, lgT = f32t("lg")
            kin, kinT = f32t("kin")
            bb, bbT = f32t("bb")
            tmp, tmpT = f32t("tmp")
            ex, exT = f32t("ex")
            qbf, qbfT = f32t("qbf")
            qtilS = [b16t("qtil") for _ in range(2)]
            ktilS = [b16t("ktil") for _ in range(2)]
            qbAS = [b16t("qbA") for _ in range(2)]
            qbBS = [b16t("qbB") for _ in range(2)]
            kdTS = [b16t("kdT") for _ in range(2)]
            dkS = [(sb("dk", [128, 8], F32), Tl()) for _ in range(2)]
            kdA = [sb("kdA", [128, 128], BF16) for _ in range(2)]
            kdB = [sb("kdB", [128, 128], BF16) for _ in range(2)]
            kdAT, kdBT = [Tl(), Tl()], [Tl(), Tl()]
            ST = [sb("ST", [128, 128], BF16) for _ in range(2)]
            STT = [Tl(), Tl()]
            st = [sb("hst", [128, 4], F32) for _ in range(2)]
            stT = [Tl(), Tl()]
            bo = [sb("bo", [128, 128], BF16) for _ in range(2)]
            boT_ = [Tl(), Tl()]
            Sall = sb("Sall", [128, 9, 128], BF16)
            SallT = [Tl() for _ in range(9)]
            boT = [sb("boT", [128, 512], BF16) for _ in range(4)]
            boTT = [Tl() for _ in range(4)]
            b3 = bb.rearrange("p (c t) -> p c t", t=64)
            tmp3 = tmp.rearrange("p (c t) -> p c t", t=64)
            n = 0
            for g in range(NG):
                gs = slice(g * 512, (g + 1) * 512)
                for j in range(4):
                    tt = g * 4 + j
                    pst = banks[n % 2]
                    n += 1
                    for kc in range(8):
                        p.op("pe", lambda e_: e_.matmul(pst.ap, lhsT=hn_sb[:, kc, tt * 128:(tt + 1) * 128],
                                                        rhs=wig[:, kc, 0:512], start=(kc == 0), stop=(kc == 7)),
                             rd=[WB, hnT[g]], wr=[pst])
                    p.op("act", lambda e_: e_.activation(out=vtok[:, j, :], in_=pst.ap, func=AF.Copy),
                         rd=[pst], wr=[vtT[j]])
                    pst = banks[n % 2]
                    n += 1
                    for kc in range(8):
                        p.op("pe", lambda e_: e_.matmul(pst.ap, lhsT=hn_sb[:, kc, tt * 128:(tt + 1) * 128],
                                                        rhs=wig[:, kc, 512:1024], start=(kc == 0), stop=(kc == 7)),
                             rd=[WB, hnT[g]], wr=[pst])
                    p.op("act", lambda e_: e_.activation(out=sg, in_=pst.ap, func=AF.Silu), rd=[pst], wr=[sgT])
                    p.op("dve", lambda e_: e_.tensor_tensor(out=gsg[:, j, :], in0=sg, in1=pr("hon%d" % e), op=ALU.mult),
                         rd=[sgT, prmT], wr=[gsT[j]])
                def h_pro(hh):
                    nonlocal n
                    k2 = hh % 2
                    qtil, qtilT = qtilS[k2]
                    ktil, ktilT = ktilS[k2]
                    qbA, qbAT = qbAS[k2]
                    qbB, qbBT = qbBS[k2]
                    kdT_, kdTT = kdTS[k2]
                    dk, dkT = dkS[k2]
                    pq, pf = banks[n % 2], banks[(n + 1) % 2]
                    for pst, c0 in ((pq, hh * 128), (pf, 512 + hh * 128)):
                        for kc in range(8):
                            p.op("pe", lambda e_: e_.matmul(pst.ap, lhsT=wqf[:, kc, c0:c0 + 128], rhs=hn_sb[:, kc, gs],
                                                            start=(kc == 0), stop=(kc == 7)), rd=[WA, hnT[g]], wr=[pst])
                    p.op("act", lambda e_: e_.activation(out=qs, in_=pq.ap, func=AF.Silu), rd=[pq], wr=[qsT])
                    p.op("act", lambda e_: e_.activation(out=sig, in_=pf.ap, func=AF.Tanh, scale=0.5), rd=[pf], wr=[sigT])
                    p.op("dve", lambda e_: e_.tensor_scalar(out=sig, in0=sig, scalar1=0.5, scalar2=0.5, op0=ALU.mult,
                                                            op1=ALU.add), rd=[sigT], wr=[sigT])
                    p.op("dve", lambda e_: e_.tensor_scalar(out=lg, in0=sig, scalar1=oml[:, hh:hh + 1],
                                                            scalar2=lb[:, hh:hh + 1], op0=ALU.mult, op1=ALU.add),
                         rd=[sigT, lbT], wr=[lgT])
                    p.op("act", lambda e_: e_.activation(out=lg, in_=lg, func=AF.Ln), rd=[lgT], wr=[lgT])
                    p.op("dve", lambda e_: e_.tensor_scalar(out=kin, in0=sig, scalar1=noml[:, hh:hh + 1],
                                                            scalar2=oml[:, hh:hh + 1], op0=ALU.mult, op1=ALU.add),
                         rd=[sigT, lbT], wr=[kinT])
                    p.op("dve", lambda e_: e_.tensor_tensor_scan(out=bb, data0=cs("scanm"), data1=lg, initial=0.0,
                                                                 op0=ALU.mult, op1=ALU.add), rd=[cstT, lgT], wr=[bbT])
                    p.op("dve", lambda e_: e_.tensor_tensor(out=tmp3, in0=b3, in1=b3[:, :, 31:32].broadcast_to([128, 8, 64]),
                                                            op=ALU.subtract), rd=[bbT], wr=[tmpT])
                    p.op("act", lambda e_: e_.activation(out=ex, in_=tmp, func=AF.Exp), rd=[tmpT], wr=[exT])
                    p.op("dve", lambda e_: e_.tensor_tensor(out=qtil, in0=qs, in1=ex, op=ALU.mult), rd=[qsT, exT], wr=[qtilT])
                    p.op("act", lambda e_: e_.activation(out=ex, in_=tmp, func=AF.Exp, scale=-1.0), rd=[tmpT], wr=[exT])
                    p.op("dve", lambda e_: e_.tensor_tensor(out=ktil, in0=kin, in1=ex, op=ALU.mult), rd=[kinT, exT], wr=[ktilT])
                    p.op("act", lambda e_: e_.activation(out=ex, in_=bb, func=AF.Exp), rd=[bbT], wr=[exT])
                    p.op("dve", lambda e_: e_.tensor_tensor(out=qbf, in0=qs, in1=ex, op=ALU.mult), rd=[qsT, exT], wr=[qbfT])
                    p.op("dve", lambda e_: e_.tensor_tensor(out=qbA.rearrange("p (j c) -> p j c", c=128), in0=qbf.rearrange("p (j c) -> p j c", c=128), in1=cs("colA").unsqueeze(1).broadcast_to([128, 4, 128]), op=ALU.mult),
                         rd=[qbfT, cstT], wr=[qbAT])
                    p.op("dve", lambda e_: e_.tensor_tensor(out=qbB.rearrange("p (j c) -> p j c", c=128), in0=qbf.rearrange("p (j c) -> p j c", c=128), in1=cs("colB").unsqueeze(1).broadcast_to([128, 4, 128]), op=ALU.mult),
                         rd=[qbfT, cstT], wr=[qbBT])
                    p.op("dve", lambda e_: e_.tensor_tensor(out=tmp3, in0=b3, in1=b3[:, :, 63:64].broadcast_to([128, 8, 64]),
                                                            op=ALU.subtract), rd=[bbT], wr=[tmpT])
                    p.op("act", lambda e_: e_.activation(out=ex, in_=tmp, func=AF.Exp, scale=-1.0), rd=[tmpT], wr=[exT])
                    p.op("dve", lambda e_: e_.tensor_tensor(out=kdT_, in0=kin, in1=ex, op=ALU.mult), rd=[kinT, exT], wr=[kdTT])
                    p.op("act", lambda e_: e_.activation(out=dk, in_=b3[:, :, 63], func=AF.Exp), rd=[bbT], wr=[dkT])

                def h_inner(hh):
                    k2 = hh % 2
                    qtil, qtilT = qtilS[k2]
                    ktil, ktilT = ktilS[k2]
                    qbA, qbAT = qbAS[k2]
                    qbB, qbBT = qbBS[k2]
                    kdT_, kdTT = kdTS[k2]
                    dk, dkT = dkS[k2]
                    p.op("act", lambda e_: e_.activation(out=Sall[:, 0, :], in_=Sst[:, hh, :], func=AF.Copy),
                         rd=[SstT[hh]], wr=[SallT[0]])

                    def stF(j):
                        kk = j % 2
                        cl = slice(j * 128, (j + 1) * 128)
                        vj = vtok[:, j, hh * 128:(hh + 1) * 128]
                        ptr, pu, psc = banks[4], banks[2 + kk], banks[6]
                        ptv = ptr.ap.bitcast(BF16)[:, 0:128]
                        p.op("pe", lambda e_: e_.transpose(out=ptv, in_=kdT_[:, cl], identity=identb),
                             rd=[kdTT, cbT], wr=[ptr])
                        p.op("dve", lambda e_: e_.tensor_scalar(out=kdA[kk], in0=ptv, scalar1=cs("rowA"), scalar2=None,
                                                                op0=ALU.mult), rd=[ptr, cstT], wr=[kdAT[kk]])
                        p.op("dve", lambda e_: e_.tensor_scalar(out=kdB[kk], in0=ptv, scalar1=cs("rowB"), scalar2=None,
                                                                op0=ALU.mult), rd=[ptr, cstT], wr=[kdBT[kk]])
                        p.op("pe", lambda e_: e_.matmul(pu.ap[:, 0:128], lhsT=kdA[kk], rhs=vj, start=True, stop=True),
                             rd=[kdAT[kk], vtT[j]], wr=[pu])
                        p.op("pe", lambda e_: e_.matmul(pu.ap[:, 128:256], lhsT=kdB[kk], rhs=vj, start=True, stop=True),
                             rd=[kdBT[kk], vtT[j]], wr=[pu])
                        p.op("pe", lambda e_: e_.matmul(psc.ap[:, 0:128], lhsT=ktil[:, cl], rhs=qtil[:, cl],
                                                        start=True, stop=True), rd=[ktilT, qtilT], wr=[psc])
                        p.op("dve", lambda e_: e_.tensor_tensor(out=ST[kk], in0=psc.ap[:, 0:128], in1=cs("maskbd"),
                                                                op=ALU.mult), rd=[psc, cstT], wr=[STT[kk]])
                        for half in range(2):
                            ci = 2 * j + half
                            p.op("dve", lambda e_: e_.scalar_tensor_tensor(out=Sst[:, hh, :], in0=Sst[:, hh, :],
                                                                           scalar=dk[:, ci:ci + 1],
                                                                           in1=pu.ap[:, half * 128:(half + 1) * 128],
                                                                           op0=ALU.mult, op1=ALU.add),
                                 rd=[SstT[hh], dkT, pu], wr=[SstT[hh]])
                            p.op("act", lambda e_: e_.activation(out=Sall[:, ci + 1, :], in_=Sst[:, hh, :], func=AF.Copy),
                                 rd=[SstT[hh]], wr=[SallT[ci + 1]])

                    def stBk(j):
                        kk = j % 2
                        cl = slice(j * 128, (j + 1) * 128)
                        vj = vtok[:, j, hh * 128:(hh + 1) * 128]
                        po, ptr2 = banks[7], banks[5]
                        p.op("pe", lambda e_: e_.matmul(po.ap[:, 0:128], lhsT=ST[kk], rhs=vj, start=True, stop=False),
                             rd=[STT[kk], vtT[j]], wr=[po])
                        p.op("pe", lambda e_: e_.matmul(po.ap[:, 0:128], lhsT=qbA[:, cl], rhs=Sall[:, 2 * j, :],
                                                        start=False, stop=False), rd=[qbAT, SallT[2 * j]], wr=[po])
                        p.op("pe", lambda e_: e_.matmul(po.ap[:, 0:128], lhsT=qbB[:, cl], rhs=Sall[:, 2 * j + 1, :],
                                                        start=False, stop=True), rd=[qbBT, SallT[2 * j + 1]], wr=[po])
                        p.op("act", lambda e_: e_.activation(out=bo[kk], in_=po.ap[:, 0:128], func=AF.Square,
                                                             accum_out=st[kk][:, 0:1]), rd=[po], wr=[boT_[kk], stT[kk]])
                        rstd_chain(st[kk][:, 0:1], st[kk][:, 1:2], st[kk][:, 2:3], 128, stT[kk])
                        p.op("dve", lambda e_: e_.scalar_tensor_tensor(out=bo[kk], in0=po.ap[:, 0:128], scalar=st[kk][:, 2:3],
                                                                       in1=gsg[:, j, hh * 128:(hh + 1) * 128],
                                                                       op0=ALU.mult, op1=ALU.mult),
                             rd=[po, stT[kk], gsT[j]], wr=[boT_[kk]])
                        ptv2 = ptr2.ap.bitcast(BF16)[:, 0:128]
                        p.op("pe", lambda e_: e_.transpose(out=ptv2, in_=bo[kk], identity=identb), rd=[boT_[kk], cbT], wr=[ptr2])
                        p.op("act", lambda e_: e_.activation(out=boT[hh][:, cl], in_=ptv2, func=AF.Copy),
                             rd=[ptr2], wr=[boTT[hh]])

                    stF(0)
                    for j in range(4):
                        if j + 1 < 4:
                            stF(j + 1)
                        stBk(j)

                h_pro(0)
                for hh in range(4):
                    if hh + 1 < 4:
                        h_pro(hh + 1)
                    h_inner(hh)
                for j in range(4):
                    tt = g * 4 + j
                    for hf in range(2):
                        pst = banks[n % 2]
                        n += 1
                        for hh in range(4):
                            p.op("pe", lambda e_: e_.matmul(pst.ap, lhsT=boT[hh][:, j * 128:(j + 1) * 128],
                                                            rhs=woH[:, hh, hf * 512:(hf + 1) * 512],
                                                            start=(hh == 0), stop=(hh == 3)),
                                 rd=[boTT[hh], WS0, WS1], wr=[pst])
                        add_h(tt, hf, pst, pst.ap)

    def ssd_mixer(l, o):
        Win = W["ssm_in"][o]
        Wout = W["ssm_out"][o]
        wdt = wload([WT], WT.ap, Win[:, 6144:6176], 8, 32)

        def load_group(g):
            WG = WA if g % 2 == 0 else WB
            WO = WS0 if g % 2 == 0 else WS1
            return (wload([WG], WG.ap, Win[:, g * 256:(g + 1) * 256], 8, 256, col0=0),
                    wload([WG], WG.ap, Win[:, 2048 + g * 256:2048 + (g + 1) * 256], 8, 256, col0=2048),
                    wload([WG], WG.ap, Win[:, 4096 + g * 128:4096 + (g + 1) * 128], 8, 128, col0=4096),
                    wload([WG], WG.ap, Win[:, 5120 + g * 128:5120 + (g + 1) * 128], 8, 128, col0=5120),
                    wload([WO], WO.ap, Wout[g * 256:(g + 1) * 256, :], 2, 1024))

        gw = {0: load_group(0)}
        with p.phase() as sb:
            rmsnorm_T(sb, NT, h_src, pr("nmix%d" % l), hn_dst, (6, 7))
        with p.phase() as sb:
            NH = NT * 32
            N4 = NT * 4
            dt = sb("dt", [128, NT, 32], F32)
            dt2 = dt.rearrange("p t h -> p (t h)")
            dtT = Tl()
            da = sb("da", [128, NT, 32], F32)
            daT = Tl()
            abc = sb("abc", [128, 32], F32)
            abcT = Tl()
            pd = banks[0]
            for tt in range(NT):
                for kc in range(8):
                    p.op("pe", lambda e_: e_.matmul(pd.ap[:, tt * 32:(tt + 1) * 32],
                                                    lhsT=hn_sb[:, kc, tt * 128:(tt + 1) * 128], rhs=wdt[:, kc, :],
                                                    start=(kc == 0), stop=(kc == 7)), rd=[hnT[tt // 4], WT], wr=[pd])
            p.op("dve", lambda e_: e_.tensor_tensor(out=dt, in0=pd.ap[:, 0:NH].rearrange("p (t h) -> p t h", h=32),
                                                    in1=pr("dtb%d" % o).unsqueeze(1).broadcast_to([128, NT, 32]),
                                                    op=ALU.add), rd=[pd, prmT], wr=[dtT])
            p.op("act", lambda e_: e_.activation(out=dt2, in_=dt2, func=AF.Exp), rd=[dtT], wr=[dtT])
            p.op("dve", lambda e_: e_.tensor_scalar(out=dt2, in0=dt2, scalar1=1.0, scalar2=None, op0=ALU.add),
                 rd=[dtT], wr=[dtT])
            p.op("act", lambda e_: e_.activation(out=dt2, in_=dt2, func=AF.Ln), rd=[dtT], wr=[dtT])
            p.op("act", lambda e_: e_.activation(out=abc, in_=pr("alog%d" % o), func=AF.Exp), rd=[prmT], wr=[abcT])
            p.op("dve", lambda e_: e_.tensor_scalar(out=abc, in0=abc, scalar1=-1.0, scalar2=None, op0=ALU.mult),
                 rd=[abcT], wr=[abcT])
            p.op("dve", lambda e_: e_.tensor_tensor(out=da, in0=dt, in1=abc.unsqueeze(1).broadcast_to([128, NT, 32]),
                                                    op=ALU.mult), rd=[dtT, abcT], wr=[daT])

            def sm(name):
                a = sb(name, [128, NT, 4], F32)
                return a, a.rearrange("p t h -> p (t h)"), Tl()

            ecum, ecum2, ecumT = sm("ecum")
            ecl, ecl2, eclT = sm("ecl")
            w_, w2, wT_ = sm("w")
            xg = sb("xg", [128, NT, 256], BF16)
            xgT = [Tl() for _ in range(NG)]
            BT = sb("BT", [128, S], BF16)
            BTt = [Tl() for _ in range(NG)]
            Bg = sb("Bg", [128, NT, 128], BF16)
            BgT = [Tl() for _ in range(NG)]
            CT = sb("CT", [128, S], BF16)
            CTt = [Tl() for _ in range(NG)]
            STall = sb("STall", [128, NT, 256], BF16)
            STaT = [Tl() for _ in range(NT)]
            STs = sb("STs", [128, 256], F32)
            STsT = Tl()
            cw = pr("cw%d" % o)
            cbp = pr("cb%d" % o)

            def v4(ap):
                return ap.rearrange("p (h d) -> p h d", d=64)

            def bc(ap):
                return ap.unsqueeze(2).broadcast_to([128, 4, 64])

            n = 0
            for g in range(8):
                WG = WA if g % 2 == 0 else WB
                WO = WS0 if g % 2 == 0 else WS1
                wz, wx, wB, wC, wo = gw[g]
                hs = slice(g * 4, g * 4 + 4)
                pcu, pcl = banks[6], banks[7]
                p.op("pe", lambda e_: e_.matmul(pcu.ap[:, 0:N4].rearrange("p (t h) -> p t h", h=4), lhsT=cs("U"),
                                                rhs=da[:, :, hs], start=True, stop=True), rd=[cstT, daT], wr=[pcu])
                p.op("pe", lambda e_: e_.matmul(pcl.ap[:, 0:N4].rearrange("p (t h) -> p t h", h=4), lhsT=cs("ones"),
                                                rhs=da[:, :, hs], start=True, stop=True), rd=[cstT, daT], wr=[pcl])
                p.op("act", lambda e_: e_.activation(out=ecum2, in_=pcu.ap[:, 0:N4], func=AF.Exp), rd=[pcu], wr=[ecumT])
                p.op("act", lambda e_: e_.activation(out=ecl2, in_=pcl.ap[:, 0:N4], func=AF.Exp), rd=[pcl], wr=[eclT])
                p.op("act", lambda e_: e_.activation(out=w2, in_=pcu.ap[:, 0:N4], func=AF.Copy), rd=[pcu], wr=[wT_])
                p.op("dve", lambda e_: e_.tensor_tensor(out=w2, in0=pcl.ap[:, 0:N4], in1=w2, op=ALU.subtract),
                     rd=[pcl, wT_], wr=[wT_])
                p.op("act", lambda e_: e_.activation(out=w2, in_=w2, func=AF.Exp), rd=[wT_], wr=[wT_])
                p.op("dve", lambda e_: e_.tensor_tensor(out=w_, in0=w_, in1=dt[:, :, hs], op=ALU.mult),
                     rd=[wT_, dtT], wr=[wT_])
                chunks = ((wx[:, :, 0:128], 2 * g, "x0"), (wx[:, :, 128:256], 2 * g + 1, "x1"),
                          (wB, 16 + g, "B"), (wC, 24 + g, "C"))
                with p.phase() as sb2:
                    xps = [sb2("xp", [128, 515], F32) for _ in range(2)]
                    xpTs = [Tl(), Tl()]
                    accs = [sb2("acc", [128, 512], F32) for _ in range(2)]
                    accTs = [Tl(), Tl()]
                    xTcs = [sb2("xTc", [128, 512], BF16) for _ in range(2)]
                    xTcTs = [Tl(), Tl()]
                    items = [(wsrc, ch, kind, tg) for (wsrc, ch, kind) in chunks for tg in range(NG)]
                    pend = {}

                    def stP1(idx):
                        nonlocal n
                        wsrc, ch, kind, tg = items[idx]
                        kk = idx % 2
                        xp, xpT = xps[kk], xpTs[kk]
                        pst = banks[n % 2]
                        n += 1
                        for kc in range(8):
                            p.op("pe", lambda e_: e_.matmul(pst.ap, lhsT=wsrc[:, kc, :], rhs=hn_sb[:, kc, tg * 512:(tg + 1) * 512],
                                                            start=(kc == 0), stop=(kc == 7)), rd=[WG, hnT[tg]], wr=[pst])
                        if tg == 0:
                            p.op("dve", lambda e_: e_.memset(xp[:, 0:3], 0.0), wr=[xpT])
                        else:
                            p.op("dve", lambda e_: e_.tensor_copy(out=xp[:, 0:3], in_=xps[1 - kk][:, 512:515]),
                                 rd=[xpTs[1 - kk]], wr=[xpT])
                        p.op("act", lambda e_: e_.activation(out=xp[:, 3:515], in_=pst.ap, func=AF.Copy),
                             rd=[pst], wr=[xpT])

                    def stP2(idx):
                        wsrc, ch, kind, tg = items[idx]
                        kk = idx % 2
                        xp, xpT, acc, accT = xps[kk], xpTs[kk], accs[kk], accTs[kk]
                        xTc, xTcT = xTcs[kk], xTcTs[kk]
                        p.op("dve", lambda e_: e_.tensor_scalar(out=acc, in0=xp[:, 0:512], scalar1=cw[:, ch * 4:ch * 4 + 1],
                                                                scalar2=None, op0=ALU.mult), rd=[xpT, prmT], wr=[accT])
                        for j in range(1, 4):
                            p.op("dve", lambda e_: e_.scalar_tensor_tensor(out=acc, in0=xp[:, j:j + 512],
                                                                           scalar=cw[:, ch * 4 + j:ch * 4 + j + 1], in1=acc,
                                                                           op0=ALU.mult, op1=ALU.add),
                                 rd=[xpT, prmT, accT], wr=[accT])
                        gsl = slice(tg * 512, (tg + 1) * 512)
                        if kind in ("x0", "x1"):
                            dst, dT = xTc, xTcT
                        elif kind == "B":
                            dst, dT = BT[:, gsl], BTt[tg]
                        else:
                            dst, dT = CT[:, gsl], CTt[tg]
                        p.op("act", lambda e_: e_.activation(out=dst, in_=acc, func=AF.Silu, bias=cbp[:, ch:ch + 1]),
                             rd=[accT, prmT], wr=[dT])
                        pend[idx] = (dst, dT)

                    def stT(idx):
                        wsrc, ch, kind, tg = items[idx]
                        if kind == "C":
                            return
                        dst, dT = pend[idx]
                        ptr = banks[2 + (idx % 2)]
                        ptv = ptr.ap.bitcast(BF16)[:, 0:512].rearrange("p (t c) -> p t c", c=128)
                        for j in range(4):
                            p.op("pe", lambda e_: e_.transpose(out=ptv[:, j, :], in_=dst[:, j * 128:(j + 1) * 128],
                                                               identity=identb), rd=[dT, cbT], wr=[ptr])
                        if kind == "B":
                            o_ap, oT_ = Bg[:, tg * 4:(tg + 1) * 4, :], BgT[tg]
                        else:
                            c0 = 0 if kind == "x0" else 128
                            o_ap, oT_ = xg[:, tg * 4:(tg + 1) * 4, c0:c0 + 128], xgT[tg]
                        p.op("act", lambda e_: e_.activation(out=o_ap, in_=ptv, func=AF.Copy), rd=[ptr], wr=[oT_])

                    stP1(0)
                    for idx in range(len(items) + 1):
                        if idx + 1 < len(items):
                            stP1(idx + 1)
                        if idx < len(items):
                            stP2(idx)
                        if idx >= 1:
                            stT(idx - 1)
                if g + 1 < 8:
                    gw[g + 1] = load_group(g + 1)
                with p.phase() as sb2:
                    def two(name, shape, dt_):
                        return [sb2(name, shape, dt_) for _ in range(2)], [Tl(), Tl()]

                    Lh4 = sb2("Lh4", [128, 4, 128], F32)
                    Lh4T = Tl()
                    eE4s, eE4Ts = two("eE4", [128, 4, 128], F32)
                    ez1 = sb2("ez", [128, 256], F32)
                    ez, ezT = [ez1, ez1], [Tl()] * 2
                    sz = [sb2("sz", [128, 256], F32) for _ in range(3)]
                    szT = [Tl(), Tl(), Tl()]
                    CBm, CBmT = two("CBm", [128, 128], F32)
                    MT4, MT4T = two("MT4", [128, 4, 128], BF16)
                    xdt1 = sb2("xdt", [128, 256], BF16)
                    xdt, xdtT = [xdt1, xdt1], [Tl()] * 2
                    t1, t1T = two("t1", [128, 256], F32)
                    t2, t2T = two("t2", [128, 256], BF16)
                    xD, xDT = two("xD", [128, 256], BF16)
                    yn, ynT_ = two("yn", [128, 256], BF16)
                    ynT1 = sb2("ynT", [128, 2, 128], BF16)
                    ynT, ynTT = [ynT1, ynT1], [Tl()] * 2
                    xw1 = sb2("xw", [128, 256], BF16)
                    xw, xwT = [xw1, xw1], [Tl()] * 2
                    st, stT = two("sst", [128, 4], F32)
                    p.op("dve", lambda e_: e_.memset(STs, 0.0), wr=[STsT])
                    p.op("dve", lambda e_: e_.memset(STall[:, 0, :], 0.0), wr=[STaT[0]])

                    def pre_a(tt):
                        k = tt % 2
                        p.op("pool", lambda e_: e_.tensor_tensor(out=v4(xw[k]), in0=v4(xg[:, tt, :]), in1=bc(w_[:, tt, :]),
                                                                 op=ALU.mult), rd=[xgT[tt // 4], wT_], wr=[xwT[k]])
                        p.op("pe", lambda e_: e_.matmul(banks[2 + k].ap[:, 0:256], lhsT=Bg[:, tt, :], rhs=xw[k],
                                                        start=True, stop=True), rd=[BgT[tt // 4], xwT[k]], wr=[banks[2 + k]])

                    def pre_b(tt):
                        k = tt % 2
                        p.op("dve", lambda e_: e_.tensor_tensor(out=v4(STs), in0=v4(STs), in1=bc(ecl[:, tt, :]), op=ALU.mult),
                             rd=[STsT, eclT], wr=[STsT])
                        p.op("dve", lambda e_: e_.tensor_tensor(out=STs, in0=STs, in1=banks[2 + k].ap[:, 0:256], op=ALU.add),
                             rd=[STsT, banks[2 + k]], wr=[STsT])
                        p.op("act", lambda e_: e_.activation(out=STall[:, tt + 1, :], in_=STs, func=AF.Copy),
                             rd=[STsT], wr=[STaT[tt + 1]])

                    if NT > 1:
                        pre_a(0)
                    for tt in range(NT - 1):
                        if tt + 1 < NT - 1:
                            pre_a(tt + 1)
                        pre_b(tt)

                    def stA(tt):
                        k = tt % 2
                        cl = slice(tt * 128, (tt + 1) * 128)
                        tg = tt // 4
                        pz = banks[6]
                        for kc in range(8):
                            p.op("pe", lambda e_: e_.matmul(pz.ap[:, 0:256], lhsT=hn_sb[:, kc, cl], rhs=wz[:, kc, :],
                                                            start=(kc == 0), stop=(kc == 7)), rd=[WG, hnT[tg]], wr=[pz])
                        p.op("act", lambda e_: e_.activation(out=ez[k], in_=pz.ap[:, 0:256], func=AF.Exp, scale=-1.0),
                             rd=[pz], wr=[ezT[k]])
                        pcb = banks[7]
                        p.op("pe", lambda e_: e_.matmul(pcb.ap[:, 0:128], lhsT=BT[:, cl], rhs=CT[:, cl], start=True, stop=True),
                             rd=[BTt[tg], CTt[tg]], wr=[pcb])
                        p.op("dve", lambda e_: e_.tensor_tensor(out=CBm[k], in0=pcb.ap[:, 0:128], in1=cs("U"), op=ALU.mult),
                             rd=[pcb, cstT], wr=[CBmT[k]])
                        p.op("pool", lambda e_: e_.tensor_tensor(out=Lh4, in0=cs("SL").unsqueeze(1).broadcast_to([128, 4, 128]),
                                                                 in1=da[:, tt, hs].unsqueeze(2).broadcast_to([128, 4, 128]),
                                                                 op=ALU.mult), rd=[cstT, daT], wr=[Lh4T])
                        pE = banks[2 + k]
                        for hl in range(4):
                            p.op("pe", lambda e_: e_.matmul(pE.ap[:, hl * 128:(hl + 1) * 128], lhsT=Lh4[:, hl, :], rhs=cs("U"),
                                                            start=True, stop=True), rd=[Lh4T, cstT], wr=[pE])
                        p.op("act", lambda e_: e_.activation(out=eE4s[k].rearrange("p h t -> p (h t)"), in_=pE.ap, func=AF.Exp),
                             rd=[pE], wr=[eE4Ts[k]])
                        p.op("dve", lambda e_: e_.tensor_scalar(out=ez[k], in0=ez[k], scalar1=1.0, scalar2=None, op0=ALU.add),
                             rd=[ezT[k]], wr=[ezT[k]])
                        p.op("dve", lambda e_: e_.reciprocal(out=ez[k], in_=ez[k]), rd=[ezT[k]], wr=[ezT[k]])
                        p.op("dve", lambda e_: e_.tensor_tensor(out=sz[tt % 3], in0=ez[k], in1=pz.ap[:, 0:256], op=ALU.mult),
                             rd=[ezT[k], pz], wr=[szT[tt % 3]])

                    def stB(tt):
                        k = tt % 2
                        cl = slice(tt * 128, (tt + 1) * 128)
                        tg = tt // 4
                        p.op("dve", lambda e_: e_.tensor_tensor(out=MT4[k], in0=eE4s[k],
                                                                in1=CBm[k].unsqueeze(1).broadcast_to([128, 4, 128]), op=ALU.mult),
                             rd=[eE4Ts[k], CBmT[k]], wr=[MT4T[k]])
                        p.op("pool", lambda e_: e_.tensor_tensor(out=v4(xdt[k]), in0=v4(xg[:, tt, :]), in1=bc(dt[:, tt, hs]),
                                                                 op=ALU.mult), rd=[xgT[tg], dtT], wr=[xdtT[k]])
                        p.op("pool", lambda e_: e_.tensor_tensor(out=v4(xD[k]), in0=v4(xg[:, tt, :]),
                                                                 in1=bc(pr("dsk%d" % o)[:, hs]), op=ALU.mult),
                             rd=[xgT[tg], prmT], wr=[xDT[k]])
                        py = banks[4 + k]
                        for hl in range(4):
                            p.op("pe", lambda e_: e_.matmul(py.ap[:, hl * 64:(hl + 1) * 64], lhsT=MT4[k][:, hl, :],
                                                            rhs=xdt[k][:, hl * 64:(hl + 1) * 64], start=True, stop=False),
                                 rd=[MT4T[k], xdtT[k]], wr=[py])
                            p.op("pe", lambda e_: e_.matmul(py.ap[:, hl * 64:(hl + 1) * 64], lhsT=identb,
                                                            rhs=xD[k][:, hl * 64:(hl + 1) * 64], start=False, stop=True),
                                 rd=[cbT, xDT[k]], wr=[py])
                        p.op("pe", lambda e_: e_.matmul(py.ap[:, 256:512], lhsT=CT[:, cl], rhs=STall[:, tt, :],
                                                        start=True, stop=True), rd=[CTt[tg], STaT[tt]], wr=[py])

                    def stC(tt):
                        k = tt % 2
                        py = banks[4 + k]
                        p.op("dve", lambda e_: e_.tensor_tensor(out=v4(t1[k]), in0=v4(py.ap[:, 256:512]), in1=bc(ecum[:, tt, :]),
                                                                op=ALU.mult), rd=[py, ecumT], wr=[t1T[k]])
                        p.op("dve", lambda e_: e_.tensor_tensor(out=t1[k], in0=t1[k], in1=py.ap[:, 0:256], op=ALU.add),
                             rd=[t1T[k], py], wr=[t1T[k]])
                        p.op("dve", lambda e_: e_.tensor_tensor(out=t1[k], in0=t1[k], in1=sz[tt % 3], op=ALU.mult),
                             rd=[t1T[k], szT[tt % 3]], wr=[t1T[k]])
                        p.op("act", lambda e_: e_.activation(out=t2[k], in_=t1[k], func=AF.Square, accum_out=st[k][:, 0:1]),
                             rd=[t1T[k]], wr=[t2T[k], stT[k]])
                        rstd_chain(st[k][:, 0:1], st[k][:, 1:2], st[k][:, 2:3], 256, stT[k])
                        p.op("act", lambda e_: e_.activation(out=yn[k], in_=t1[k], func=AF.Copy, scale=st[k][:, 2:3]),
                             rd=[t1T[k], stT[k]], wr=[ynT_[k]])

                    def stD(tt):
                        nonlocal n
                        k = tt % 2
                        ptr = banks[7]
                        ptv = ptr.ap.bitcast(BF16)[:, 512:768].rearrange("p (j c) -> p j c", c=128)
                        for j in range(2):
                            p.op("pe", lambda e_: e_.transpose(out=ptv[:, j, :], in_=yn[k][:, j * 128:(j + 1) * 128],
                                                               identity=identb), rd=[ynT_[k], cbT], wr=[ptr])
                        p.op("dve", lambda e_: e_.tensor_tensor(out=ynT[k], in0=ptv,
                                                                in1=pr("snorm%d" % o)[:, 2 * g:2 * g + 2].unsqueeze(2).broadcast_to([128, 2, 128]),
                                                                op=ALU.mult), rd=[ptr, prmT], wr=[ynTT[k]])
                        for hf in range(2):
                            pst = banks[n % 2]
                            n += 1
                            for j in range(2):
                                p.op("pe", lambda e_: e_.matmul(pst.ap, lhsT=ynT[k][:, j, :], rhs=wo[:, j, hf * 512:(hf + 1) * 512],
                                                                start=(j == 0), stop=(j == 1)), rd=[ynTT[k], WO], wr=[pst])
                            add_h(tt, hf, pst, pst.ap)

                    for j in range(-1, NT + 2):
                        if 0 <= j + 1 < NT:
                            stA(j + 1)
                        if 0 <= j < NT:
                            stB(j)
                        if 0 <= j - 1 < NT:
                            stC(j - 1)
                        if 0 <= j - 2 < NT:
                            stD(j - 2)


    out_events = []
    for s_ in range(NSEQ):
        for tt in range(NT):
            p.dma("sp", h_sb[:, tt, :], x_d[s_, tt * 128:(tt + 1) * 128, :], wr=hT[tt])
        with p.phase() as sb:
            memraw = sb("memraw", [128, 2, D], F32)
            memrawT = [Tl(), Tl()]
            for kb in range(2):
                p.dma("sp", memraw[:, kb, :], mem_d[s_, kb * 128:(kb + 1) * 128, :], wr=[memrawT[kb]])
            rmsnorm_T(sb, 2, lambda tt: ([memrawT[tt]], memraw[:, tt, :]), pr("mem_norm"),
                      lambda tt: (memT, mem_sb[:, :, tt * 128:(tt + 1) * 128]), (6, 7))
        ei = oi = 0
        for l, kd in enumerate(kinds):
            if "mix" in flags:
                if kd == "e":
                    even_mixer(l, ei)
                else:
                    ssd_mixer(l, oi)
            if kd == "e":
                ei += 1
            else:
                oi += 1
            if "xa" in flags:
                xattn(l)
            if "mlp" in flags:
                mlp(l)
        with p.phase() as sb:
            gfin = sb("gfin", [128, D], F32)
            gfT = Tl()
            outb = [sb("outb", [128, D], F32) for i in range(2)]
            outT = [Tl(), Tl()]
            dg = sb("dg", [128, 128], F32)
            dgT = Tl()
            for kc in range(8):
                p.op("dve", lambda e: e.tensor_scalar(out=dg, in0=cs("ident"),
                                                      scalar1=pr("norm_final")[:, kc:kc + 1], scalar2=None,
                                                      op0=ALU.mult), rd=[cstT, prmT], wr=[dgT])
                pst = banks[kc % 2]
                p.op("pe", lambda e: e.matmul(pst.ap[:, 0:128], lhsT=cs("ones"), rhs=dg, start=True, stop=True),
                     rd=[cstT, dgT], wr=[pst])
                p.op("act", lambda e: e.activation(out=gfin[:, kc * 128:(kc + 1) * 128], in_=pst.ap[:, 0:128],
                                                   func=AF.Copy), rd=[pst], wr=[gfT])
            junk = sb("junkf", [128, D], BF16)
            junkT = Tl()
            st = sb("stf", [128, 8], F32)
            stT = [Tl(), Tl()]
            for tt in range(NT):
                k = tt % 2
                ap = h_sb[:, tt, :]
                ss, sd, rs = st[:, 4 * k:4 * k + 1], st[:, 4 * k + 1:4 * k + 2], st[:, 4 * k + 2:4 * k + 3]
                p.op("act", lambda e: e.activation(out=junk, in_=ap, func=AF.Square, accum_out=ss),
                     rd=hT[tt], wr=[junkT, stT[k]])
                rstd_chain(ss, sd, rs, D, stT[k])
                p.op("dve", lambda e: e.scalar_tensor_tensor(out=outb[k], in0=ap, scalar=rs, in1=gfin,
                                                             op0=ALU.mult, op1=ALU.mult),
                     rd=list(hT[tt]) + [stT[k], gfT], wr=[outT[k]])
                out_events.append(p.dma("sp", y_d[s_, tt * 128:(tt + 1) * 128, :], outb[k], rd=[outT[k]]))
    for key, val in out_events[-(NDS // 2):]:
        p.need("sp", key, val)
    C.nins = p.nins
    return nc, C


KINDS = ("e", "o", "e", "o")
NCORES = 8


def run(inp, S, NSEQ, kinds, flags=("mix", "xa", "mlp")):
    nc, C = build(S, NSEQ, kinds, flags)
    cst = make_consts()
    prm = make_params(kinds, inp)
    names = ["xa_q", "xa_kv", "xa_o", "mlp_up", "mlp_down"]
    if "e" in kinds:
        names += ["ev_in_proj", "ev_out_proj"]
    if "o" in kinds:
        names += ["ssm_in_proj", "ssm_out_proj"]
    shared = {k: np.ascontiguousarray(np.asarray(inp[k], np.float32)) for k in names}
    x = np.asarray(inp["x"], np.float32)
    mem = np.asarray(inp["mem"], np.float32)
    in_maps = []
    for c in range(NCORES):
        d = dict(shared)
        d["x"] = np.ascontiguousarray(x[c * NSEQ:(c + 1) * NSEQ])
        d["mem"] = np.ascontiguousarray(mem[c * NSEQ:(c + 1) * NSEQ])
        d["cst"] = cst
        d["prm"] = prm
        in_maps.append(d)
    res = run_bass_kernel_spmd(nc, in_maps, core_ids=list(range(NCORES)))
    return np.concatenate([np.asarray(r["y"]) for r in res.results], axis=0)


def kernel(**inputs):
    x = inputs["x"]
    B, S, _ = x.shape
    return run(inputs, S, B // NCORES, KINDS).astype(np.float32)
```

```python
import numpy as np
from contextlib import contextmanager, ExitStack
import concourse.bass as bass
import concourse.mybir as mybir
from concourse.bass_utils import run_bass_kernel_spmd

F32 = mybir.dt.float32
BF16 = mybir.dt.bfloat16
AF = mybir.ActivationFunctionType
ALU = mybir.AluOpType
D = 1024
EPS = 1e-6
MEM = 256
NDS = 24


class Tl:
    __slots__ = ("ap", "w", "r")

    def __init__(self, ap=None):
        self.ap = ap
        self.w = None
        self.r = {}


class Prog:
    def __init__(self, nc):
        self.nc = nc
        self.eng = dict(pe=nc.tensor, act=nc.scalar, dve=nc.vector, pool=nc.gpsimd, sp=nc.sync)
        self.sem = {k: nc.alloc_semaphore("s_" + k) for k in self.eng}
        self.cnt = {k: 0 for k in self.eng}
        self.seen = {k: {} for k in self.eng}
        for i in range(NDS):
            self.sem[("d", i)] = nc.alloc_semaphore("d%d" % i)
        self.dval = [0] * NDS
        self.dnext = {"sp": 0, "pool": 0}
        self.uid = 0
        self.nins = 0

    def need(self, e, key, val):
        if e == "pe" and key == "pe":
            return
        if self.seen[e].get(key, 0) >= val:
            return
        self.eng[e].wait_ge(self.sem[key], val)
        self.seen[e][key] = val
        self.nins += 1

    def deps(self, e, rd, wr):
        for t in rd:
            if t.w is not None:
                self.need(e, t.w[0], t.w[1])
        for t in wr:
            if t.w is not None:
                self.need(e, t.w[0], t.w[1])
            for k, v in t.r.items():
                self.need(e, k, v)

    def mark(self, key, val, rd, wr):
        for t in rd:
            t.r[key] = val
        for t in wr:
            t.w = (key, val)
            t.r = {}

    def op(self, e, fn, rd=(), wr=()):
        self.deps(e, rd, wr)
        ins = fn(self.eng[e])
        self.cnt[e] += 1
        ins.then_inc(self.sem[e], 1)
        self.mark(e, self.cnt[e], rd, wr)
        self.nins += 1
        return ins

    def dma(self, q, out, in_, rd=(), wr=()):
        half = NDS // 2
        i = self.dnext[q] + (0 if q == "sp" else half)
        self.dnext[q] = (self.dnext[q] + 1) % half
        key = ("d", i)
        if self.dval[i] > 0:
            self.need(q, key, self.dval[i])
        self.deps(q, rd, wr)
        ins = self.eng[q].dma_start(out=out, in_=in_)
        self.dval[i] += 16
        ins.then_inc(self.sem[key], 16)
        self.mark(key, self.dval[i], rd, wr)
        self.nins += 1
        return key, self.dval[i]

    def barrier(self):
        es = ("pe", "act", "dve")
        for e in es:
            for x in es:
                if x != e and self.cnt[x] > 0:
                    self.need(e, x, self.cnt[x])

    @contextmanager
    def phase(self):
        es = ExitStack()

        def sb(name, shape, dt):
            self.uid += 1
            t = es.enter_context(self.nc.sbuf_tensor("%s_%d" % (name, self.uid), list(shape), dt))
            return t.ap()

        yield sb
        self.barrier()
        es.close()


def const_layout():
    off = {}
    c = 0
    for name, n in (("ident", 128), ("U", 128), ("SL", 128), ("ones", 128), ("sel64", 128),
                    ("maskbd", 128), ("rowA", 1), ("rowB", 1), ("colA", 128), ("colB", 128),
                    ("scanm", 512)):
        off[name] = (c, n)
        c += n
    return off, c


def make_consts():
    off, n = const_layout()
    a = np.zeros((128, n), np.float32)
    r = np.arange(128)[:, None]
    t = np.arange(128)[None, :]

    def put(name, v):
        o, w = off[name]
        a[:, o:o + w] = v

    put("ident", (r == t))
    put("U", (r <= t))
    put("SL", (r > t))
    put("ones", 1.0)
    put("sel64", np.broadcast_to(r == 64, (128, 128)))
    put("maskbd", (r <= t) & ((r // 64) == (t // 64)))
    put("rowA", (r < 64))
    put("rowB", (r >= 64))
    t5 = np.arange(512)[None, :]
    put("colA", np.broadcast_to(t < 64, (128, 128)))
    put("colB", np.broadcast_to(t >= 64, (128, 128)))
    put("scanm", np.broadcast_to((t5 % 64) != 0, (128, 512)))
    return a


def param_layout(kinds):
    L = len(kinds)
    ne = sum(1 for k in kinds if k == "e")
    no = L - ne
    off = {}
    c = [0]

    def add(name, n):
        off[name] = (c[0], n)
        c[0] += n

    add("mem_norm", 8)
    add("norm_final", 8)
    for l in range(L):
        add("nmix%d" % l, 8)
        add("nxa%d" % l, 8)
        add("nmlp%d" % l, 8)
    for e in range(ne):
        add("fgb%d" % e, 8)
        add("hon%d" % e, 512)
    for e in range(max(ne, 1)):
        add("lbl%d" % e, 4)
    for o in range(no):
        add("cw%d" % o, 128)
        add("cb%d" % o, 32)
        add("dtb%d" % o, 32)
        add("alog%d" % o, 32)
        add("dsk%d" % o, 32)
        add("snorm%d" % o, 16)
    return off, c[0]


def fm(v, n):
    return np.ascontiguousarray(np.asarray(v, np.float32).reshape(n, 128).T)


def make_params(kinds, inp):
    off, n = param_layout(kinds)
    a = np.zeros((128, n), np.float32)

    def put(name, v):
        o, w = off[name]
        a[:, o:o + w] = v

    put("mem_norm", fm(inp["mem_norm"], 8))
    put("norm_final", fm(inp["norm_final"], 8))
    e = o = 0
    for l, k in enumerate(kinds):
        put("nmix%d" % l, fm(inp["norm_mix"][l], 8))
        put("nxa%d" % l, fm(inp["norm_xattn"][l], 8))
        put("nmlp%d" % l, fm(inp["norm_mlp"][l], 8))
        if k == "e":
            put("fgb%d" % e, np.broadcast_to(np.asarray(inp["fox_fgate_bias"][e])[None, :], (128, 8)))
            put("hon%d" % e, np.broadcast_to(np.asarray(inp["hgrn_out_norm"][e])[None, :], (128, 512)))
            put("lbl%d" % e, fm(inp["hgrn_lb_logits"][e], 4))
            e += 1
        else:
            cw = np.asarray(inp["ssm_conv_w"][o], np.float32)
            put("cw%d" % o, np.ascontiguousarray(cw.reshape(4, 32, 128).transpose(2, 1, 0)).reshape(128, 128))
            put("cb%d" % o, fm(inp["ssm_conv_b"][o], 32))
            put("dtb%d" % o, np.broadcast_to(np.asarray(inp["ssm_dt_bias"][o])[None, :], (128, 32)))
            put("alog%d" % o, np.broadcast_to(np.asarray(inp["ssm_A_log"][o])[None, :], (128, 32)))
            put("dsk%d" % o, np.broadcast_to(np.asarray(inp["ssm_D"][o])[None, :], (128, 32)))
            put("snorm%d" % o, fm(inp["ssm_norm"][o], 16))
            o += 1
    return a


class Ctx:
    pass


def build(S, NSEQ, kinds, flags=("mix", "xa", "mlp")):
    nc = bass.Bass("TRN2", target_bir_lowering=False)
    p = Prog(nc)
    C = Ctx()
    C.S, C.NT, C.NG = S, S // 128, S // 512
    NT, NG = C.NT, C.NG
    L = len(kinds)
    ne = sum(1 for k in kinds if k == "e")
    no = L - ne
    coff, cn = const_layout()
    poff, pn = param_layout(kinds)

    def din(name, shape):
        return nc.dram_tensor(name, list(shape), F32, kind="ExternalInput").ap()

    x_d = din("x", [NSEQ, S, D])
    mem_d = din("mem", [NSEQ, MEM, D])
    cst_d = din("cst", [128, cn])
    prm_d = din("prm", [128, pn])
    W = {}
    if ne:
        W["ev_in"] = din("ev_in_proj", [ne, D, 3592])
        W["ev_out"] = din("ev_out_proj", [ne, D, D])
    if no:
        W["ssm_in"] = din("ssm_in_proj", [no, D, 6176])
        W["ssm_out"] = din("ssm_out_proj", [no, 2048, D])
    W["xa_q"] = din("xa_q", [L, D, 512])
    W["xa_kv"] = din("xa_kv", [L, D, 1024])
    W["xa_o"] = din("xa_o", [L, 512, D])
    W["mlp_up"] = din("mlp_up", [L, D, 4096])
    W["mlp_down"] = din("mlp_down", [L, 4096, D])
    y_d = nc.dram_tensor("y", [NSEQ, S, D], F32, kind="ExternalOutput").ap()

    def sba(name, shape, dt):
        return nc.alloc_sbuf_tensor(name, list(shape), dt).ap()

    h_sb = sba("h", [128, NT, D], F32)
    hT = [[Tl(h_sb[:, tt, hf * 512:(hf + 1) * 512]) for hf in range(2)] for tt in range(NT)]
    hn_sb = sba("hnT", [128, 8, S], BF16)
    hnT = [Tl(hn_sb[:, :, g * 512:(g + 1) * 512]) for g in range(NG)]
    mem_sb = sba("memT", [128, 8, MEM], BF16)
    memT = Tl(mem_sb)
    cst = sba("cst_sb", [128, cn], F32)
    cstT = Tl(cst)
    prm = sba("prm_sb", [128, pn], F32)
    prmT = Tl(prm)
    cb_sb = sba("cstb", [128, 512], BF16)
    cbT = Tl(cb_sb)
    WA = Tl(sba("WA", [128, 8192], BF16))
    WB = Tl(sba("WB", [128, 8192], BF16))
    ws_ap = sba("WS", [128, 4096], BF16)
    WS0 = Tl(ws_ap[:, 0:2048])
    WS1 = Tl(ws_ap[:, 2048:4096])
    WT = Tl(sba("WT", [128, 8 * 32], BF16))
    banks = [Tl(nc.alloc_psum_tensor("ps%d" % i, [128, 512], F32).ap()) for i in range(8)]

    def cs(name):
        o, w = coff[name]
        return cst[:, o:o + w]

    def pr(name):
        o, w = poff[name]
        return prm[:, o:o + w]

    identb = cb_sb[:, 0:128]
    Ub = cb_sb[:, 128:256]
    onesb128 = cb_sb[:, 256:384]
    onesb = cb_sb[:, 256:320]

    p.dma("sp", cst, cst_d, wr=[cstT])
    p.dma("sp", prm, prm_d, wr=[prmT])
    p.op("dve", lambda e: e.tensor_copy(out=identb, in_=cs("ident")), rd=[cstT], wr=[cbT])
    p.op("dve", lambda e: e.tensor_copy(out=Ub, in_=cs("U")), rd=[cstT], wr=[cbT])
    p.op("dve", lambda e: e.tensor_copy(out=onesb128, in_=cs("ones")), rd=[cstT], wr=[cbT])
    epsc = sba("epsc", [128, 1], F32)
    epsT = Tl()
    p.op("dve", lambda e: e.memset(epsc, EPS), wr=[epsT])

    def rstd_chain(ss, lnv, rs, n, tl):
        p.op("act", lambda e: e.activation(out=lnv, in_=ss, func=AF.Ln, scale=1.0 / n, bias=epsc), rd=[tl, epsT], wr=[tl])
        p.op("act", lambda e: e.activation(out=rs, in_=lnv, func=AF.Exp, scale=-0.5), rd=[tl], wr=[tl])

    def wload(tiles, dst2d, src, kc, ncols, col0=0, pdim=128):
        dst = dst2d[0:pdim, col0:col0 + kc * ncols].rearrange("p (k n) -> p k n", k=kc)
        p.dma("pool", dst, src.rearrange("(k p) n -> p k n", p=pdim), wr=tiles)
        return dst

    def rmsnorm_T(sb, ntt, src, gcol, dst, psb):
        xn = [sb("xn", [128, D], BF16) for _ in range(2)]
        xnT = [Tl() for _ in range(2)]
        junk = sb("junk", [128, D], BF16)
        junkT = Tl()
        st = sb("st", [128, 8], F32)
        stT = [Tl(), Tl()]
        for tt in range(ntt):
            tiles, ap = src(tt)
            k = tt % 2
            ss, sd, rs = st[:, 4 * k:4 * k + 1], st[:, 4 * k + 1:4 * k + 2], st[:, 4 * k + 2:4 * k + 3]
            p.op("act", lambda e: e.activation(out=junk, in_=ap, func=AF.Square, accum_out=ss),
                 rd=tiles, wr=[junkT, stT[k]])
            rstd_chain(ss, sd, rs, D, stT[k])
            p.op("act", lambda e: e.activation(out=xn[k], in_=ap, func=AF.Copy, scale=rs),
                 rd=list(tiles) + [stT[k]], wr=[xnT[k]])
            pst = banks[psb[tt % len(psb)]]
            psv = pst.ap.bitcast(BF16)[:, 0:1024].rearrange("p (k n) -> p k n", k=8)
            for kc in range(8):
                p.op("pe", lambda e: e.transpose(out=psv[:, kc, :], in_=xn[k][:, kc * 128:(kc + 1) * 128],
                                                 identity=identb), rd=[xnT[k], cbT], wr=[pst])
            dt_, dap = dst(tt)
            p.op("dve", lambda e: e.tensor_tensor(out=dap, in0=psv,
                                                  in1=gcol.unsqueeze(2).broadcast_to([128, 8, 128]),
                                                  op=ALU.mult), rd=[pst, prmT], wr=[dt_])

    def h_src(tt):
        return hT[tt], h_sb[:, tt, :]

    def hn_dst(tt):
        return hnT[tt // 4], hn_sb[:, :, tt * 128:(tt + 1) * 128]

    def add_h(tt, hf, ps_t, ps_ap):
        p.op("dve", lambda e: e.tensor_tensor(out=hT[tt][hf].ap, in0=hT[tt][hf].ap, in1=ps_ap, op=ALU.add),
             rd=[ps_t, hT[tt][hf]], wr=[hT[tt][hf]])

    def mlp(l):
        wu0 = wload([WA], WA.ap, W["mlp_up"][l, :, 0:1024], 8, 1024)
        wd0 = wload([WB], WB.ap, W["mlp_down"][l, 0:1024, :], 8, 1024)
        with p.phase() as sb:
            rmsnorm_T(sb, NT, h_src, pr("nmlp%d" % l), hn_dst, (6, 7))
        with p.phase() as sb:
            aT = sb("aT", [128, 8, S], BF16)
            aTt = [Tl() for _ in range(NG)]
            rl = [sb("rl", [128, 512], F32) for _ in range(2)]
            rlT = [Tl(), Tl()]
            n = 0
            for fb in range(4):
                if fb == 0:
                    wu, wd = wu0, wd0
                else:
                    wu = wload([WA], WA.ap, W["mlp_up"][l, :, fb * 1024:(fb + 1) * 1024], 8, 1024)
                    wd = wload([WB], WB.ap, W["mlp_down"][l, fb * 1024:(fb + 1) * 1024, :], 8, 1024)
                for g in range(NG):
                    for fc in range(8):
                        pst = banks[n % 4]
                        for kc in range(8):
                            p.op("pe", lambda e: e.matmul(pst.ap, lhsT=wu[:, kc, fc * 128:(fc + 1) * 128],
                                                          rhs=hn_sb[:, kc, g * 512:(g + 1) * 512],
                                                          start=(kc == 0), stop=(kc == 7)),
                                 rd=[WA, hnT[g]], wr=[pst])
                        k = n % 2
                        p.op("act", lambda e: e.activation(out=rl[k], in_=pst.ap, func=AF.Relu),
                             rd=[pst], wr=[rlT[k]])
                        p.op("dve", lambda e: e.tensor_tensor(out=aT[:, fc, g * 512:(g + 1) * 512], in0=rl[k],
                                                              in1=rl[k], op=ALU.mult), rd=[rlT[k]], wr=[aTt[g]])
                        n += 1
                for tt in range(NT):
                    for hf in range(2):
                        pst = banks[4 + (n % 4)]
                        n += 1
                        for fc in range(8):
                            p.op("pe", lambda e: e.matmul(pst.ap, lhsT=aT[:, fc, tt * 128:(tt + 1) * 128],
                                                          rhs=wd[:, fc, hf * 512:(hf + 1) * 512],
                                                          start=(fc == 0), stop=(fc == 7)),
                                 rd=[WB, aTt[tt // 4]], wr=[pst])
                        add_h(tt, hf, pst, pst.ap)

    def xattn(l):
        wkv = wload([WA], WA.ap, W["xa_kv"][l], 8, 1024)
        wq = wload([WS0, WS1], ws_ap, W["xa_q"][l], 8, 512)
        wo = wload([WB], WB.ap, W["xa_o"][l], 4, 1024)
        with p.phase() as sb:
            rmsnorm_T(sb, NT, h_src, pr("nxa%d" % l), hn_dst, (6, 7))
        with p.phase() as sb:
            kT = sb("kT", [128, 4, MEM], BF16)
            kTt = Tl()
            va = sb("va", [128, 2, 4, 128], BF16)
            vat = Tl()
            qT = sb("qT", [128, 4, S], BF16)
            qTt = [Tl() for _ in range(NG)]
            n = 0
            for hh in range(4):
                pst = banks[n % 2]
                n += 1
                for kc in range(8):
                    p.op("pe", lambda e: e.matmul(pst.ap[:, 0:MEM], lhsT=wkv[:, kc, hh * 128:(hh + 1) * 128],
                                                  rhs=mem_sb[:, kc, :], start=(kc == 0), stop=(kc == 7)),
                         rd=[WA, memT], wr=[pst])
                p.op("act", lambda e: e.activation(out=kT[:, hh, :], in_=pst.ap[:, 0:MEM], func=AF.Copy),
                     rd=[pst], wr=[kTt])
            for kb in range(2):
                pst = banks[n % 2]
                n += 1
                for kc in range(8):
                    p.op("pe", lambda e: e.matmul(pst.ap, lhsT=mem_sb[:, kc, kb * 128:(kb + 1) * 128],
                                                  rhs=wkv[:, kc, 512:1024], start=(kc == 0), stop=(kc == 7)),
                         rd=[WA, memT], wr=[pst])
                p.op("act", lambda e: e.activation(out=va[:, kb, :, :],
                                                   in_=pst.ap.rearrange("p (h d) -> p h d", h=4), func=AF.Copy),
                     rd=[pst], wr=[vat])
            for g in range(NG):
                for hh in range(4):
                    pst = banks[n % 2]
                    n += 1
                    for kc in range(8):
                        p.op("pe", lambda e: e.matmul(pst.ap, lhsT=wq[:, kc, hh * 128:(hh + 1) * 128],
                                                      rhs=hn_sb[:, kc, g * 512:(g + 1) * 512],
                                                      start=(kc == 0), stop=(kc == 7)),
                             rd=[WS0, WS1, hnT[g]], wr=[pst])
                    p.op("act", lambda e: e.activation(out=qT[:, hh, g * 512:(g + 1) * 512], in_=pst.ap,
                                                       func=AF.Copy, scale=float(128 ** -0.5)),
                         rd=[pst], wr=[qTt[g]])
            PT = [sb("PT", [128, 512], BF16) for _ in range(4)]
            PTt = [Tl() for _ in range(4)]
            rs_ = [sb("rs", [128, 512], F32) for _ in range(2)]
            rst = [Tl(), Tl()]
            oT = [sb("oT", [128, 4, 512], BF16) for _ in range(2)]
            oTt = [Tl(), Tl()]
            items = [(g, hh) for g in range(NG) for hh in range(4)]

            def sbanks(idx):
                return (banks[0], banks[1]) if idx % 2 == 0 else (banks[6], banks[7])

            def xX(idx):
                g, hh = items[idx]
                for kb in range(2):
                    pst = sbanks(idx)[kb]
                    p.op("pe", lambda e: e.matmul(pst.ap, lhsT=kT[:, hh, kb * 128:(kb + 1) * 128],
                                                  rhs=qT[:, hh, g * 512:(g + 1) * 512], start=True, stop=True),
                         rd=[kTt, qTt[g]], wr=[pst])

            def xY(idx):
                nonlocal n
                g, hh = items[idx]
                og = g % 2
                pts = []
                for kb in range(2):
                    pst = sbanks(idx)[kb]
                    k = (2 * idx + kb) % 4
                    p.op("act", lambda e: e.activation(out=PT[k], in_=pst.ap, func=AF.Exp),
                         rd=[pst], wr=[PTt[k]])
                    pts.append(k)
                po, psum_ = banks[2 + (hh % 2) * 2], banks[3 + (hh % 2) * 2]
                for i, k in enumerate(pts):
                    p.op("pe", lambda e: e.matmul(po.ap, lhsT=va[:, i, hh, :], rhs=PT[k],
                                                  start=(i == 0), stop=(i == 1)), rd=[vat, PTt[k]], wr=[po])
                for i, k in enumerate(pts):
                    p.op("pe", lambda e: e.matmul(psum_.ap, lhsT=onesb128, rhs=PT[k],
                                                  start=(i == 0), stop=(i == 1)), rd=[cbT, PTt[k]], wr=[psum_])
                r = hh % 2
                p.op("dve", lambda e: e.reciprocal(out=rs_[r], in_=psum_.ap), rd=[psum_], wr=[rst[r]])
                p.op("dve", lambda e: e.tensor_tensor(out=oT[og][:, hh, :], in0=po.ap, in1=rs_[r], op=ALU.mult),
                     rd=[po, rst[r]], wr=[oTt[og]])
                if hh == 3:
                    for j in range(4):
                        tt = g * 4 + j
                        for hf in range(2):
                            pst = sbanks(idx)[n % 2]
                            n += 1
                            for kc in range(4):
                                p.op("pe", lambda e: e.matmul(pst.ap, lhsT=oT[og][:, kc, j * 128:(j + 1) * 128],
                                                              rhs=wo[:, kc, hf * 512:(hf + 1) * 512],
                                                              start=(kc == 0), stop=(kc == 3)),
                                     rd=[oTt[og], WB], wr=[pst])
                            add_h(tt, hf, pst, pst.ap)

            xX(0)
            for idx in range(len(items)):
                if idx + 1 < len(items):
                    xX(idx + 1)
                xY(idx)

    def even_mixer(l, e):
        Win = W["ev_in"][e]
        Wout = W["ev_out"][e]
        fw = (wload([WA], WA.ap, Win[:, 0:1024], 8, 1024),
              wload([WS0, WS1], ws_ap, Win[:, 1024:1536], 8, 512),
              wload([WT], WT.ap, Win[:, 1536:1544], 8, 8),
              wload([WB], WB.ap, Wout[0:512, :], 8, 1024, pdim=64))
        with p.phase() as sb:
            rmsnorm_T(sb, NT, h_src, pr("nmix%d" % l), hn_dst, (6, 7))
        fox(l, e, Win, Wout, fw)
        hgrn(l, e, Win, Wout)

    def fox(l, e, Win, Wout, fw):
        with p.phase() as sb:
            wqk, wv, wg, woF = fw
            NH = NT * 8
            lf = sb("lf", [128, NT, 8], F32)
            lfT = Tl()
            lf2 = lf.rearrange("p t h -> p (t h)")
            pg = banks[0]
            for tt in range(NT):
                for kc in range(8):
                    p.op("pe", lambda e_: e_.matmul(pg.ap[:, tt * 8:(tt + 1) * 8],
                                                    lhsT=hn_sb[:, kc, tt * 128:(tt + 1) * 128], rhs=wg[:, kc, :],
                                                    start=(kc == 0), stop=(kc == 7)), rd=[hnT[tt // 4], WT], wr=[pg])
            pgv = pg.ap[:, 0:NH].rearrange("p (t h) -> p t h", h=8)
            p.op("dve", lambda e_: e_.tensor_tensor(out=lf, in0=pgv,
                                                    in1=pr("fgb%d" % e).unsqueeze(1).broadcast_to([128, NT, 8]),
                                                    op=ALU.add), rd=[pg, prmT], wr=[lfT])
            p.op("act", lambda e_: e_.activation(out=lf, in_=lf, func=AF.Exp, scale=-1.0), rd=[lfT], wr=[lfT])
            p.op("dve", lambda e_: e_.tensor_scalar(out=lf, in0=lf, scalar1=1.0, scalar2=None, op0=ALU.add),
                 rd=[lfT], wr=[lfT])
            p.op("act", lambda e_: e_.activation(out=lf, in_=lf, func=AF.Ln), rd=[lfT], wr=[lfT])
            p.op("dve", lambda e_: e_.tensor_scalar(out=lf, in0=lf, scalar1=-1.0, scalar2=None, op0=ALU.mult),
                 rd=[lfT], wr=[lfT])
            pc, ptot, pm = banks[1], banks[2], banks[3]
            p.op("pe", lambda e_: e_.matmul(pc.ap[:, 0:NH], lhsT=cs("U"), rhs=lf2, start=True, stop=True),
                 rd=[cstT, lfT], wr=[pc])
            p.op("pe", lambda e_: e_.matmul(ptot.ap[:, 0:NH], lhsT=cs("ones"), rhs=lf2, start=True, stop=True),
                 rd=[cstT, lfT], wr=[ptot])
            tot = sb("tot", [128, NT, 8], F32)
            totT = Tl()
            offs = sb("offs", [128, NT, 8], F32)
            offT = Tl()
            c = sb("c", [128, NT, 8], F32)
            cT = Tl()
            cmb = sb("cmb", [128, NT, 8], F32)
            cmbT = Tl()
            p.op("act", lambda e_: e_.activation(out=tot.rearrange("p t h -> p (t h)"), in_=ptot.ap[:, 0:NH],
                                                 func=AF.Copy), rd=[ptot], wr=[totT])
            p.op("dve", lambda e_: e_.memset(offs[:, 0, :], 0.0), wr=[offT])
            for i in range(1, NT):
                p.op("dve", lambda e_: e_.tensor_tensor(out=offs[:, i, :], in0=offs[:, i - 1, :], in1=tot[:, i - 1, :],
                                                        op=ALU.add), rd=[totT, offT], wr=[offT])
            p.op("dve", lambda e_: e_.tensor_tensor(out=c.rearrange("p t h -> p (t h)"), in0=pc.ap[:, 0:NH],
                                                    in1=offs.rearrange("p t h -> p (t h)"), op=ALU.add),
                 rd=[pc, offT], wr=[cT])
            p.op("pe", lambda e_: e_.matmul(pm.ap[:, 0:NH], lhsT=cs("sel64"), rhs=c.rearrange("p t h -> p (t h)"),
                                            start=True, stop=True), rd=[cstT, cT], wr=[pm])
            p.op("act", lambda e_: e_.activation(out=cmb.rearrange("p t h -> p (t h)"), in_=pm.ap[:, 0:NH],
                                                 func=AF.Copy), rd=[pm], wr=[cmbT])
            negc = sb("negc", [128, NT, 8], F32)
            negcT = Tl()
            p.op("dve", lambda e_: e_.tensor_scalar(out=negc, in0=c, scalar1=-1.0, scalar2=None, op0=ALU.mult),
                 rd=[cT], wr=[negcT])
            hib = sb("hib", [128, NT], BF16)
            lo32 = sb("lo32", [128, NT], F32)
            hlT = Tl()
            fq = [sb("fq", [97, S], BF16) for _ in range(2)]
            fk = [sb("fk", [97, S], BF16) for _ in range(2)]
            fqT = [Tl(), Tl()]
            fkT = [Tl(), Tl()]
            for b2 in range(2):
                p.op("dve", lambda e_: e_.memset(fq[b2][64:97, :], 0.0), wr=[fqT[b2]])
                p.op("dve", lambda e_: e_.memset(fk[b2][64:97, :], 0.0), wr=[fkT[b2]])
                p.op("dve", lambda e_: e_.memset(fk[b2][64:65, :], 1.0), wr=[fkT[b2]])
                p.op("dve", lambda e_: e_.memset(fk[b2][96:97, :], 1.0), wr=[fkT[b2]])
            vh = [sb("vh", [128, NT, 64], BF16) for _ in range(2)]
            vhT = [Tl(), Tl()]
            oT = [sb("foT", [64, S], BF16) for _ in range(2)]
            oTt = [Tl(), Tl()]
            PT = [sb("fPT", [128, 512], BF16) for _ in range(3)]
            PTt = [[Tl() for _ in range(4)] for _ in range(3)]
            rs = [sb("frs", [64, 512], F32) for _ in range(2)]
            rsT = [Tl(), Tl()]
            n = 0
            m = 0
            def f_proj(hd):
                nonlocal n
                b_ = hd % 2
                p.op("dve", lambda e_: e_.tensor_copy(
                    out=fq[b_][64:65, :].rearrange("p (t c) -> p t c", c=128),
                    in_=cmb[64:65, :, hd].unsqueeze(2).broadcast_to([1, NT, 128])), rd=[cmbT], wr=[fqT[b_]])
                p.op("dve", lambda e_: e_.tensor_copy(out=hib[96:97, :], in_=cmb[96:97, :, hd]), rd=[cmbT], wr=[hlT])
                p.op("dve", lambda e_: e_.tensor_tensor(out=lo32[96:97, :], in0=cmb[96:97, :, hd], in1=hib[96:97, :],
                                                        op=ALU.subtract), rd=[cmbT, hlT], wr=[hlT])
                p.op("dve", lambda e_: e_.tensor_copy(
                    out=fq[b_][96:97, :].rearrange("p (t c) -> p t c", c=128),
                    in_=lo32[96:97, :].unsqueeze(2).broadcast_to([1, NT, 128])), rd=[hlT], wr=[fqT[b_]])
                for g in range(NG):
                    for dst, dT, col0, sc in ((fq[b_], fqT[b_], hd * 64, 0.125), (fk[b_], fkT[b_], 512 + hd * 64, 1.0)):
                        pst = banks[n % 2]
                        n += 1
                        for kc in range(8):
                            p.op("pe", lambda e_: e_.matmul(pst.ap[0:64, :], lhsT=wqk[:, kc, col0:col0 + 64],
                                                            rhs=hn_sb[:, kc, g * 512:(g + 1) * 512],
                                                            start=(kc == 0), stop=(kc == 7)),
                                 rd=[WA, hnT[g]], wr=[pst])
                        p.op("dve", lambda e_: e_.tensor_scalar(out=dst[0:64, g * 512:(g + 1) * 512], in0=pst.ap[0:64, :],
                                                                scalar1=sc, scalar2=None, op0=ALU.mult), rd=[pst], wr=[dT])
                nv = min(8, NT)
                for t0 in range(0, NT, nv):
                    pst = banks[n % 2]
                    n += 1
                    for j in range(nv):
                        tt = t0 + j
                        for kc in range(8):
                            p.op("pe", lambda e_: e_.matmul(pst.ap[:, j * 64:(j + 1) * 64],
                                                            lhsT=hn_sb[:, kc, tt * 128:(tt + 1) * 128],
                                                            rhs=wv[:, kc, hd * 64:(hd + 1) * 64],
                                                            start=(kc == 0), stop=(kc == 7)),
                                 rd=[WS0, WS1, hnT[tt // 4]], wr=[pst])
                    p.op("act", lambda e_: e_.activation(out=vh[b_][:, t0:t0 + nv, :],
                                                         in_=pst.ap[:, 0:nv * 64].rearrange("p (t d) -> p t d", d=64),
                                                         func=AF.Copy), rd=[pst], wr=[vhT[b_]])

            def f_attn(hd):
                nonlocal m
                b_ = hd % 2
                items = [(G, j) for G in range(NG) for j in range(4 * G + 4)]

                def stX(idx):
                    G, j = items[idx]
                    i0 = max(j, 4 * G)
                    c0 = (i0 - 4 * G) * 128
                    pst = banks[2 + ((m + idx) % 2)]
                    p.op("pe", lambda e_: e_.matmul(pst.ap[:, c0:512], lhsT=fk[b_][:, j * 128:(j + 1) * 128],
                                                    rhs=fq[b_][:, G * 512 + c0:(G + 1) * 512], start=True, stop=True),
                         rd=[fkT[b_], fqT[b_]], wr=[pst])

                def stY(idx):
                    G, j = items[idx]
                    jmax = 4 * G + 3
                    i0 = max(j, 4 * G)
                    c0 = (i0 - 4 * G) * 128
                    pst = banks[2 + ((m + idx) % 2)]
                    po, psm = (banks[4], banks[5]) if G % 2 == 0 else (banks[6], banks[7])
                    k = (m + idx) % 3
                    blks = list(range(i0 - 4 * G, 4))
                    p.op("act", lambda e_: e_.activation(out=PT[k][:, c0:512], in_=pst.ap[:, c0:512],
                                                         func=AF.Exp, bias=negc[:, j, hd:hd + 1]),
                         rd=[pst, negcT], wr=[PTt[k][bi] for bi in blks])
                    if j >= 4 * G:
                        bi = j - 4 * G
                        cc = bi * 128
                        p.op("dve", lambda e_: e_.tensor_tensor(out=PT[k][:, cc:cc + 128], in0=PT[k][:, cc:cc + 128],
                                                                in1=Ub, op=ALU.mult),
                             rd=[PTt[k][bi], cbT], wr=[PTt[k][bi]])
                    rdt = [PTt[k][bi] for bi in blks]
                    p.op("pe", lambda e_: e_.matmul(po.ap[0:64, c0:512], lhsT=vh[b_][:, j, :], rhs=PT[k][:, c0:512],
                                                    start=(j == 0), stop=(j == jmax)), rd=rdt + [vhT[b_]], wr=[po])
                    p.op("pe", lambda e_: e_.matmul(psm.ap[0:64, c0:512], lhsT=onesb, rhs=PT[k][:, c0:512],
                                                    start=(j == 0), stop=(j == jmax)), rd=rdt + [cbT], wr=[psm])
                    if j == jmax:
                        r_ = G % 2
                        p.op("dve", lambda e_: e_.reciprocal(out=rs[r_], in_=psm.ap[0:64, :]), rd=[psm], wr=[rsT[r_]])
                        p.op("dve", lambda e_: e_.tensor_tensor(out=oT[b_][:, G * 512:(G + 1) * 512], in0=po.ap[0:64, :],
                                                                in1=rs[r_], op=ALU.mult), rd=[po, rsT[r_]], wr=[oTt[b_]])

                stX(0)
                for idx in range(len(items)):
                    if idx + 1 < len(items):
                        stX(idx + 1)
                    stY(idx)
                m += len(items)

            def f_out(hd):
                nonlocal n
                b_ = hd % 2
                for tt in range(NT):
                    for hf in range(2):
                        pst = banks[n % 2]
                        n += 1
                        p.op("pe", lambda e_: e_.matmul(pst.ap, lhsT=oT[b_][:, tt * 128:(tt + 1) * 128],
                                                        rhs=woF[:, hd, hf * 512:(hf + 1) * 512], start=True, stop=True),
                             rd=[oTt[b_], WB], wr=[pst])
                        add_h(tt, hf, pst, pst.ap)


            f_proj(0)
            for hd in range(8):
                if hd + 1 < 8:
                    f_proj(hd + 1)
                f_attn(hd)
                if hd >= 1:
                    f_out(hd - 1)
            f_out(7)
    def hgrn(l, e, Win, Wout):
        assert ne <= 2
        with p.phase() as sb:
            wqf = wload([WA], WA.ap, Win[:, 1544:2568], 8, 1024)
            wig = wload([WB], WB.ap, Win[:, 2568:3592], 8, 1024)
            woH = wload([WS0, WS1], ws_ap, Wout[512:1024, :], 4, 1024)
            lbs = sb("lbs", [128, 12], F32)
            lbT = Tl()
            lb, oml, noml = lbs[:, 0:4], lbs[:, 4:8], lbs[:, 8:12]
            if e == 0:
                p.op("dve", lambda e_: e_.memset(lb, 0.0), wr=[lbT])
            else:
                p.op("dve", lambda e_: e_.tensor_tensor(out=lb, in0=pr("lbl1"), in1=pr("lbl0"), op=ALU.subtract),
                     rd=[prmT], wr=[lbT])
                p.op("act", lambda e_: e_.activation(out=lb, in_=lb, func=AF.Sigmoid), rd=[lbT], wr=[lbT])
            p.op("dve", lambda e_: e_.tensor_scalar(out=oml, in0=lb, scalar1=-1.0, scalar2=1.0, op0=ALU.mult,
                                                    op1=ALU.add), rd=[lbT], wr=[lbT])
            p.op("dve", lambda e_: e_.tensor_scalar(out=noml, in0=oml, scalar1=-1.0, scalar2=None, op0=ALU.mult),
                 rd=[lbT], wr=[lbT])
            Sst = sb("Sst", [128, 4, 128], F32)
            SstT = [Tl() for _ in range(4)]
            p.op("dve", lambda e_: e_.memset(Sst, 0.0), wr=SstT)
            vtok = sb("vtok", [128, 4, 512], BF16)
            vtT = [Tl() for _ in range(4)]
            gsg = sb("gsg", [128, 4, 512], F32)
            gsT = [Tl() for _ in range(4)]
            sg = sb("sg", [128, 512], F32)
            sgT = Tl()

            def f32t(name):
                return sb(name, [128, 512], F32), Tl()

            def b16t(name):
                return sb(name, [128, 512], BF16), Tl()

            qs, qsT = f32t("qs")
            sig, sigT = f32t("sig")
            lg, lgT = f32t("lg")
            kin, kinT = f32t("kin")
            bb, bbT = f32t("bb")
            tmp, tmpT = f32t("tmp")
            ex, exT = f32t("ex")
            qbf, qbfT = f32t("qbf")
            qtilS = [b16t("qtil") for _ in range(2)]
            ktilS = [b16t("ktil") for _ in range(2)]
            qbAS = [b16t("qbA") for _ in range(2)]
            qbBS = [b16t("qbB") for _ in range(2)]
            kdTS = [b16t("kdT") for _ in range(2)]
            dkS = [(sb("dk", [128, 8], F32), Tl()) for _ in range(2)]
            kdA = [sb("kdA", [128, 128], BF16) for _ in range(2)]
            kdB = [sb("kdB", [128, 128], BF16) for _ in range(2)]
            kdAT, kdBT = [Tl(), Tl()], [Tl(), Tl()]
            ST = [sb("ST", [128, 128], BF16) for _ in range(2)]
            STT = [Tl(), Tl()]
            st = [sb("hst", [128, 4], F32) for _ in range(2)]
            stT = [Tl(), Tl()]
            bo = [sb("bo", [128, 128], BF16) for _ in range(2)]
            boT_ = [Tl(), Tl()]
            Sall = sb("Sall", [128, 9, 128], BF16)
            SallT = [Tl() for _ in range(9)]
            boT = [sb("boT", [128, 512], BF16) for _ in range(4)]
            boTT = [Tl() for _ in range(4)]
            b3 = bb.rearrange("p (c t) -> p c t", t=64)
            tmp3 = tmp.rearrange("p (c t) -> p c t", t=64)
            n = 0
            for g in range(NG):
                gs = slice(g * 512, (g + 1) * 512)
                for j in range(4):
                    tt = g * 4 + j
                    pst = banks[n % 2]
                    n += 1
                    for kc in range(8):
                        p.op("pe", lambda e_: e_.matmul(pst.ap, lhsT=hn_sb[:, kc, tt * 128:(tt + 1) * 128],
                                                        rhs=wig[:, kc, 0:512], start=(kc == 0), stop=(kc == 7)),
                             rd=[WB, hnT[g]], wr=[pst])
                    p.op("act", lambda e_: e_.activation(out=vtok[:, j, :], in_=pst.ap, func=AF.Copy),
                         rd=[pst], wr=[vtT[j]])
                    pst = banks[n % 2]
                    n += 1
                    for kc in range(8):
                        p.op("pe", lambda e_: e_.matmul(pst.ap, lhsT=hn_sb[:, kc, tt * 128:(tt + 1) * 128],
                                                        rhs=wig[:, kc, 512:1024], start=(kc == 0), stop=(kc == 7)),
                             rd=[WB, hnT[g]], wr=[pst])
                    p.op("act", lambda e_: e_.activation(out=sg, in_=pst.ap, func=AF.Silu), rd=[pst], wr=[sgT])
                    p.op("dve", lambda e_: e_.tensor_tensor(out=gsg[:, j, :], in0=sg, in1=pr("hon%d" % e), op=ALU.mult),
                         rd=[sgT, prmT], wr=[gsT[j]])
                def h_pro(hh):
                    nonlocal n
                    k2 = hh % 2
                    qtil, qtilT = qtilS[k2]
                    ktil, ktilT = ktilS[k2]
                    qbA, qbAT = qbAS[k2]
                    qbB, qbBT = qbBS[k2]
                    kdT_, kdTT = kdTS[k2]
                    dk, dkT = dkS[k2]
                    pq, pf = banks[n % 2], banks[(n + 1) % 2]
                    for pst, c0 in ((pq, hh * 128), (pf, 512 + hh * 128)):
                        for kc in range(8):
                            p.op("pe", lambda e_: e_.matmul(pst.ap, lhsT=wqf[:, kc, c0:c0 + 128], rhs=hn_sb[:, kc, gs],
                                                            start=(kc == 0), stop=(kc == 7)), rd=[WA, hnT[g]], wr=[pst])
                    p.op("act", lambda e_: e_.activation(out=qs, in_=pq.ap, func=AF.Silu), rd=[pq], wr=[qsT])
                    p.op("act", lambda e_: e_.activation(out=sig, in_=pf.ap, func=AF.Tanh, scale=0.5), rd=[pf], wr=[sigT])
                    p.op("dve", lambda e_: e_.tensor_scalar(out=sig, in0=sig, scalar1=0.5, scalar2=0.5, op0=ALU.mult,
                                                            op1=ALU.add), rd=[sigT], wr=[sigT])
                    p.op("dve", lambda e_: e_.tensor_scalar(out=lg, in0=sig, scalar1=oml[:, hh:hh + 1],
                                                            scalar2=lb[:, hh:hh + 1], op0=ALU.mult, op1=ALU.add),
                         rd=[sigT, lbT], wr=[lgT])
                    p.op("act", lambda e_: e_.activation(out=lg, in_=lg, func=AF.Ln), rd=[lgT], wr=[lgT])
                    p.op("dve", lambda e_: e_.tensor_scalar(out=kin, in0=sig, scalar1=noml[:, hh:hh + 1],
                                                            scalar2=oml[:, hh:hh + 1], op0=ALU.mult, op1=ALU.add),
                         rd=[sigT, lbT], wr=[kinT])
                    p.op("dve", lambda e_: e_.tensor_tensor_scan(out=bb, data0=cs("scanm"), data1=lg, initial=0.0,
                                                                 op0=ALU.mult, op1=ALU.add), rd=[cstT, lgT], wr=[bbT])
                    p.op("dve", lambda e_: e_.tensor_tensor(out=tmp3, in0=b3, in1=b3[:, :, 31:32].broadcast_to([128, 8, 64]),
                                                            op=ALU.subtract), rd=[bbT], wr=[tmpT])
                    p.op("act", lambda e_: e_.activation(out=ex, in_=tmp, func=AF.Exp), rd=[tmpT], wr=[exT])
                    p.op("dve", lambda e_: e_.tensor_tensor(out=qtil, in0=qs, in1=ex, op=ALU.mult), rd=[qsT, exT], wr=[qtilT])
                    p.op("act", lambda e_: e_.activation(out=ex, in_=tmp, func=AF.Exp, scale=-1.0), rd=[tmpT], wr=[exT])
                    p.op("dve", lambda e_: e_.tensor_tensor(out=ktil, in0=kin, in1=ex, op=ALU.mult), rd=[kinT, exT], wr=[ktilT])
                    p.op("act", lambda e_: e_.activation(out=ex, in_=bb, func=AF.Exp), rd=[bbT], wr=[exT])
                    p.op("dve", lambda e_: e_.tensor_tensor(out=qbf, in0=qs, in1=ex, op=ALU.mult), rd=[qsT, exT], wr=[qbfT])
                    p.op("dve", lambda e_: e_.tensor_tensor(out=qbA.rearrange("p (j c) -> p j c", c=128), in0=qbf.rearrange("p (j c) -> p j c", c=128), in1=cs("colA").unsqueeze(1).broadcast_to([128, 4, 128]), op=ALU.mult),
                         rd=[qbfT, cstT], wr=[qbAT])
                    p.op("dve", lambda e_: e_.tensor_tensor(out=qbB.rearrange("p (j c) -> p j c", c=128), in0=qbf.rearrange("p (j c) -> p j c", c=128), in1=cs("colB").unsqueeze(1).broadcast_to([128, 4, 128]), op=ALU.mult),
                         rd=[qbfT, cstT], wr=[qbBT])
                    p.op("dve", lambda e_: e_.tensor_tensor(out=tmp3, in0=b3, in1=b3[:, :, 63:64].broadcast_to([128, 8, 64]),
                                                            op=ALU.subtract), rd=[bbT], wr=[tmpT])
                    p.op("act", lambda e_: e_.activation(out=ex, in_=tmp, func=AF.Exp, scale=-1.0), rd=[tmpT], wr=[exT])
                    p.op("dve", lambda e_: e_.tensor_tensor(out=kdT_, in0=kin, in1=ex, op=ALU.mult), rd=[kinT, exT], wr=[kdTT])
                    p.op("act", lambda e_: e_.activation(out=dk, in_=b3[:, :, 63], func=AF.Exp), rd=[bbT], wr=[dkT])

                def h_inner(hh):
                    k2 = hh % 2
                    qtil, qtilT = qtilS[k2]
                    ktil, ktilT = ktilS[k2]
                    qbA, qbAT = qbAS[k2]
                    qbB, qbBT = qbBS[k2]
                    kdT_, kdTT = kdTS[k2]
                    dk, dkT = dkS[k2]
                    p.op("act", lambda e_: e_.activation(out=Sall[:, 0, :], in_=Sst[:, hh, :], func=AF.Copy),
                         rd=[SstT[hh]], wr=[SallT[0]])

                    def stF(j):
                        kk = j % 2
                        cl = slice(j * 128, (j + 1) * 128)
                        vj = vtok[:, j, hh * 128:(hh + 1) * 128]
                        ptr, pu, psc = banks[4], banks[2 + kk], banks[6]
                        ptv = ptr.ap.bitcast(BF16)[:, 0:128]
                        p.op("pe", lambda e_: e_.transpose(out=ptv, in_=kdT_[:, cl], identity=identb),
                             rd=[kdTT, cbT], wr=[ptr])
                        p.op("dve", lambda e_: e_.tensor_scalar(out=kdA[kk], in0=ptv, scalar1=cs("rowA"), scalar2=None,
                                                                op0=ALU.mult), rd=[ptr, cstT], wr=[kdAT[kk]])
                        p.op("dve", lambda e_: e_.tensor_scalar(out=kdB[kk], in0=ptv, scalar1=cs("rowB"), scalar2=None,
                                                                op0=ALU.mult), rd=[ptr, cstT], wr=[kdBT[kk]])
                        p.op("pe", lambda e_: e_.matmul(pu.ap[:, 0:128], lhsT=kdA[kk], rhs=vj, start=True, stop=True),
                             rd=[kdAT[kk], vtT[j]], wr=[pu])
                        p.op("pe", lambda e_: e_.matmul(pu.ap[:, 128:256], lhsT=kdB[kk], rhs=vj, start=True, stop=True),
                             rd=[kdBT[kk], vtT[j]], wr=[pu])
                        p.op("pe", lambda e_: e_.matmul(psc.ap[:, 0:128], lhsT=ktil[:, cl], rhs=qtil[:, cl],
                                                        start=True, stop=True), rd=[ktilT, qtilT], wr=[psc])
                        p.op("dve", lambda e_: e_.tensor_tensor(out=ST[kk], in0=psc.ap[:, 0:128], in1=cs("maskbd"),
                                                                op=ALU.mult), rd=[psc, cstT], wr=[STT[kk]])
                        for half in range(2):
                            ci = 2 * j + half
                            p.op("dve", lambda e_: e_.scalar_tensor_tensor(out=Sst[:, hh, :], in0=Sst[:, hh, :],
                                                                           scalar=dk[:, ci:ci + 1],
                                                                           in1=pu.ap[:, half * 128:(half + 1) * 128],
                                                                           op0=ALU.mult, op1=ALU.add),
                                 rd=[SstT[hh], dkT, pu], wr=[SstT[hh]])
                            p.op("act", lambda e_: e_.activation(out=Sall[:, ci + 1, :], in_=Sst[:, hh, :], func=AF.Copy),
                                 rd=[SstT[hh]], wr=[SallT[ci + 1]])

                    def stBk(j):
                        kk = j % 2
                        cl = slice(j * 128, (j + 1) * 128)
                        vj = vtok[:, j, hh * 128:(hh + 1) * 128]
                        po, ptr2 = banks[7], banks[5]
                        p.op("pe", lambda e_: e_.matmul(po.ap[:, 0:128], lhsT=ST[kk], rhs=vj, start=True, stop=False),
                             rd=[STT[kk], vtT[j]], wr=[po])
                        p.op("pe", lambda e_: e_.matmul(po.ap[:, 0:128], lhsT=qbA[:, cl], rhs=Sall[:, 2 * j, :],
                                                        start=False, stop=False), rd=[qbAT, SallT[2 * j]], wr=[po])
                        p.op("pe", lambda e_: e_.matmul(po.ap[:, 0:128], lhsT=qbB[:, cl], rhs=Sall[:, 2 * j + 1, :],
                                                        start=False, stop=True), rd=[qbBT, SallT[2 * j + 1]], wr=[po])
                        p.op("act", lambda e_: e_.activation(out=bo[kk], in_=po.ap[:, 0:128], func=AF.Square,
                                                             accum_out=st[kk][:, 0:1]), rd=[po], wr=[boT_[kk], stT[kk]])
                        rstd_chain(st[kk][:, 0:1], st[kk][:, 1:2], st[kk][:, 2:3], 128, stT[kk])
                        p.op("dve", lambda e_: e_.scalar_tensor_tensor(out=bo[kk], in0=po.ap[:, 0:128], scalar=st[kk][:, 2:3],
                                                                       in1=gsg[:, j, hh * 128:(hh + 1) * 128],
                                                                       op0=ALU.mult, op1=ALU.mult),
                             rd=[po, stT[kk], gsT[j]], wr=[boT_[kk]])
                        ptv2 = ptr2.ap.bitcast(BF16)[:, 0:128]
                        p.op("pe", lambda e_: e_.transpose(out=ptv2, in_=bo[kk], identity=identb), rd=[boT_[kk], cbT], wr=[ptr2])
                        p.op("act", lambda e_: e_.activation(out=boT[hh][:, cl], in_=ptv2, func=AF.Copy),
                             rd=[ptr2], wr=[boTT[hh]])

                    stF(0)
                    for j in range(4):
                        if j + 1 < 4:
                            stF(j + 1)
                        stBk(j)

                h_pro(0)
                for hh in range(4):
                    if hh + 1 < 4:
                        h_pro(hh + 1)
                    h_inner(hh)
                for j in range(4):
                    tt = g * 4 + j
                    for hf in range(2):
                        pst = banks[n % 2]
                        n += 1
                        for hh in range(4):
                            p.op("pe", lambda e_: e_.matmul(pst.ap, lhsT=boT[hh][:, j * 128:(j + 1) * 128],
                                                            rhs=woH[:, hh, hf * 512:(hf + 1) * 512],
                                                            start=(hh == 0), stop=(hh == 3)),
                                 rd=[boTT[hh], WS0, WS1], wr=[pst])
                        add_h(tt, hf, pst, pst.ap)

    def ssd_mixer(l, o):
        Win = W["ssm_in"][o]
        Wout = W["ssm_out"][o]
        wdt = wload([WT], WT.ap, Win[:, 6144:6176], 8, 32)

        def load_group(g):
            WG = WA if g % 2 == 0 else WB
            WO = WS0 if g % 2 == 0 else WS1
            return (wload([WG], WG.ap, Win[:, g * 256:(g + 1) * 256], 8, 256, col0=0),
                    wload([WG], WG.ap, Win[:, 2048 + g * 256:2048 + (g + 1) * 256], 8, 256, col0=2048),
                    wload([WG], WG.ap, Win[:, 4096 + g * 128:4096 + (g + 1) * 128], 8, 128, col0=4096),
                    wload([WG], WG.ap, Win[:, 5120 + g * 128:5120 + (g + 1) * 128], 8, 128, col0=5120),
                    wload([WO], WO.ap, Wout[g * 256:(g + 1) * 256, :], 2, 1024))

        gw = {0: load_group(0)}
        with p.phase() as sb:
            rmsnorm_T(sb, NT, h_src, pr("nmix%d" % l), hn_dst, (6, 7))
        with p.phase() as sb:
            NH = NT * 32
            N4 = NT * 4
            dt = sb("dt", [128, NT, 32], F32)
            dt2 = dt.rearrange("p t h -> p (t h)")
            dtT = Tl()
            da = sb("da", [128, NT, 32], F32)
            daT = Tl()
            abc = sb("abc", [128, 32], F32)
            abcT = Tl()
            pd = banks[0]
            for tt in range(NT):
                for kc in range(8):
                    p.op("pe", lambda e_: e_.matmul(pd.ap[:, tt * 32:(tt + 1) * 32],
                                                    lhsT=hn_sb[:, kc, tt * 128:(tt + 1) * 128], rhs=wdt[:, kc, :],
                                                    start=(kc == 0), stop=(kc == 7)), rd=[hnT[tt // 4], WT], wr=[pd])
            p.op("dve", lambda e_: e_.tensor_tensor(out=dt, in0=pd.ap[:, 0:NH].rearrange("p (t h) -> p t h", h=32),
                                                    in1=pr("dtb%d" % o).unsqueeze(1).broadcast_to([128, NT, 32]),
                                                    op=ALU.add), rd=[pd, prmT], wr=[dtT])
            p.op("act", lambda e_: e_.activation(out=dt2, in_=dt2, func=AF.Exp), rd=[dtT], wr=[dtT])
            p.op("dve", lambda e_: e_.tensor_scalar(out=dt2, in0=dt2, scalar1=1.0, scalar2=None, op0=ALU.add),
                 rd=[dtT], wr=[dtT])
            p.op("act", lambda e_: e_.activation(out=dt2, in_=dt2, func=AF.Ln), rd=[dtT], wr=[dtT])
            p.op("act", lambda e_: e_.activation(out=abc, in_=pr("alog%d" % o), func=AF.Exp), rd=[prmT], wr=[abcT])
            p.op("dve", lambda e_: e_.tensor_scalar(out=abc, in0=abc, scalar1=-1.0, scalar2=None, op0=ALU.mult),
                 rd=[abcT], wr=[abcT])
            p.op("dve", lambda e_: e_.tensor_tensor(out=da, in0=dt, in1=abc.unsqueeze(1).broadcast_to([128, NT, 32]),
                                                    op=ALU.mult), rd=[dtT, abcT], wr=[daT])

            def sm(name):
                a = sb(name, [128, NT, 4], F32)
                return a, a.rearrange("p t h -> p (t h)"), Tl()

            ecum, ecum2, ecumT = sm("ecum")
            ecl, ecl2, eclT = sm("ecl")
            w_, w2, wT_ = sm("w")
            xg = sb("xg", [128, NT, 256], BF16)
            xgT = [Tl() for _ in range(NG)]
            BT = sb("BT", [128, S], BF16)
            BTt = [Tl() for _ in range(NG)]
            Bg = sb("Bg", [128, NT, 128], BF16)
            BgT = [Tl() for _ in range(NG)]
            CT = sb("CT", [128, S], BF16)
            CTt = [Tl() for _ in range(NG)]
            STall = sb("STall", [128, NT, 256], BF16)
            STaT = [Tl() for _ in range(NT)]
            STs = sb("STs", [128, 256], F32)
            STsT = Tl()
            cw = pr("cw%d" % o)
            cbp = pr("cb%d" % o)

            def v4(ap):
                return ap.rearrange("p (h d) -> p h d", d=64)

            def bc(ap):
                return ap.unsqueeze(2).broadcast_to([128, 4, 64])

            n = 0
            for g in range(8):
                WG = WA if g % 2 == 0 else WB
                WO = WS0 if g % 2 == 0 else WS1
                wz, wx, wB, wC, wo = gw[g]
                hs = slice(g * 4, g * 4 + 4)
                pcu, pcl = banks[6], banks[7]
                p.op("pe", lambda e_: e_.matmul(pcu.ap[:, 0:N4].rearrange("p (t h) -> p t h", h=4), lhsT=cs("U"),
                                                rhs=da[:, :, hs], start=True, stop=True), rd=[cstT, daT], wr=[pcu])
                p.op("pe", lambda e_: e_.matmul(pcl.ap[:, 0:N4].rearrange("p (t h) -> p t h", h=4), lhsT=cs("ones"),
                                                rhs=da[:, :, hs], start=True, stop=True), rd=[cstT, daT], wr=[pcl])
                p.op("act", lambda e_: e_.activation(out=ecum2, in_=pcu.ap[:, 0:N4], func=AF.Exp), rd=[pcu], wr=[ecumT])
                p.op("act", lambda e_: e_.activation(out=ecl2, in_=pcl.ap[:, 0:N4], func=AF.Exp), rd=[pcl], wr=[eclT])
                p.op("act", lambda e_: e_.activation(out=w2, in_=pcu.ap[:, 0:N4], func=AF.Copy), rd=[pcu], wr=[wT_])
                p.op("dve", lambda e_: e_.tensor_tensor(out=w2, in0=pcl.ap[:, 0:N4], in1=w2, op=ALU.subtract),
                     rd=[pcl, wT_], wr=[wT_])
                p.op("act", lambda e_: e_.activation(out=w2, in_=w2, func=AF.Exp), rd=[wT_], wr=[wT_])
                p.op("dve", lambda e_: e_.tensor_tensor(out=w_, in0=w_, in1=dt[:, :, hs], op=ALU.mult),
                     rd=[wT_, dtT], wr=[wT_])
                chunks = ((wx[:, :, 0:128], 2 * g, "x0"), (wx[:, :, 128:256], 2 * g + 1, "x1"),
                          (wB, 16 + g, "B"), (wC, 24 + g, "C"))
                with p.phase() as sb2:
                    xps = [sb2("xp", [128, 515], F32) for _ in range(2)]
                    xpTs = [Tl(), Tl()]
                    accs = [sb2("acc", [128, 512], F32) for _ in range(2)]
                    accTs = [Tl(), Tl()]
                    xTcs = [sb2("xTc", [128, 512], BF16) for _ in range(2)]
                    xTcTs = [Tl(), Tl()]
                    items = [(wsrc, ch, kind, tg) for (wsrc, ch, kind) in chunks for tg in range(NG)]
                    pend = {}

                    def stP1(idx):
                        nonlocal n
                        wsrc, ch, kind, tg = items[idx]
                        kk = idx % 2
                        xp, xpT = xps[kk], xpTs[kk]
                        pst = banks[n % 2]
                        n += 1
                        for kc in range(8):
                            p.op("pe", lambda e_: e_.matmul(pst.ap, lhsT=wsrc[:, kc, :], rhs=hn_sb[:, kc, tg * 512:(tg + 1) * 512],
                                                            start=(kc == 0), stop=(kc == 7)), rd=[WG, hnT[tg]], wr=[pst])
                        if tg == 0:
                            p.op("dve", lambda e_: e_.memset(xp[:, 0:3], 0.0), wr=[xpT])
                        else:
                            p.op("dve", lambda e_: e_.tensor_copy(out=xp[:, 0:3], in_=xps[1 - kk][:, 512:515]),
                                 rd=[xpTs[1 - kk]], wr=[xpT])
                        p.op("act", lambda e_: e_.activation(out=xp[:, 3:515], in_=pst.ap, func=AF.Copy),
                             rd=[pst], wr=[xpT])

                    def stP2(idx):
                        wsrc, ch, kind, tg = items[idx]
                        kk = idx % 2
                        xp, xpT, acc, accT = xps[kk], xpTs[kk], accs[kk], accTs[kk]
                        xTc, xTcT = xTcs[kk], xTcTs[kk]
                        p.op("dve", lambda e_: e_.tensor_scalar(out=acc, in0=xp[:, 0:512], scalar1=cw[:, ch * 4:ch * 4 + 1],
                                                                scalar2=None, op0=ALU.mult), rd=[xpT, prmT], wr=[accT])
                        for j in range(1, 4):
                            p.op("dve", lambda e_: e_.scalar_tensor_tensor(out=acc, in0=xp[:, j:j + 512],
                                                                           scalar=cw[:, ch * 4 + j:ch * 4 + j + 1], in1=acc,
                                                                           op0=ALU.mult, op1=ALU.add),
                                 rd=[xpT, prmT, accT], wr=[accT])
                        gsl = slice(tg * 512, (tg + 1) * 512)
                        if kind in ("x0", "x1"):
                            dst, dT = xTc, xTcT
                        elif kind == "B":
                            dst, dT = BT[:, gsl], BTt[tg]
                        else:
                            dst, dT = CT[:, gsl], CTt[tg]
                        p.op("act", lambda e_: e_.activation(out=dst, in_=acc, func=AF.Silu, bias=cbp[:, ch:ch + 1]),
                             rd=[accT, prmT], wr=[dT])
                        pend[idx] = (dst, dT)

                    def stT(idx):
                        wsrc, ch, kind, tg = items[idx]
                        if kind == "C":
                            return
                        dst, dT = pend[idx]
                        ptr = banks[2 + (idx % 2)]
                        ptv = ptr.ap.bitcast(BF16)[:, 0:512].rearrange("p (t c) -> p t c", c=128)
                        for j in range(4):
                            p.op("pe", lambda e_: e_.transpose(out=ptv[:, j, :], in_=dst[:, j * 128:(j + 1) * 128],
                                                               identity=identb), rd=[dT, cbT], wr=[ptr])
                        if kind == "B":
                            o_ap, oT_ = Bg[:, tg * 4:(tg + 1) * 4, :], BgT[tg]
                        else:
                            c0 = 0 if kind == "x0" else 128
                            o_ap, oT_ = xg[:, tg * 4:(tg + 1) * 4, c0:c0 + 128], xgT[tg]
                        p.op("act", lambda e_: e_.activation(out=o_ap, in_=ptv, func=AF.Copy), rd=[ptr], wr=[oT_])

                    stP1(0)
                    for idx in range(len(items) + 1):
                        if idx + 1 < len(items):
                            stP1(idx + 1)
                        if idx < len(items):
                            stP2(idx)
                        if idx >= 1:
                            stT(idx - 1)
                if g + 1 < 8:
                    gw[g + 1] = load_group(g + 1)
                with p.phase() as sb2:
                    def two(name, shape, dt_):
                        return [sb2(name, shape, dt_) for _ in range(2)], [Tl(), Tl()]

                    Lh4 = sb2("Lh4", [128, 4, 128], F32)
                    Lh4T = Tl()
                    eE4s, eE4Ts = two("eE4", [128, 4, 128], F32)
                    ez1 = sb2("ez", [128, 256], F32)
                    ez, ezT = [ez1, ez1], [Tl()] * 2
                    sz = [sb2("sz", [128, 256], F32) for _ in range(3)]
                    szT = [Tl(), Tl(), Tl()]
                    CBm, CBmT = two("CBm", [128, 128], F32)
                    MT4, MT4T = two("MT4", [128, 4, 128], BF16)
                    xdt1 = sb2("xdt", [128, 256], BF16)
                    xdt, xdtT = [xdt1, xdt1], [Tl()] * 2
                    t1, t1T = two("t1", [128, 256], F32)
                    t2, t2T = two("t2", [128, 256], BF16)
                    xD, xDT = two("xD", [128, 256], BF16)
                    yn, ynT_ = two("yn", [128, 256], BF16)
                    ynT1 = sb2("ynT", [128, 2, 128], BF16)
                    ynT, ynTT = [ynT1, ynT1], [Tl()] * 2
                    xw1 = sb2("xw", [128, 256], BF16)
                    xw, xwT = [xw1, xw1], [Tl()] * 2
                    st, stT = two("sst", [128, 4], F32)
                    p.op("dve", lambda e_: e_.memset(STs, 0.0), wr=[STsT])
                    p.op("dve", lambda e_: e_.memset(STall[:, 0, :], 0.0), wr=[STaT[0]])

                    def pre_a(tt):
                        k = tt % 2
                        p.op("pool", lambda e_: e_.tensor_tensor(out=v4(xw[k]), in0=v4(xg[:, tt, :]), in1=bc(w_[:, tt, :]),
                                                                 op=ALU.mult), rd=[xgT[tt // 4], wT_], wr=[xwT[k]])
                        p.op("pe", lambda e_: e_.matmul(banks[2 + k].ap[:, 0:256], lhsT=Bg[:, tt, :], rhs=xw[k],
                                                        start=True, stop=True), rd=[BgT[tt // 4], xwT[k]], wr=[banks[2 + k]])

                    def pre_b(tt):
                        k = tt % 2
                        p.op("dve", lambda e_: e_.tensor_tensor(out=v4(STs), in0=v4(STs), in1=bc(ecl[:, tt, :]), op=ALU.mult),
                             rd=[STsT, eclT], wr=[STsT])
                        p.op("dve", lambda e_: e_.tensor_tensor(out=STs, in0=STs, in1=banks[2 + k].ap[:, 0:256], op=ALU.add),
                             rd=[STsT, banks[2 + k]], wr=[STsT])
                        p.op("act", lambda e_: e_.activation(out=STall[:, tt + 1, :], in_=STs, func=AF.Copy),
                             rd=[STsT], wr=[STaT[tt + 1]])

                    if NT > 1:
                        pre_a(0)
                    for tt in range(NT - 1):
                        if tt + 1 < NT - 1:
                            pre_a(tt + 1)
                        pre_b(tt)

                    def stA(tt):
                        k = tt % 2
                        cl = slice(tt * 128, (tt + 1) * 128)
                        tg = tt // 4
                        pz = banks[6]
                        for kc in range(8):
                            p.op("pe", lambda e_: e_.matmul(pz.ap[:, 0:256], lhsT=hn_sb[:, kc, cl], rhs=wz[:, kc, :],
                                                            start=(kc == 0), stop=(kc == 7)), rd=[WG, hnT[tg]], wr=[pz])
                        p.op("act", lambda e_: e_.activation(out=ez[k], in_=pz.ap[:, 0:256], func=AF.Exp, scale=-1.0),
                             rd=[pz], wr=[ezT[k]])
                        pcb = banks[7]
                        p.op("pe", lambda e_: e_.matmul(pcb.ap[:, 0:128], lhsT=BT[:, cl], rhs=CT[:, cl], start=True, stop=True),
                             rd=[BTt[tg], CTt[tg]], wr=[pcb])
                        p.op("dve", lambda e_: e_.tensor_tensor(out=CBm[k], in0=pcb.ap[:, 0:128], in1=cs("U"), op=ALU.mult),
                             rd=[pcb, cstT], wr=[CBmT[k]])
                        p.op("pool", lambda e_: e_.tensor_tensor(out=Lh4, in0=cs("SL").unsqueeze(1).broadcast_to([128, 4, 128]),
                                                                 in1=da[:, tt, hs].unsqueeze(2).broadcast_to([128, 4, 128]),
                                                                 op=ALU.mult), rd=[cstT, daT], wr=[Lh4T])
                        pE = banks[2 + k]
                        for hl in range(4):
                            p.op("pe", lambda e_: e_.matmul(pE.ap[:, hl * 128:(hl + 1) * 128], lhsT=Lh4[:, hl, :], rhs=cs("U"),
                                                            start=True, stop=True), rd=[Lh4T, cstT], wr=[pE])
                        p.op("act", lambda e_: e_.activation(out=eE4s[k].rearrange("p h t -> p (h t)"), in_=pE.ap, func=AF.Exp),
                             rd=[pE], wr=[eE4Ts[k]])
                        p.op("dve", lambda e_: e_.tensor_scalar(out=ez[k], in0=ez[k], scalar1=1.0, scalar2=None, op0=ALU.add),
                             rd=[ezT[k]], wr=[ezT[k]])
                        p.op("dve", lambda e_: e_.reciprocal(out=ez[k], in_=ez[k]), rd=[ezT[k]], wr=[ezT[k]])
                        p.op("dve", lambda e_: e_.tensor_tensor(out=sz[tt % 3], in0=ez[k], in1=pz.ap[:, 0:256], op=ALU.mult),
                             rd=[ezT[k], pz], wr=[szT[tt % 3]])

                    def stB(tt):
                        k = tt % 2
                        cl = slice(tt * 128, (tt + 1) * 128)
                        tg = tt // 4
                        p.op("dve", lambda e_: e_.tensor_tensor(out=MT4[k], in0=eE4s[k],
                                                                in1=CBm[k].unsqueeze(1).broadcast_to([128, 4, 128]), op=ALU.mult),
                             rd=[eE4Ts[k], CBmT[k]], wr=[MT4T[k]])
                        p.op("pool", lambda e_: e_.tensor_tensor(out=v4(xdt[k]), in0=v4(xg[:, tt, :]), in1=bc(dt[:, tt, hs]),
                                                                 op=ALU.mult), rd=[xgT[tg], dtT], wr=[xdtT[k]])
                        p.op("pool", lambda e_: e_.tensor_tensor(out=v4(xD[k]), in0=v4(xg[:, tt, :]),
                                                                 in1=bc(pr("dsk%d" % o)[:, hs]), op=ALU.mult),
                             rd=[xgT[tg], prmT], wr=[xDT[k]])
                        py = banks[4 + k]
                        for hl in range(4):
                            p.op("pe", lambda e_: e_.matmul(py.ap[:, hl * 64:(hl + 1) * 64], lhsT=MT4[k][:, hl, :],
                                                            rhs=xdt[k][:, hl * 64:(hl + 1) * 64], start=True, stop=False),
                                 rd=[MT4T[k], xdtT[k]], wr=[py])
                            p.op("pe", lambda e_: e_.matmul(py.ap[:, hl * 64:(hl + 1) * 64], lhsT=identb,
                                                            rhs=xD[k][:, hl * 64:(hl + 1) * 64], start=False, stop=True),
                                 rd=[cbT, xDT[k]], wr=[py])
                        p.op("pe", lambda e_: e_.matmul(py.ap[:, 256:512], lhsT=CT[:, cl], rhs=STall[:, tt, :],
                                                        start=True, stop=True), rd=[CTt[tg], STaT[tt]], wr=[py])

                    def stC(tt):
                        k = tt % 2
                        py = banks[4 + k]
                        p.op("dve", lambda e_: e_.tensor_tensor(out=v4(t1[k]), in0=v4(py.ap[:, 256:512]), in1=bc(ecum[:, tt, :]),
                                                                op=ALU.mult), rd=[py, ecumT], wr=[t1T[k]])
                        p.op("dve", lambda e_: e_.tensor_tensor(out=t1[k], in0=t1[k], in1=py.ap[:, 0:256], op=ALU.add),
                             rd=[t1T[k], py], wr=[t1T[k]])
                        p.op("dve", lambda e_: e_.tensor_tensor(out=t1[k], in0=t1[k], in1=sz[tt % 3], op=ALU.mult),
                             rd=[t1T[k], szT[tt % 3]], wr=[t1T[k]])
                        p.op("act", lambda e_: e_.activation(out=t2[k], in_=t1[k], func=AF.Square, accum_out=st[k][:, 0:1]),
                             rd=[t1T[k]], wr=[t2T[k], stT[k]])
                        rstd_chain(st[k][:, 0:1], st[k][:, 1:2], st[k][:, 2:3], 256, stT[k])
                        p.op("act", lambda e_: e_.activation(out=yn[k], in_=t1[k], func=AF.Copy, scale=st[k][:, 2:3]),
                             rd=[t1T[k], stT[k]], wr=[ynT_[k]])

                    def stD(tt):
                        nonlocal n
                        k = tt % 2
                        ptr = banks[7]
                        ptv = ptr.ap.bitcast(BF16)[:, 512:768].rearrange("p (j c) -> p j c", c=128)
                        for j in range(2):
                            p.op("pe", lambda e_: e_.transpose(out=ptv[:, j, :], in_=yn[k][:, j * 128:(j + 1) * 128],
                                                               identity=identb), rd=[ynT_[k], cbT], wr=[ptr])
                        p.op("dve", lambda e_: e_.tensor_tensor(out=ynT[k], in0=ptv,
                                                                in1=pr("snorm%d" % o)[:, 2 * g:2 * g + 2].unsqueeze(2).broadcast_to([128, 2, 128]),
                                                                op=ALU.mult), rd=[ptr, prmT], wr=[ynTT[k]])
                        for hf in range(2):
                            pst = banks[n % 2]
                            n += 1
                            for j in range(2):
                                p.op("pe", lambda e_: e_.matmul(pst.ap, lhsT=ynT[k][:, j, :], rhs=wo[:, j, hf * 512:(hf + 1) * 512],
                                                                start=(j == 0), stop=(j == 1)), rd=[ynTT[k], WO], wr=[pst])
                            add_h(tt, hf, pst, pst.ap)

                    for j in range(-1, NT + 2):
                        if 0 <= j + 1 < NT:
                            stA(j + 1)
                        if 0 <= j < NT:
                            stB(j)
                        if 0 <= j - 1 < NT:
                            stC(j - 1)
                        if 0 <= j - 2 < NT:
                            stD(j - 2)


    out_events = []
    for s_ in range(NSEQ):
        for tt in range(NT):
            p.dma("sp", h_sb[:, tt, :], x_d[s_, tt * 128:(tt + 1) * 128, :], wr=hT[tt])
        with p.phase() as sb:
            memraw = sb("memraw", [128, 2, D], F32)
            memrawT = [Tl(), Tl()]
            for kb in range(2):
                p.dma("sp", memraw[:, kb, :], mem_d[s_, kb * 128:(kb + 1) * 128, :], wr=[memrawT[kb]])
            rmsnorm_T(sb, 2, lambda tt: ([memrawT[tt]], memraw[:, tt, :]), pr("mem_norm"),
                      lambda tt: (memT, mem_sb[:, :, tt * 128:(tt + 1) * 128]), (6, 7))
        ei = oi = 0
        for l, kd in enumerate(kinds):
            if "mix" in flags:
                if kd == "e":
                    even_mixer(l, ei)
                else:
                    ssd_mixer(l, oi)
            if kd == "e":
                ei += 1
            else:
                oi += 1
            if "xa" in flags:
                xattn(l)
            if "mlp" in flags:
                mlp(l)
        with p.phase() as sb:
            gfin = sb("gfin", [128, D], F32)
            gfT = Tl()
            outb = [sb("outb", [128, D], F32) for i in range(2)]
            outT = [Tl(), Tl()]
            dg = sb("dg", [128, 128], F32)
            dgT = Tl()
            for kc in range(8):
                p.op("dve", lambda e: e.tensor_scalar(out=dg, in0=cs("ident"),
                                                      scalar1=pr("norm_final")[:, kc:kc + 1], scalar2=None,
                                                      op0=ALU.mult), rd=[cstT, prmT], wr=[dgT])
                pst = banks[kc % 2]
                p.op("pe", lambda e: e.matmul(pst.ap[:, 0:128], lhsT=cs("ones"), rhs=dg, start=True, stop=True),
                     rd=[cstT, dgT], wr=[pst])
                p.op("act", lambda e: e.activation(out=gfin[:, kc * 128:(kc + 1) * 128], in_=pst.ap[:, 0:128],
                                                   func=AF.Copy), rd=[pst], wr=[gfT])
            junk = sb("junkf", [128, D], BF16)
            junkT = Tl()
            st = sb("stf", [128, 8], F32)
            stT = [Tl(), Tl()]
            for tt in range(NT):
                k = tt % 2
                ap = h_sb[:, tt, :]
                ss, sd, rs = st[:, 4 * k:4 * k + 1], st[:, 4 * k + 1:4 * k + 2], st[:, 4 * k + 2:4 * k + 3]
                p.op("act", lambda e: e.activation(out=junk, in_=ap, func=AF.Square, accum_out=ss),
                     rd=hT[tt], wr=[junkT, stT[k]])
                rstd_chain(ss, sd, rs, D, stT[k])
                p.op("dve", lambda e: e.scalar_tensor_tensor(out=outb[k], in0=ap, scalar=rs, in1=gfin,
                                                             op0=ALU.mult, op1=ALU.mult),
                     rd=list(hT[tt]) + [stT[k], gfT], wr=[outT[k]])
                out_events.append(p.dma("sp", y_d[s_, tt * 128:(tt + 1) * 128, :], outb[k], rd=[outT[k]]))
    for key, val in out_events[-(NDS // 2):]:
        p.need("sp", key, val)
    C.nins = p.nins
    return nc, C


KINDS = ("e", "o", "e", "o")
NCORES = 8


def run(inp, S, NSEQ, kinds, flags=("mix", "xa", "mlp")):
    nc, C = build(S, NSEQ, kinds, flags)
    cst = make_consts()
    prm = make_params(kinds, inp)
    names = ["xa_q", "xa_kv", "xa_o", "mlp_up", "mlp_down"]
    if "e" in kinds:
        names += ["ev_in_proj", "ev_out_proj"]
    if "o" in kinds:
        names += ["ssm_in_proj", "ssm_out_proj"]
    shared = {k: np.ascontiguousarray(np.asarray(inp[k], np.float32)) for k in names}
    x = np.asarray(inp["x"], np.float32)
    mem = np.asarray(inp["mem"], np.float32)
    in_maps = []
    for c in range(NCORES):
        d = dict(shared)
        d["x"] = np.ascontiguousarray(x[c * NSEQ:(c + 1) * NSEQ])
        d["mem"] = np.ascontiguousarray(mem[c * NSEQ:(c + 1) * NSEQ])
        d["cst"] = cst
        d["prm"] = prm
        in_maps.append(d)
    res = run_bass_kernel_spmd(nc, in_maps, core_ids=list(range(NCORES)))
    return np.concatenate([np.asarray(r["y"]) for r in res.results], axis=0)


def kernel(**inputs):
    x = inputs["x"]
    B, S, _ = x.shape
    return run(inputs, S, B // NCORES, KINDS).astype(np.float32)
```

```python
import numpy as np
from contextlib import contextmanager, ExitStack
import concourse.bass as bass
import concourse.mybir as mybir
from concourse.bass_utils import run_bass_kernel_spmd

F32 = mybir.dt.float32
BF16 = mybir.dt.bfloat16
AF = mybir.ActivationFunctionType
ALU = mybir.AluOpType
D = 1024
EPS = 1e-6
MEM = 256
NDS = 24


class Tl:
    __slots__ = ("ap", "w", "r")

    def __init__(self, ap=None):
        self.ap = ap
        self.w = None
        self.r = {}


class Prog:
    def __init__(self, nc):
        self.nc = nc
        self.eng = dict(pe=nc.tensor, act=nc.scalar, dve=nc.vector, pool=nc.gpsimd, sp=nc.sync)
        self.sem = {k: nc.alloc_semaphore("s_" + k) for k in self.eng}
        self.cnt = {k: 0 for k in self.eng}
        self.seen = {k: {} for k in self.eng}
        for i in range(NDS):
            self.sem[("d", i)] = nc.alloc_semaphore("d%d" % i)
        self.dval = [0] * NDS
        self.dnext = {"sp": 0, "pool": 0}
        self.uid = 0
        self.nins = 0

    def need(self, e, key, val):
        if e == "pe" and key == "pe":
            return
        if self.seen[e].get(key, 0) >= val:
            return
        self.eng[e].wait_ge(self.sem[key], val)
        self.seen[e][key] = val
        self.nins += 1

    def deps(self, e, rd, wr):
        for t in rd:
            if t.w is not None:
                self.need(e, t.w[0], t.w[1])
        for t in wr:
            if t.w is not None:
                self.need(e, t.w[0], t.w[1])
            for k, v in t.r.items():
                self.need(e, k, v)

    def mark(self, key, val, rd, wr):
        for t in rd:
            t.r[key] = val
        for t in wr:
            t.w = (key, val)
            t.r = {}

    def op(self, e, fn, rd=(), wr=()):
        self.deps(e, rd, wr)
        ins = fn(self.eng[e])
        self.cnt[e] += 1
        ins.then_inc(self.sem[e], 1)
        self.mark(e, self.cnt[e], rd, wr)
        self.nins += 1
        return ins

    def dma(self, q, out, in_, rd=(), wr=()):
        half = NDS // 2
        i = self.dnext[q] + (0 if q == "sp" else half)
        self.dnext[q] = (self.dnext[q] + 1) % half
        key = ("d", i)
        if self.dval[i] > 0:
            self.need(q, key, self.dval[i])
        self.deps(q, rd, wr)
        ins = self.eng[q].dma_start(out=out, in_=in_)
        self.dval[i] += 16
        ins.then_inc(self.sem[key], 16)
        self.mark(key, self.dval[i], rd, wr)
        self.nins += 1
        return key, self.dval[i]

    def barrier(self):
        es = ("pe", "act", "dve")
        for e in es:
            for x in es:
                if x != e and self.cnt[x] > 0:
                    self.need(e, x, self.cnt[x])

    @contextmanager
    def phase(self):
        es = ExitStack()

        def sb(name, shape, dt):
            self.uid += 1
            t = es.enter_context(self.nc.sbuf_tensor("%s_%d" % (name, self.uid), list(shape), dt))
            return t.ap()

        yield sb
        self.barrier()
        es.close()


def const_layout():
    off = {}
    c = 0
    for name, n in (("ident", 128), ("U", 128), ("SL", 128), ("ones", 128), ("sel64", 128),
                    ("maskbd", 128), ("rowA", 1), ("rowB", 1), ("colA", 128), ("colB", 128),
                    ("scanm", 512)):
        off[name] = (c, n)
        c += n
    return off, c


def make_consts():
    off, n = const_layout()
    a = np.zeros((128, n), np.float32)
    r = np.arange(128)[:, None]
    t = np.arange(128)[None, :]

    def put(name, v):
        o, w = off[name]
        a[:, o:o + w] = v

    put("ident", (r == t))
    put("U", (r <= t))
    put("SL", (r > t))
    put("ones", 1.0)
    put("sel64", np.broadcast_to(r == 64, (128, 128)))
    put("maskbd", (r <= t) & ((r // 64) == (t // 64)))
    put("rowA", (r < 64))
    put("rowB", (r >= 64))
    t5 = np.arange(512)[None, :]
    put("colA", np.broadcast_to(t < 64, (128, 128)))
    put("colB", np.broadcast_to(t >= 64, (128, 128)))
    put("scanm", np.broadcast_to((t5 % 64) != 0, (128, 512)))
    return a


def param_layout(kinds):
    L = len(kinds)
    ne = sum(1 for k in kinds if k == "e")
    no = L - ne
    off = {}
    c = [0]

    def add(name, n):
        off[name] = (c[0], n)
        c[0] += n

    add("mem_norm", 8)
    add("norm_final", 8)
    for l in range(L):
        add("nmix%d" % l, 8)
        add("nxa%d" % l, 8)
        add("nmlp%d" % l, 8)
    for e in range(ne):
        add("fgb%d" % e, 8)
        add("hon%d" % e, 512)
    for e in range(max(ne, 1)):
        add("lbl%d" % e, 4)
    for o in range(no):
        add("cw%d" % o, 128)
        add("cb%d" % o, 32)
        add("dtb%d" % o, 32)
        add("alog%d" % o, 32)
        add("dsk%d" % o, 32)
        add("snorm%d" % o, 16)
    return off, c[0]


def fm(v, n):
    return np.ascontiguousarray(np.asarray(v, np.float32).reshape(n, 128).T)


def make_params(kinds, inp):
    off, n = param_layout(kinds)
    a = np.zeros((128, n), np.float32)

    def put(name, v):
        o, w = off[name]
        a[:, o:o + w] = v

    put("mem_norm", fm(inp["mem_norm"], 8))
    put("norm_final", fm(inp["norm_final"], 8))
    e = o = 0
    for l, k in enumerate(kinds):
        put("nmix%d" % l, fm(inp["norm_mix"][l], 8))
        put("nxa%d" % l, fm(inp["norm_xattn"][l], 8))
        put("nmlp%d" % l, fm(inp["norm_mlp"][l], 8))
        if k == "e":
            put("fgb%d" % e, np.broadcast_to(np.asarray(inp["fox_fgate_bias"][e])[None, :], (128, 8)))
            put("hon%d" % e, np.broadcast_to(np.asarray(inp["hgrn_out_norm"][e])[None, :], (128, 512)))
            put("lbl%d" % e, fm(inp["hgrn_lb_logits"][e], 4))
            e += 1
        else:
            cw = np.asarray(inp["ssm_conv_w"][o], np.float32)
            put("cw%d" % o, np.ascontiguousarray(cw.reshape(4, 32, 128).transpose(2, 1, 0)).reshape(128, 128))
            put("cb%d" % o, fm(inp["ssm_conv_b"][o], 32))
            put("dtb%d" % o, np.broadcast_to(np.asarray(inp["ssm_dt_bias"][o])[None, :], (128, 32)))
            put("alog%d" % o, np.broadcast_to(np.asarray(inp["ssm_A_log"][o])[None, :], (128, 32)))
            put("dsk%d" % o, np.broadcast_to(np.asarray(inp["ssm_D"][o])[None, :], (128, 32)))
            put("snorm%d" % o, fm(inp["ssm_norm"][o], 16))
            o += 1
    return a


class Ctx:
    pass


def build(S, NSEQ, kinds, flags=("mix", "xa", "mlp")):
    nc = bass.Bass("TRN2", target_bir_lowering=False)
    p = Prog(nc)
    C = Ctx()
    C.S, C.NT, C.NG = S, S // 128, S // 512
    NT, NG = C.NT, C.NG
    L = len(kinds)
    ne = sum(1 for k in kinds if k == "e")
    no = L - ne
    coff, cn = const_layout()
    poff, pn = param_layout(kinds)

    def din(name, shape):
        return nc.dram_tensor(name, list(shape), F32, kind="ExternalInput").ap()

    x_d = din("x", [NSEQ, S, D])
    mem_d = din("mem", [NSEQ, MEM, D])
    cst_d = din("cst", [128, cn])
    prm_d = din("prm", [128, pn])
    W = {}
    if ne:
        W["ev_in"] = din("ev_in_proj", [ne, D, 3592])
        W["ev_out"] = din("ev_out_proj", [ne, D, D])
    if no:
        W["ssm_in"] = din("ssm_in_proj", [no, D, 6176])
        W["ssm_out"] = din("ssm_out_proj", [no, 2048, D])
    W["xa_q"] = din("xa_q", [L, D, 512])
    W["xa_kv"] = din("xa_kv", [L, D, 1024])
    W["xa_o"] = din("xa_o", [L, 512, D])
    W["mlp_up"] = din("mlp_up", [L, D, 4096])
    W["mlp_down"] = din("mlp_down", [L, 4096, D])
    y_d = nc.dram_tensor("y", [NSEQ, S, D], F32, kind="ExternalOutput").ap()

    def sba(name, shape, dt):
        return nc.alloc_sbuf_tensor(name, list(shape), dt).ap()

    h_sb = sba("h", [128, NT, D], F32)
    hT = [[Tl(h_sb[:, tt, hf * 512:(hf + 1) * 512]) for hf in range(2)] for tt in range(NT)]
    hn_sb = sba("hnT", [128, 8, S], BF16)
    hnT = [Tl(hn_sb[:, :, g * 512:(g + 1) * 512]) for g in range(NG)]
    mem_sb = sba("memT", [128, 8, MEM], BF16)
    memT = Tl(mem_sb)
    cst = sba("cst_sb", [128, cn], F32)
    cstT = Tl(cst)
    prm = sba("prm_sb", [128, pn], F32)
    prmT = Tl(prm)
    cb_sb = sba("cstb", [128, 512], BF16)
    cbT = Tl(cb_sb)
    WA = Tl(sba("WA", [128, 8192], BF16))
    WB = Tl(sba("WB", [128, 8192], BF16))
    ws_ap = sba("WS", [128, 4096], BF16)
    WS0 = Tl(ws_ap[:, 0:2048])
    WS1 = Tl(ws_ap[:, 2048:4096])
    WT = Tl(sba("WT", [128, 8 * 32], BF16))
    banks = [Tl(nc.alloc_psum_tensor("ps%d" % i, [128, 512], F32).ap()) for i in range(8)]

    def cs(name):
        o, w = coff[name]
        return cst[:, o:o + w]

    def pr(name):
        o, w = poff[name]
        return prm[:, o:o + w]

    identb = cb_sb[:, 0:128]
    Ub = cb_sb[:, 128:256]
    onesb128 = cb_sb[:, 256:384]
    onesb = cb_sb[:, 256:320]

    p.dma("sp", cst, cst_d, wr=[cstT])
    p.dma("sp", prm, prm_d, wr=[prmT])
    p.op("dve", lambda e: e.tensor_copy(out=identb, in_=cs("ident")), rd=[cstT], wr=[cbT])
    p.op("dve", lambda e: e.tensor_copy(out=Ub, in_=cs("U")), rd=[cstT], wr=[cbT])
    p.op("dve", lambda e: e.tensor_copy(out=onesb128, in_=cs("ones")), rd=[cstT], wr=[cbT])
    epsc = sba("epsc", [128, 1], F32)
    epsT = Tl()
    p.op("dve", lambda e: e.memset(epsc, EPS), wr=[epsT])

    def rstd_chain(ss, lnv, rs, n, tl):
        p.op("act", lambda e: e.activation(out=lnv, in_=ss, func=AF.Ln, scale=1.0 / n, bias=epsc), rd=[tl, epsT], wr=[tl])
        p.op("act", lambda e: e.activation(out=rs, in_=lnv, func=AF.Exp, scale=-0.5), rd=[tl], wr=[tl])

    def wload(tiles, dst2d, src, kc, ncols, col0=0, pdim=128):
        dst = dst2d[0:pdim, col0:col0 + kc * ncols].rearrange("p (k n) -> p k n", k=kc)
        p.dma("pool", dst, src.rearrange("(k p) n -> p k n", p=pdim), wr=tiles)
        return dst

    def rmsnorm_T(sb, ntt, src, gcol, dst, psb):
        xn = [sb("xn", [128, D], BF16) for _ in range(2)]
        xnT = [Tl() for _ in range(2)]
        junk = sb("junk", [128, D], BF16)
        junkT = Tl()
        st = sb("st", [128, 8], F32)
        stT = [Tl(), Tl()]
        for tt in range(ntt):
            tiles, ap = src(tt)
            k = tt % 2
            ss, sd, rs = st[:, 4 * k:4 * k + 1], st[:, 4 * k + 1:4 * k + 2], st[:, 4 * k + 2:4 * k + 3]
            p.op("act", lambda e: e.activation(out=junk, in_=ap, func=AF.Square, accum_out=ss),
                 rd=tiles, wr=[junkT, stT[k]])
            rstd_chain(ss, sd, rs, D, stT[k])
            p.op("act", lambda e: e.activation(out=xn[k], in_=ap, func=AF.Copy, scale=rs),
                 rd=list(tiles) + [stT[k]], wr=[xnT[k]])
            pst = banks[psb[tt % len(psb)]]
            psv = pst.ap.bitcast(BF16)[:, 0:1024].rearrange("p (k n) -> p k n", k=8)
            for kc in range(8):
                p.op("pe", lambda e: e.transpose(out=psv[:, kc, :], in_=xn[k][:, kc * 128:(kc + 1) * 128],
                                                 identity=identb), rd=[xnT[k], cbT], wr=[pst])
            dt_, dap = dst(tt)
            p.op("dve", lambda e: e.tensor_tensor(out=dap, in0=psv,
                                                  in1=gcol.unsqueeze(2).broadcast_to([128, 8, 128]),
                                                  op=ALU.mult), rd=[pst, prmT], wr=[dt_])

    def h_src(tt):
        return hT[tt], h_sb[:, tt, :]

    def hn_dst(tt):
        return hnT[tt // 4], hn_sb[:, :, tt * 128:(tt + 1) * 128]

    def add_h(tt, hf, ps_t, ps_ap):
        p.op("dve", lambda e: e.tensor_tensor(out=hT[tt][hf].ap, in0=hT[tt][hf].ap, in1=ps_ap, op=ALU.add),
             rd=[ps_t, hT[tt][hf]], wr=[hT[tt][hf]])

    def mlp(l):
        wu0 = wload([WA], WA.ap, W["mlp_up"][l, :, 0:1024], 8, 1024)
        wd0 = wload([WB], WB.ap, W["mlp_down"][l, 0:1024, :], 8, 1024)
        with p.phase() as sb:
            rmsnorm_T(sb, NT, h_src, pr("nmlp%d" % l), hn_dst, (6, 7))
        with p.phase() as sb:
            aT = sb("aT", [128, 8, S], BF16)
            aTt = [Tl() for _ in range(NG)]
            rl = [sb("rl", [128, 512], F32) for _ in range(2)]
            rlT = [Tl(), Tl()]
            n = 0
            for fb in range(4):
                if fb == 0:
                    wu, wd = wu0, wd0
                else:
                    wu = wload([WA], WA.ap, W["mlp_up"][l, :, fb * 1024:(fb + 1) * 1024], 8, 1024)
                    wd = wload([WB], WB.ap, W["mlp_down"][l, fb * 1024:(fb + 1) * 1024, :], 8, 1024)
                for g in range(NG):
                    for fc in range(8):
                        pst = banks[n % 4]
                        for kc in range(8):
                            p.op("pe", lambda e: e.matmul(pst.ap, lhsT=wu[:, kc, fc * 128:(fc + 1) * 128],
                                                          rhs=hn_sb[:, kc, g * 512:(g + 1) * 512],
                                                          start=(kc == 0), stop=(kc == 7)),
                                 rd=[WA, hnT[g]], wr=[pst])
                        k = n % 2
                        p.op("act", lambda e: e.activation(out=rl[k], in_=pst.ap, func=AF.Relu),
                             rd=[pst], wr=[rlT[k]])
                        p.op("dve", lambda e: e.tensor_tensor(out=aT[:, fc, g * 512:(g + 1) * 512], in0=rl[k],
                                                              in1=rl[k], op=ALU.mult), rd=[rlT[k]], wr=[aTt[g]])
                        n += 1
                for tt in range(NT):
                    for hf in range(2):
                        pst = banks[4 + (n % 4)]
                        n += 1
                        for fc in range(8):
                            p.op("pe", lambda e: e.matmul(pst.ap, lhsT=aT[:, fc, tt * 128:(tt + 1) * 128],
                                                          rhs=wd[:, fc, hf * 512:(hf + 1) * 512],
                                                          start=(fc == 0), stop=(fc == 7)),
                                 rd=[WB, aTt[tt // 4]], wr=[pst])
                        add_h(tt, hf, pst, pst.ap)

    def xattn(l):
        wkv = wload([WA], WA.ap, W["xa_kv"][l], 8, 1024)
        wq = wload([WS0, WS1], ws_ap, W["xa_q"][l], 8, 512)
        wo = wload([WB], WB.ap, W["xa_o"][l], 4, 1024)
        with p.phase() as sb:
            rmsnorm_T(sb, NT, h_src, pr("nxa%d" % l), hn_dst, (6, 7))
        with p.phase() as sb:
            kT = sb("kT", [128, 4, MEM], BF16)
            kTt = Tl()
            va = sb("va", [128, 2, 4, 128], BF16)
            vat = Tl()
            qT = sb("qT", [128, 4, S], BF16)
            qTt = [Tl() for _ in range(NG)]
            n = 0
            for hh in range(4):
                pst = banks[n % 2]
                n += 1
                for kc in range(8):
                    p.op("pe", lambda e: e.matmul(pst.ap[:, 0:MEM], lhsT=wkv[:, kc, hh * 128:(hh + 1) * 128],
                                                  rhs=mem_sb[:, kc, :], start=(kc == 0), stop=(kc == 7)),
                         rd=[WA, memT], wr=[pst])
                p.op("act", lambda e: e.activation(out=kT[:, hh, :], in_=pst.ap[:, 0:MEM], func=AF.Copy),
                     rd=[pst], wr=[kTt])
            for kb in range(2):
                pst = banks[n % 2]
                n += 1
                for kc in range(8):
                    p.op("pe", lambda e: e.matmul(pst.ap, lhsT=mem_sb[:, kc, kb * 128:(kb + 1) * 128],
                                                  rhs=wkv[:, kc, 512:1024], start=(kc == 0), stop=(kc == 7)),
                         rd=[WA, memT], wr=[pst])
                p.op("act", lambda e: e.activation(out=va[:, kb, :, :],
                                                   in_=pst.ap.rearrange("p (h d) -> p h d", h=4), func=AF.Copy),
                     rd=[pst], wr=[vat])
            for g in range(NG):
                for hh in range(4):
                    pst = banks[n % 2]
                    n += 1
                    for kc in range(8):
                        p.op("pe", lambda e: e.matmul(pst.ap, lhsT=wq[:, kc, hh * 128:(hh + 1) * 128],
                                                      rhs=hn_sb[:, kc, g * 512:(g + 1) * 512],
                                                      start=(kc == 0), stop=(kc == 7)),
                             rd=[WS0, WS1, hnT[g]], wr=[pst])
                    p.op("act", lambda e: e.activation(out=qT[:, hh, g * 512:(g + 1) * 512], in_=pst.ap,
                                                       func=AF.Copy, scale=float(128 ** -0.5)),
                         rd=[pst], wr=[qTt[g]])
            PT = [sb("PT", [128, 512], BF16) for _ in range(4)]
            PTt = [Tl() for _ in range(4)]
            rs_ = [sb("rs", [128, 512], F32) for _ in range(2)]
            rst = [Tl(), Tl()]
            oT = [sb("oT", [128, 4, 512], BF16) for _ in range(2)]
            oTt = [Tl(), Tl()]
            items = [(g, hh) for g in range(NG) for hh in range(4)]

            def sbanks(idx):
                return (banks[0], banks[1]) if idx % 2 == 0 else (banks[6], banks[7])

            def xX(idx):
                g, hh = items[idx]
                for kb in range(2):
                    pst = sbanks(idx)[kb]
                    p.op("pe", lambda e: e.matmul(pst.ap, lhsT=kT[:, hh, kb * 128:(kb + 1) * 128],
                                                  rhs=qT[:, hh, g * 512:(g + 1) * 512], start=True, stop=True),
                         rd=[kTt, qTt[g]], wr=[pst])

            def xY(idx):
                nonlocal n
                g, hh = items[idx]
                og = g % 2
                pts = []
                for kb in range(2):
                    pst = sbanks(idx)[kb]
                    k = (2 * idx + kb) % 4
                    p.op("act", lambda e: e.activation(out=PT[k], in_=pst.ap, func=AF.Exp),
                         rd=[pst], wr=[PTt[k]])
                    pts.append(k)
                po, psum_ = banks[2 + (hh % 2) * 2], banks[3 + (hh % 2) * 2]
                for i, k in enumerate(pts):
                    p.op("pe", lambda e: e.matmul(po.ap, lhsT=va[:, i, hh, :], rhs=PT[k],
                                                  start=(i == 0), stop=(i == 1)), rd=[vat, PTt[k]], wr=[po])
                for i, k in enumerate(pts):
                    p.op("pe", lambda e: e.matmul(psum_.ap, lhsT=onesb128, rhs=PT[k],
                                                  start=(i == 0), stop=(i == 1)), rd=[cbT, PTt[k]], wr=[psum_])
                r = hh % 2
                p.op("dve", lambda e: e.reciprocal(out=rs_[r], in_=psum_.ap), rd=[psum_], wr=[rst[r]])
                p.op("dve", lambda e: e.tensor_tensor(out=oT[og][:, hh, :], in0=po.ap, in1=rs_[r], op=ALU.mult),
                     rd=[po, rst[r]], wr=[oTt[og]])
                if hh == 3:
                    for j in range(4):
                        tt = g * 4 + j
                        for hf in range(2):
                            pst = sbanks(idx)[n % 2]
                            n += 1
                            for kc in range(4):
                                p.op("pe", lambda e: e.matmul(pst.ap, lhsT=oT[og][:, kc, j * 128:(j + 1) * 128],
                                                              rhs=wo[:, kc, hf * 512:(hf + 1) * 512],
                                                              start=(kc == 0), stop=(kc == 3)),
                                     rd=[oTt[og], WB], wr=[pst])
                            add_h(tt, hf, pst, pst.ap)

            xX(0)
            for idx in range(len(items)):
                if idx + 1 < len(items):
                    xX(idx + 1)
                xY(idx)

    def even_mixer(l, e):
        Win = W["ev_in"][e]
        Wout = W["ev_out"][e]
        fw = (wload([WA], WA.ap, Win[:, 0:1024], 8, 1024),
              wload([WS0, WS1], ws_ap, Win[:, 1024:1536], 8, 512),
              wload([WT], WT.ap, Win[:, 1536:1544], 8, 8),
              wload([WB], WB.ap, Wout[0:512, :], 8, 1024, pdim=64))
        with p.phase() as sb:
            rmsnorm_T(sb, NT, h_src, pr("nmix%d" % l), hn_dst, (6, 7))
        fox(l, e, Win, Wout, fw)
        hgrn(l, e, Win, Wout)

    def fox(l, e, Win, Wout, fw):
        with p.phase() as sb:
            wqk, wv, wg, woF = fw
            NH = NT * 8
            lf = sb("lf", [128, NT, 8], F32)
            lfT = Tl()
            lf2 = lf.rearrange("p t h -> p (t h)")
            pg = banks[0]
            for tt in range(NT):
                for kc in range(8):
                    p.op("pe", lambda e_: e_.matmul(pg.ap[:, tt * 8:(tt + 1) * 8],
                                                    lhsT=hn_sb[:, kc, tt * 128:(tt + 1) * 128], rhs=wg[:, kc, :],
                                                    start=(kc == 0), stop=(kc == 7)), rd=[hnT[tt // 4], WT], wr=[pg])
            pgv = pg.ap[:, 0:NH].rearrange("p (t h) -> p t h", h=8)
            p.op("dve", lambda e_: e_.tensor_tensor(out=lf, in0=pgv,
                                                    in1=pr("fgb%d" % e).unsqueeze(1).broadcast_to([128, NT, 8]),
                                                    op=ALU.add), rd=[pg, prmT], wr=[lfT])
            p.op("act", lambda e_: e_.activation(out=lf, in_=lf, func=AF.Exp, scale=-1.0), rd=[lfT], wr=[lfT])
            p.op("dve", lambda e_: e_.tensor_scalar(out=lf, in0=lf, scalar1=1.0, scalar2=None, op0=ALU.add),
                 rd=[lfT], wr=[lfT])
            p.op("act", lambda e_: e_.activation(out=lf, in_=lf, func=AF.Ln), rd=[lfT], wr=[lfT])
            p.op("dve", lambda e_: e_.tensor_scalar(out=lf, in0=lf, scalar1=-1.0, scalar2=None, op0=ALU.mult),
                 rd=[lfT], wr=[lfT])
            pc, ptot, pm = banks[1], banks[2], banks[3]
            p.op("pe", lambda e_: e_.matmul(pc.ap[:, 0:NH], lhsT=cs("U"), rhs=lf2, start=True, stop=True),
                 rd=[cstT, lfT], wr=[pc])
            p.op("pe", lambda e_: e_.matmul(ptot.ap[:, 0:NH], lhsT=cs("ones"), rhs=lf2, start=True, stop=True),
                 rd=[cstT, lfT], wr=[ptot])
            tot = sb("tot", [128, NT, 8], F32)
            totT = Tl()
            offs = sb("offs", [128, NT, 8], F32)
            offT = Tl()
            c = sb("c", [128, NT, 8], F32)
            cT = Tl()
            cmb = sb("cmb", [128, NT, 8], F32)
            cmbT = Tl()
            p.op("act", lambda e_: e_.activation(out=tot.rearrange("p t h -> p (t h)"), in_=ptot.ap[:, 0:NH],
                                                 func=AF.Copy), rd=[ptot], wr=[totT])
            p.op("dve", lambda e_: e_.memset(offs[:, 0, :], 0.0), wr=[offT])
            for i in range(1, NT):
                p.op("dve", lambda e_: e_.tensor_tensor(out=offs[:, i, :], in0=offs[:, i - 1, :], in1=tot[:, i - 1, :],
                                                        op=ALU.add), rd=[totT, offT], wr=[offT])
            p.op("dve", lambda e_: e_.tensor_tensor(out=c.rearrange("p t h -> p (t h)"), in0=pc.ap[:, 0:NH],
                                                    in1=offs.rearrange("p t h -> p (t h)"), op=ALU.add),
                 rd=[pc, offT], wr=[cT])
            p.op("pe", lambda e_: e_.matmul(pm.ap[:, 0:NH], lhsT=cs("sel64"), rhs=c.rearrange("p t h -> p (t h)"),
                                            start=True, stop=True), rd=[cstT, cT], wr=[pm])
            p.op("act", lambda e_: e_.activation(out=cmb.rearrange("p t h -> p (t h)"), in_=pm.ap[:, 0:NH],
                                                 func=AF.Copy), rd=[pm], wr=[cmbT])
            negc = sb("negc", [128, NT, 8], F32)
            negcT = Tl()
            p.op("dve", lambda e_: e_.tensor_scalar(out=negc, in0=c, scalar1=-1.0, scalar2=None, op0=ALU.mult),
                 rd=[cT], wr=[negcT])
            hib = sb("hib", [128, NT], BF16)
            lo32 = sb("lo32", [128, NT], F32)
            hlT = Tl()
            fq = [sb("fq", [97, S], BF16) for _ in range(2)]
            fk = [sb("fk", [97, S], BF16) for _ in range(2)]
            fqT = [Tl(), Tl()]
            fkT = [Tl(), Tl()]
            for b2 in range(2):
                p.op("dve", lambda e_: e_.memset(fq[b2][64:97, :], 0.0), wr=[fqT[b2]])
                p.op("dve", lambda e_: e_.memset(fk[b2][64:97, :], 0.0), wr=[fkT[b2]])
                p.op("dve", lambda e_: e_.memset(fk[b2][64:65, :], 1.0), wr=[fkT[b2]])
                p.op("dve", lambda e_: e_.memset(fk[b2][96:97, :], 1.0), wr=[fkT[b2]])
            vall = sb("vall", [128, NT, 512], BF16)
            vallT = [Tl() for _ in range(NG)]
            nv_ = 0
            for tt in range(NT):
                pst = banks[nv_ % 2]
                nv_ += 1
                for kc in range(8):
                    p.op("pe", lambda e_: e_.matmul(pst.ap, lhsT=hn_sb[:, kc, tt * 128:(tt + 1) * 128], rhs=wv[:, kc, :],
                                                    start=(kc == 0), stop=(kc == 7)),
                         rd=[WS0, WS1, hnT[tt // 4]], wr=[pst])
                p.op("act", lambda e_: e_.activation(out=vall[:, tt, :], in_=pst.ap, func=AF.Copy),
                     rd=[pst], wr=[vallT[tt // 4]])
            oT = [sb("foT", [64, S], BF16) for _ in range(2)]
            oTt = [Tl(), Tl()]
            PT = [sb("fPT", [128, 512], BF16) for _ in range(3)]
            PTt = [[Tl() for _ in range(4)] for _ in range(3)]
            rs = [sb("frs", [64, 512], F32) for _ in range(2)]
            rsT = [Tl(), Tl()]
            n = 0
            m = 0
            def f_proj(hd):
                nonlocal n
                b_ = hd % 2
                p.op("dve", lambda e_: e_.tensor_copy(
                    out=fq[b_][64:65, :].rearrange("p (t c) -> p t c", c=128),
                    in_=cmb[64:65, :, hd].unsqueeze(2).broadcast_to([1, NT, 128])), rd=[cmbT], wr=[fqT[b_]])
                p.op("dve", lambda e_: e_.tensor_copy(out=hib[96:97, :], in_=cmb[96:97, :, hd]), rd=[cmbT], wr=[hlT])
                p.op("dve", lambda e_: e_.tensor_tensor(out=lo32[96:97, :], in0=cmb[96:97, :, hd], in1=hib[96:97, :],
                                                        op=ALU.subtract), rd=[cmbT, hlT], wr=[hlT])
                p.op("dve", lambda e_: e_.tensor_copy(
                    out=fq[b_][96:97, :].rearrange("p (t c) -> p t c", c=128),
                    in_=lo32[96:97, :].unsqueeze(2).broadcast_to([1, NT, 128])), rd=[hlT], wr=[fqT[b_]])
                for g in range(NG):
                    for dst, dT, col0, sc in ((fq[b_], fqT[b_], hd * 64, 0.125), (fk[b_], fkT[b_], 512 + hd * 64, 1.0)):
                        pst = banks[n % 2]
                        n += 1
                        for kc in range(8):
                            p.op("pe", lambda e_: e_.matmul(pst.ap[0:64, :], lhsT=wqk[:, kc, col0:col0 + 64],
                                                            rhs=hn_sb[:, kc, g * 512:(g + 1) * 512],
                                                            start=(kc == 0), stop=(kc == 7)),
                                 rd=[WA, hnT[g]], wr=[pst])
                        p.op("dve", lambda e_: e_.tensor_scalar(out=dst[0:64, g * 512:(g + 1) * 512], in0=pst.ap[0:64, :],
                                                                scalar1=sc, scalar2=None, op0=ALU.mult), rd=[pst], wr=[dT])

            def f_attn(hd):
                nonlocal m
                b_ = hd % 2
                items = [(G, j) for G in range(NG) for j in range(4 * G + 4)]

                def stX(idx):
                    G, j = items[idx]
                    i0 = max(j, 4 * G)
                    c0 = (i0 - 4 * G) * 128
                    pst = banks[2 + ((m + idx) % 2)]
                    p.op("pe", lambda e_: e_.matmul(pst.ap[:, c0:512], lhsT=fk[b_][:, j * 128:(j + 1) * 128],
                                                    rhs=fq[b_][:, G * 512 + c0:(G + 1) * 512], start=True, stop=True),
                         rd=[fkT[b_], fqT[b_]], wr=[pst])

                def stY(idx):
                    G, j = items[idx]
                    jmax = 4 * G + 3
                    i0 = max(j, 4 * G)
                    c0 = (i0 - 4 * G) * 128
                    pst = banks[2 + ((m + idx) % 2)]
                    po, psm = (banks[4], banks[5]) if G % 2 == 0 else (banks[6], banks[7])
                    k = (m + idx) % 3
                    blks = list(range(i0 - 4 * G, 4))
                    p.op("act", lambda e_: e_.activation(out=PT[k][:, c0:512], in_=pst.ap[:, c0:512],
                                                         func=AF.Exp, bias=negc[:, j, hd:hd + 1]),
                         rd=[pst, negcT], wr=[PTt[k][bi] for bi in blks])
                    if j >= 4 * G:
                        bi = j - 4 * G
                        cc = bi * 128
                        p.op("dve", lambda e_: e_.tensor_tensor(out=PT[k][:, cc:cc + 128], in0=PT[k][:, cc:cc + 128],
                                                                in1=Ub, op=ALU.mult),
                             rd=[PTt[k][bi], cbT], wr=[PTt[k][bi]])
                    rdt = [PTt[k][bi] for bi in blks]
                    p.op("pe", lambda e_: e_.matmul(po.ap[0:64, c0:512], lhsT=vall[:, j, hd * 64:(hd + 1) * 64], rhs=PT[k][:, c0:512],
                                                    start=(j == 0), stop=(j == jmax)), rd=rdt + [vallT[j // 4]], wr=[po])
                    p.op("pe", lambda e_: e_.matmul(psm.ap[0:64, c0:512], lhsT=onesb, rhs=PT[k][:, c0:512],
                                                    start=(j == 0), stop=(j == jmax)), rd=rdt + [cbT], wr=[psm])
                    if j == jmax:
                        r_ = G % 2
                        p.op("dve", lambda e_: e_.reciprocal(out=rs[r_], in_=psm.ap[0:64, :]), rd=[psm], wr=[rsT[r_]])
                        p.op("dve", lambda e_: e_.tensor_tensor(out=oT[b_][:, G * 512:(G + 1) * 512], in0=po.ap[0:64, :],
                                                                in1=rs[r_], op=ALU.mult), rd=[po, rsT[r_]], wr=[oTt[b_]])

                stX(0)
                for idx in range(len(items)):
                    if idx + 1 < len(items):
                        stX(idx + 1)
                    stY(idx)
                m += len(items)

            def f_out_pair(i):
                nonlocal n
                for tt in range(NT):
                    for hf in range(2):
                        pst = banks[n % 2]
                        n += 1
                        for b2 in range(2):
                            p.op("pe", lambda e_: e_.matmul(pst.ap, lhsT=oT[b2][:, tt * 128:(tt + 1) * 128],
                                                            rhs=woF[:, 2 * i + b2, hf * 512:(hf + 1) * 512],
                                                            start=(b2 == 0), stop=(b2 == 1)),
                                 rd=[oTt[b2], WB], wr=[pst])
                        add_h(tt, hf, pst, pst.ap)

            f_proj(0)
            for hd in range(8):
                if hd + 1 < 8:
                    f_proj(hd + 1)
                f_attn(hd)
                if hd % 2 == 1:
                    f_out_pair(hd // 2)
    def hgrn(l, e, Win, Wout):
        assert ne <= 2
        with p.phase() as sb:
            wqf = wload([WA], WA.ap, Win[:, 1544:2568], 8, 1024)
            wig = wload([WB], WB.ap, Win[:, 2568:3592], 8, 1024)
            woH = wload([WS0, WS1], ws_ap, Wout[512:1024, :], 4, 1024)
            lbs = sb("lbs", [128, 12], F32)
            lbT = Tl()
            lb, oml, noml = lbs[:, 0:4], lbs[:, 4:8], lbs[:, 8:12]
            if e == 0:
                p.op("dve", lambda e_: e_.memset(lb, 0.0), wr=[lbT])
            else:
                p.op("dve", lambda e_: e_.tensor_tensor(out=lb, in0=pr("lbl1"), in1=pr("lbl0"), op=ALU.subtract),
                     rd=[prmT], wr=[lbT])
                p.op("act", lambda e_: e_.activation(out=lb, in_=lb, func=AF.Sigmoid), rd=[lbT], wr=[lbT])
            p.op("dve", lambda e_: e_.tensor_scalar(out=oml, in0=lb, scalar1=-1.0, scalar2=1.0, op0=ALU.mult,
                                                    op1=ALU.add), rd=[lbT], wr=[lbT])
            p.op("dve", lambda e_: e_.tensor_scalar(out=noml, in0=oml, scalar1=-1.0, scalar2=None, op0=ALU.mult),
                 rd=[lbT], wr=[lbT])
            Sst = sb("Sst", [128, 4, 128], F32)
            SstT = [Tl() for _ in range(4)]
            p.op("dve", lambda e_: e_.memset(Sst, 0.0), wr=SstT)
            vtok = sb("vtok", [128, 4, 512], BF16)
            vtT = [Tl() for _ in range(4)]
            gsg = sb("gsg", [128, 4, 512], F32)
            gsT = [Tl() for _ in range(4)]
            sg = sb("sg", [128, 512], F32)
            sgT = Tl()

            def f32t(name):
                return sb(name, [128, 512], F32), Tl()

            def b16t(name):
                return sb(name, [128, 512], BF16), Tl()

            qs, qsT = f32t("qs")
            sig, sigT = f32t("sig")
            lg, lgT = f32t("lg")
            kin, kinT = f32t("kin")
            bb, bbT = f32t("bb")
            tmp, tmpT = f32t("tmp")
            ex, exT = f32t("ex")
            qbf, qbfT = f32t("qbf")
            qtilS = [b16t("qtil") for _ in range(2)]
            ktilS = [b16t("ktil") for _ in range(2)]
            qbAS = [b16t("qbA") for _ in range(2)]
            qbBS = [b16t("qbB") for _ in range(2)]
            kdTS = [b16t("kdT") for _ in range(2)]
            dkS = [(sb("dk", [128, 8], F32), Tl()) for _ in range(2)]
            kdA = [sb("kdA", [128, 128], BF16) for _ in range(2)]
            kdB = [sb("kdB", [128, 128], BF16) for _ in range(2)]
            kdAT, kdBT = [Tl(), Tl()], [Tl(), Tl()]
            ST = [sb("ST", [128, 128], BF16) for _ in range(2)]
            STT = [Tl(), Tl()]
            st = [sb("hst", [128, 4], F32) for _ in range(2)]
            stT = [Tl(), Tl()]
            bo = [sb("bo", [128, 128], BF16) for _ in range(2)]
            boT_ = [Tl(), Tl()]
            Sall = sb("Sall", [128, 9, 128], BF16)
            SallT = [Tl() for _ in range(9)]
            boT = [sb("boT", [128, 512], BF16) for _ in range(4)]
            boTT = [Tl() for _ in range(4)]
            b3 = bb.rearrange("p (c t) -> p c t", t=64)
            tmp3 = tmp.rearrange("p (c t) -> p c t", t=64)
            n = 0
            for g in range(NG):
                gs = slice(g * 512, (g + 1) * 512)
                for j in range(4):
                    tt = g * 4 + j
                    pst = banks[n % 2]
                    n += 1
                    for kc in range(8):
                        p.op("pe", lambda e_: e_.matmul(pst.ap, lhsT=hn_sb[:, kc, tt * 128:(tt + 1) * 128],
                                                        rhs=wig[:, kc, 0:512], start=(kc == 0), stop=(kc == 7)),
                             rd=[WB, hnT[g]], wr=[pst])
                    p.op("act", lambda e_: e_.activation(out=vtok[:, j, :], in_=pst.ap, func=AF.Copy),
                         rd=[pst], wr=[vtT[j]])
                    pst = banks[n % 2]
                    n += 1
                    for kc in range(8):
                        p.op("pe", lambda e_: e_.matmul(pst.ap, lhsT=hn_sb[:, kc, tt * 128:(tt + 1) * 128],
                                                        rhs=wig[:, kc, 512:1024], start=(kc == 0), stop=(kc == 7)),
                             rd=[WB, hnT[g]], wr=[pst])
                    p.op("act", lambda e_: e_.activation(out=sg, in_=pst.ap, func=AF.Silu), rd=[pst], wr=[sgT])
                    p.op("dve", lambda e_: e_.tensor_tensor(out=gsg[:, j, :], in0=sg, in1=pr("hon%d" % e), op=ALU.mult),
                         rd=[sgT, prmT], wr=[gsT[j]])
                def h_pro(hh):
                    nonlocal n
                    k2 = hh % 2
                    qtil, qtilT = qtilS[k2]
                    ktil, ktilT = ktilS[k2]
                    qbA, qbAT = qbAS[k2]
                    qbB, qbBT = qbBS[k2]
                    kdT_, kdTT = kdTS[k2]
                    dk, dkT = dkS[k2]
                    pq, pf = banks[n % 2], banks[(n + 1) % 2]
                    for pst, c0 in ((pq, hh * 128), (pf, 512 + hh * 128)):
                        for kc in range(8):
                            p.op("pe", lambda e_: e_.matmul(pst.ap, lhsT=wqf[:, kc, c0:c0 + 128], rhs=hn_sb[:, kc, gs],
                                                            start=(kc == 0), stop=(kc == 7)), rd=[WA, hnT[g]], wr=[pst])
                    p.op("act", lambda e_: e_.activation(out=qs, in_=pq.ap, func=AF.Silu), rd=[pq], wr=[qsT])
                    p.op("act", lambda e_: e_.activation(out=sig, in_=pf.ap, func=AF.Tanh, scale=0.5), rd=[pf], wr=[sigT])
                    p.op("dve", lambda e_: e_.tensor_scalar(out=sig, in0=sig, scalar1=0.5, scalar2=0.5, op0=ALU.mult,
                                                            op1=ALU.add), rd=[sigT], wr=[sigT])
                    p.op("dve", lambda e_: e_.tensor_scalar(out=lg, in0=sig, scalar1=oml[:, hh:hh + 1],
                                                            scalar2=lb[:, hh:hh + 1], op0=ALU.mult, op1=ALU.add),
                         rd=[sigT, lbT], wr=[lgT])
                    p.op("act", lambda e_: e_.activation(out=lg, in_=lg, func=AF.Ln), rd=[lgT], wr=[lgT])
                    p.op("dve", lambda e_: e_.tensor_scalar(out=kin, in0=sig, scalar1=noml[:, hh:hh + 1],
                                                            scalar2=oml[:, hh:hh + 1], op0=ALU.mult, op1=ALU.add),
                         rd=[sigT, lbT], wr=[kinT])
                    p.op("dve", lambda e_: e_.tensor_tensor_scan(out=bb, data0=cs("scanm"), data1=lg, initial=0.0,
                                                                 op0=ALU.mult, op1=ALU.add), rd=[cstT, lgT], wr=[bbT])
                    p.op("dve", lambda e_: e_.tensor_tensor(out=tmp3, in0=b3, in1=b3[:, :, 31:32].broadcast_to([128, 8, 64]),
                                                            op=ALU.subtract), rd=[bbT], wr=[tmpT])
                    p.op("act", lambda e_: e_.activation(out=ex, in_=tmp, func=AF.Exp), rd=[tmpT], wr=[exT])
                    p.op("dve", lambda e_: e_.tensor_tensor(out=qtil, in0=qs, in1=ex, op=ALU.mult), rd=[qsT, exT], wr=[qtilT])
                    p.op("act", lambda e_: e_.activation(out=ex, in_=tmp, func=AF.Exp, scale=-1.0), rd=[tmpT], wr=[exT])
                    p.op("dve", lambda e_: e_.tensor_tensor(out=ktil, in0=kin, in1=ex, op=ALU.mult), rd=[kinT, exT], wr=[ktilT])
                    p.op("act", lambda e_: e_.activation(out=ex, in_=bb, func=AF.Exp), rd=[bbT], wr=[exT])
                    p.op("dve", lambda e_: e_.tensor_tensor(out=qbf, in0=qs, in1=ex, op=ALU.mult), rd=[qsT, exT], wr=[qbfT])
                    p.op("dve", lambda e_: e_.tensor_tensor(out=qbA.rearrange("p (j c) -> p j c", c=128), in0=qbf.rearrange("p (j c) -> p j c", c=128), in1=cs("colA").unsqueeze(1).broadcast_to([128, 4, 128]), op=ALU.mult),
                         rd=[qbfT, cstT], wr=[qbAT])
                    p.op("dve", lambda e_: e_.tensor_tensor(out=qbB.rearrange("p (j c) -> p j c", c=128), in0=qbf.rearrange("p (j c) -> p j c", c=128), in1=cs("colB").unsqueeze(1).broadcast_to([128, 4, 128]), op=ALU.mult),
                         rd=[qbfT, cstT], wr=[qbBT])
                    p.op("dve", lambda e_: e_.tensor_tensor(out=tmp3, in0=b3, in1=b3[:, :, 63:64].broadcast_to([128, 8, 64]),
                                                            op=ALU.subtract), rd=[bbT], wr=[tmpT])
                    p.op("act", lambda e_: e_.activation(out=ex, in_=tmp, func=AF.Exp, scale=-1.0), rd=[tmpT], wr=[exT])
                    p.op("dve", lambda e_: e_.tensor_tensor(out=kdT_, in0=kin, in1=ex, op=ALU.mult), rd=[kinT, exT], wr=[kdTT])
                    p.op("act", lambda e_: e_.activation(out=dk, in_=b3[:, :, 63], func=AF.Exp), rd=[bbT], wr=[dkT])

                def h_inner(hh):
                    k2 = hh % 2
                    qtil, qtilT = qtilS[k2]
                    ktil, ktilT = ktilS[k2]
                    qbA, qbAT = qbAS[k2]
                    qbB, qbBT = qbBS[k2]
                    kdT_, kdTT = kdTS[k2]
                    dk, dkT = dkS[k2]
                    p.op("act", lambda e_: e_.activation(out=Sall[:, 0, :], in_=Sst[:, hh, :], func=AF.Copy),
                         rd=[SstT[hh]], wr=[SallT[0]])

                    def stF(j):
                        kk = j % 2
                        cl = slice(j * 128, (j + 1) * 128)
                        vj = vtok[:, j, hh * 128:(hh + 1) * 128]
                        ptr, pu, psc = banks[4], banks[2 + kk], banks[6]
                        ptv = ptr.ap.bitcast(BF16)[:, 0:128]
                        p.op("pe", lambda e_: e_.transpose(out=ptv, in_=kdT_[:, cl], identity=identb),
                             rd=[kdTT, cbT], wr=[ptr])
                        p.op("dve", lambda e_: e_.tensor_scalar(out=kdA[kk], in0=ptv, scalar1=cs("rowA"), scalar2=None,
                                                                op0=ALU.mult), rd=[ptr, cstT], wr=[kdAT[kk]])
                        p.op("dve", lambda e_: e_.tensor_scalar(out=kdB[kk], in0=ptv, scalar1=cs("rowB"), scalar2=None,
                                                                op0=ALU.mult), rd=[ptr, cstT], wr=[kdBT[kk]])
                        p.op("pe", lambda e_: e_.matmul(pu.ap[:, 0:128], lhsT=kdA[kk], rhs=vj, start=True, stop=True),
                             rd=[kdAT[kk], vtT[j]], wr=[pu])
                        p.op("pe", lambda e_: e_.matmul(pu.ap[:, 128:256], lhsT=kdB[kk], rhs=vj, start=True, stop=True),
                             rd=[kdBT[kk], vtT[j]], wr=[pu])
                        p.op("pe", lambda e_: e_.matmul(psc.ap[:, 0:128], lhsT=ktil[:, cl], rhs=qtil[:, cl],
                                                        start=True, stop=True), rd=[ktilT, qtilT], wr=[psc])
                        p.op("dve", lambda e_: e_.tensor_tensor(out=ST[kk], in0=psc.ap[:, 0:128], in1=cs("maskbd"),
                                                                op=ALU.mult), rd=[psc, cstT], wr=[STT[kk]])
                        for half in range(2):
                            ci = 2 * j + half
                            p.op("dve", lambda e_: e_.scalar_tensor_tensor(out=Sst[:, hh, :], in0=Sst[:, hh, :],
                                                                           scalar=dk[:, ci:ci + 1],
                                                                           in1=pu.ap[:, half * 128:(half + 1) * 128],
                                                                           op0=ALU.mult, op1=ALU.add),
                                 rd=[SstT[hh], dkT, pu], wr=[SstT[hh]])
                            p.op("act", lambda e_: e_.activation(out=Sall[:, ci + 1, :], in_=Sst[:, hh, :], func=AF.Copy),
                                 rd=[SstT[hh]], wr=[SallT[ci + 1]])

                    def stBk(j):
                        kk = j % 2
                        cl = slice(j * 128, (j + 1) * 128)
                        vj = vtok[:, j, hh * 128:(hh + 1) * 128]
                        po, ptr2 = banks[7], banks[5]
                        p.op("pe", lambda e_: e_.matmul(po.ap[:, 0:128], lhsT=ST[kk], rhs=vj, start=True, stop=False),
                             rd=[STT[kk], vtT[j]], wr=[po])
                        p.op("pe", lambda e_: e_.matmul(po.ap[:, 0:128], lhsT=qbA[:, cl], rhs=Sall[:, 2 * j, :],
                                                        start=False, stop=False), rd=[qbAT, SallT[2 * j]], wr=[po])
                        p.op("pe", lambda e_: e_.matmul(po.ap[:, 0:128], lhsT=qbB[:, cl], rhs=Sall[:, 2 * j + 1, :],
                                                        start=False, stop=True), rd=[qbBT, SallT[2 * j + 1]], wr=[po])
                        p.op("act", lambda e_: e_.activation(out=bo[kk], in_=po.ap[:, 0:128], func=AF.Square,
                                                             accum_out=st[kk][:, 0:1]), rd=[po], wr=[boT_[kk], stT[kk]])
                        rstd_chain(st[kk][:, 0:1], st[kk][:, 1:2], st[kk][:, 2:3], 128, stT[kk])
                        p.op("dve", lambda e_: e_.scalar_tensor_tensor(out=bo[kk], in0=po.ap[:, 0:128], scalar=st[kk][:, 2:3],
                                                                       in1=gsg[:, j, hh * 128:(hh + 1) * 128],
                                                                       op0=ALU.mult, op1=ALU.mult),
                             rd=[po, stT[kk], gsT[j]], wr=[boT_[kk]])
                        ptv2 = ptr2.ap.bitcast(BF16)[:, 0:128]
                        p.op("pe", lambda e_: e_.transpose(out=ptv2, in_=bo[kk], identity=identb), rd=[boT_[kk], cbT], wr=[ptr2])
                        p.op("act", lambda e_: e_.activation(out=boT[hh][:, cl], in_=ptv2, func=AF.Copy),
                             rd=[ptr2], wr=[boTT[hh]])

                    stF(0)
                    for j in range(4):
                        if j + 1 < 4:
                            stF(j + 1)
                        stBk(j)

                h_pro(0)
                for hh in range(4):
                    if hh + 1 < 4:
                        h_pro(hh + 1)
                    h_inner(hh)
                for j in range(4):
                    tt = g * 4 + j
                    for hf in range(2):
                        pst = banks[n % 2]
                        n += 1
                        for hh in range(4):
                            p.op("pe", lambda e_: e_.matmul(pst.ap, lhsT=boT[hh][:, j * 128:(j + 1) * 128],
                                                            rhs=woH[:, hh, hf * 512:(hf + 1) * 512],
                                                            start=(hh == 0), stop=(hh == 3)),
                                 rd=[boTT[hh], WS0, WS1], wr=[pst])
                        add_h(tt, hf, pst, pst.ap)

    def ssd_mixer(l, o):
        Win = W["ssm_in"][o]
        Wout = W["ssm_out"][o]
        wdt = wload([WT], WT.ap, Win[:, 6144:6176], 8, 32)

        def load_group(g):
            WG = WA if g % 2 == 0 else WB
            WO = WS0 if g % 2 == 0 else WS1
            return (wload([WG], WG.ap, Win[:, g * 256:(g + 1) * 256], 8, 256, col0=0),
                    wload([WG], WG.ap, Win[:, 2048 + g * 256:2048 + (g + 1) * 256], 8, 256, col0=2048),
                    wload([WG], WG.ap, Win[:, 4096 + g * 128:4096 + (g + 1) * 128], 8, 128, col0=4096),
                    wload([WG], WG.ap, Win[:, 5120 + g * 128:5120 + (g + 1) * 128], 8, 128, col0=5120),
                    wload([WO], WO.ap, Wout[g * 256:(g + 1) * 256, :], 2, 1024))

        gw = {0: load_group(0)}
        with p.phase() as sb:
            rmsnorm_T(sb, NT, h_src, pr("nmix%d" % l), hn_dst, (6, 7))
        with p.phase() as sb:
            NH = NT * 32
            N4 = NT * 4
            dt = sb("dt", [128, NT, 32], F32)
            dt2 = dt.rearrange("p t h -> p (t h)")
            dtT = Tl()
            da = sb("da", [128, NT, 32], F32)
            daT = Tl()
            abc = sb("abc", [128, 32], F32)
            abcT = Tl()
            pd = banks[0]
            for tt in range(NT):
                for kc in range(8):
                    p.op("pe", lambda e_: e_.matmul(pd.ap[:, tt * 32:(tt + 1) * 32],
                                                    lhsT=hn_sb[:, kc, tt * 128:(tt + 1) * 128], rhs=wdt[:, kc, :],
                                                    start=(kc == 0), stop=(kc == 7)), rd=[hnT[tt // 4], WT], wr=[pd])
            p.op("dve", lambda e_: e_.tensor_tensor(out=dt, in0=pd.ap[:, 0:NH].rearrange("p (t h) -> p t h", h=32),
                                                    in1=pr("dtb%d" % o).unsqueeze(1).broadcast_to([128, NT, 32]),
                                                    op=ALU.add), rd=[pd, prmT], wr=[dtT])
            p.op("act", lambda e_: e_.activation(out=dt2, in_=dt2, func=AF.Exp), rd=[dtT], wr=[dtT])
            p.op("dve", lambda e_: e_.tensor_scalar(out=dt2, in0=dt2, scalar1=1.0, scalar2=None, op0=ALU.add),
                 rd=[dtT], wr=[dtT])
            p.op("act", lambda e_: e_.activation(out=dt2, in_=dt2, func=AF.Ln), rd=[dtT], wr=[dtT])
            p.op("act", lambda e_: e_.activation(out=abc, in_=pr("alog%d" % o), func=AF.Exp), rd=[prmT], wr=[abcT])
            p.op("dve", lambda e_: e_.tensor_scalar(out=abc, in0=abc, scalar1=-1.0, scalar2=None, op0=ALU.mult),
                 rd=[abcT], wr=[abcT])
            p.op("dve", lambda e_: e_.tensor_tensor(out=da, in0=dt, in1=abc.unsqueeze(1).broadcast_to([128, NT, 32]),
                                                    op=ALU.mult), rd=[dtT, abcT], wr=[daT])

            def sm(name):
                a = sb(name, [128, NT, 4], F32)
                return a, a.rearrange("p t h -> p (t h)"), Tl()

            ecum, ecum2, ecumT = sm("ecum")
            ecl, ecl2, eclT = sm("ecl")
            w_, w2, wT_ = sm("w")
            xg = sb("xg", [128, NT, 256], BF16)
            xgT = [Tl() for _ in range(NG)]
            BT = sb("BT", [128, S], BF16)
            BTt = [Tl() for _ in range(NG)]
            Bg = sb("Bg", [128, NT, 128], BF16)
            BgT = [Tl() for _ in range(NG)]
            CT = sb("CT", [128, S], BF16)
            CTt = [Tl() for _ in range(NG)]
            STall = sb("STall", [128, NT, 256], BF16)
            STaT = [Tl() for _ in range(NT)]
            STs = sb("STs", [128, 256], F32)
            STsT = Tl()
            cw = pr("cw%d" % o)
            cbp = pr("cb%d" % o)

            def v4(ap):
                return ap.rearrange("p (h d) -> p h d", d=64)

            def bc(ap):
                return ap.unsqueeze(2).broadcast_to([128, 4, 64])

            n = 0
            for g in range(8):
                WG = WA if g % 2 == 0 else WB
                WO = WS0 if g % 2 == 0 else WS1
                wz, wx, wB, wC, wo = gw[g]
                hs = slice(g * 4, g * 4 + 4)
                pcu, pcl = banks[6], banks[7]
                p.op("pe", lambda e_: e_.matmul(pcu.ap[:, 0:N4].rearrange("p (t h) -> p t h", h=4), lhsT=cs("U"),
                                                rhs=da[:, :, hs], start=True, stop=True), rd=[cstT, daT], wr=[pcu])
                p.op("pe", lambda e_: e_.matmul(pcl.ap[:, 0:N4].rearrange("p (t h) -> p t h", h=4), lhsT=cs("ones"),
                                                rhs=da[:, :, hs], start=True, stop=True), rd=[cstT, daT], wr=[pcl])
                p.op("act", lambda e_: e_.activation(out=ecum2, in_=pcu.ap[:, 0:N4], func=AF.Exp), rd=[pcu], wr=[ecumT])
                p.op("act", lambda e_: e_.activation(out=ecl2, in_=pcl.ap[:, 0:N4], func=AF.Exp), rd=[pcl], wr=[eclT])
                p.op("act", lambda e_: e_.activation(out=w2, in_=pcu.ap[:, 0:N4], func=AF.Copy), rd=[pcu], wr=[wT_])
                p.op("dve", lambda e_: e_.tensor_tensor(out=w2, in0=pcl.ap[:, 0:N4], in1=w2, op=ALU.subtract),
                     rd=[pcl, wT_], wr=[wT_])
                p.op("act", lambda e_: e_.activation(out=w2, in_=w2, func=AF.Exp), rd=[wT_], wr=[wT_])
                p.op("dve", lambda e_: e_.tensor_tensor(out=w_, in0=w_, in1=dt[:, :, hs], op=ALU.mult),
                     rd=[wT_, dtT], wr=[wT_])
                chunks = ((wx[:, :, 0:128], 2 * g, "x0"), (wx[:, :, 128:256], 2 * g + 1, "x1"),
                          (wB, 16 + g, "B"), (wC, 24 + g, "C"))
                with p.phase() as sb2:
                    xps = [sb2("xp", [128, 515], F32) for _ in range(2)]
                    xpTs = [Tl(), Tl()]
                    accs = [sb2("acc", [128, 512], F32) for _ in range(2)]
                    accTs = [Tl(), Tl()]
                    xTcs = [sb2("xTc", [128, 512], BF16) for _ in range(2)]
                    xTcTs = [Tl(), Tl()]
                    items = [(wsrc, ch, kind, tg) for (wsrc, ch, kind) in chunks for tg in range(NG)]
                    pend = {}

                    def stP1(idx):
                        nonlocal n
                        wsrc, ch, kind, tg = items[idx]
                        kk = idx % 2
                        xp, xpT = xps[kk], xpTs[kk]
                        pst = banks[n % 2]
                        n += 1
                        for kc in range(8):
                            p.op("pe", lambda e_: e_.matmul(pst.ap, lhsT=wsrc[:, kc, :], rhs=hn_sb[:, kc, tg * 512:(tg + 1) * 512],
                                                            start=(kc == 0), stop=(kc == 7)), rd=[WG, hnT[tg]], wr=[pst])
                        if tg == 0:
                            p.op("dve", lambda e_: e_.memset(xp[:, 0:3], 0.0), wr=[xpT])
                        else:
                            p.op("dve", lambda e_: e_.tensor_copy(out=xp[:, 0:3], in_=xps[1 - kk][:, 512:515]),
                                 rd=[xpTs[1 - kk]], wr=[xpT])
                        p.op("act", lambda e_: e_.activation(out=xp[:, 3:515], in_=pst.ap, func=AF.Copy),
                             rd=[pst], wr=[xpT])

                    def stP2(idx):
                        wsrc, ch, kind, tg = items[idx]
                        kk = idx % 2
                        xp, xpT, acc, accT = xps[kk], xpTs[kk], accs[kk], accTs[kk]
                        xTc, xTcT = xTcs[kk], xTcTs[kk]
                        p.op("dve", lambda e_: e_.tensor_scalar(out=acc, in0=xp[:, 0:512], scalar1=cw[:, ch * 4:ch * 4 + 1],
                                                                scalar2=None, op0=ALU.mult), rd=[xpT, prmT], wr=[accT])
                        for j in range(1, 4):
                            p.op("dve", lambda e_: e_.scalar_tensor_tensor(out=acc, in0=xp[:, j:j + 512],
                                                                           scalar=cw[:, ch * 4 + j:ch * 4 + j + 1], in1=acc,
                                                                           op0=ALU.mult, op1=ALU.add),
                                 rd=[xpT, prmT, accT], wr=[accT])
                        gsl = slice(tg * 512, (tg + 1) * 512)
                        if kind in ("x0", "x1"):
                            dst, dT = xTc, xTcT
                        elif kind == "B":
                            dst, dT = BT[:, gsl], BTt[tg]
                        else:
                            dst, dT = CT[:, gsl], CTt[tg]
                        p.op("act", lambda e_: e_.activation(out=dst, in_=acc, func=AF.Silu, bias=cbp[:, ch:ch + 1]),
                             rd=[accT, prmT], wr=[dT])
                        pend[idx] = (dst, dT)

                    def stT(idx):
                        wsrc, ch, kind, tg = items[idx]
                        if kind == "C":
                            return
                        dst, dT = pend[idx]
                        ptr = banks[2 + (idx % 2)]
                        ptv = ptr.ap.bitcast(BF16)[:, 0:512].rearrange("p (t c) -> p t c", c=128)
                        for j in range(4):
                            p.op("pe", lambda e_: e_.transpose(out=ptv[:, j, :], in_=dst[:, j * 128:(j + 1) * 128],
                                                               identity=identb), rd=[dT, cbT], wr=[ptr])
                        if kind == "B":
                            o_ap, oT_ = Bg[:, tg * 4:(tg + 1) * 4, :], BgT[tg]
                        else:
                            c0 = 0 if kind == "x0" else 128
                            o_ap, oT_ = xg[:, tg * 4:(tg + 1) * 4, c0:c0 + 128], xgT[tg]
                        p.op("act", lambda e_: e_.activation(out=o_ap, in_=ptv, func=AF.Copy), rd=[ptr], wr=[oT_])

                    stP1(0)
                    for idx in range(len(items) + 1):
                        if idx + 1 < len(items):
                            stP1(idx + 1)
                        if idx < len(items):
                            stP2(idx)
                        if idx >= 1:
                            stT(idx - 1)
                if g + 1 < 8:
                    gw[g + 1] = load_group(g + 1)
                with p.phase() as sb2:
                    def two(name, shape, dt_):
                        return [sb2(name, shape, dt_) for _ in range(2)], [Tl(), Tl()]

                    Lh4 = sb2("Lh4", [128, 4, 128], F32)
                    Lh4T = Tl()
                    eE4s, eE4Ts = two("eE4", [128, 4, 128], F32)
                    ez1 = sb2("ez", [128, 256], F32)
                    ez, ezT = [ez1, ez1], [Tl()] * 2
                    sz = [sb2("sz", [128, 256], F32) for _ in range(3)]
                    szT = [Tl(), Tl(), Tl()]
                    CBm, CBmT = two("CBm", [128, 128], F32)
                    MT4, MT4T = two("MT4", [128, 4, 128], BF16)
                    xdt1 = sb2("xdt", [128, 256], BF16)
                    xdt, xdtT = [xdt1, xdt1], [Tl()] * 2
                    t1, t1T = two("t1", [128, 256], F32)
                    t2, t2T = two("t2", [128, 256], BF16)
                    xD, xDT = two("xD", [128, 256], BF16)
                    yn, ynT_ = two("yn", [128, 256], BF16)
                    ynT1 = sb2("ynT", [128, 2, 128], BF16)
                    ynT, ynTT = [ynT1, ynT1], [Tl()] * 2
                    xw1 = sb2("xw", [128, 256], BF16)
                    xw, xwT = [xw1, xw1], [Tl()] * 2
                    st, stT = two("sst", [128, 4], F32)
                    p.op("dve", lambda e_: e_.memset(STs, 0.0), wr=[STsT])
                    p.op("dve", lambda e_: e_.memset(STall[:, 0, :], 0.0), wr=[STaT[0]])

                    def pre_a(tt):
                        k = tt % 2
                        p.op("pool", lambda e_: e_.tensor_tensor(out=v4(xw[k]), in0=v4(xg[:, tt, :]), in1=bc(w_[:, tt, :]),
                                                                 op=ALU.mult), rd=[xgT[tt // 4], wT_], wr=[xwT[k]])
                        p.op("pe", lambda e_: e_.matmul(banks[2 + k].ap[:, 0:256], lhsT=Bg[:, tt, :], rhs=xw[k],
                                                        start=True, stop=True), rd=[BgT[tt // 4], xwT[k]], wr=[banks[2 + k]])

                    def pre_b(tt):
                        k = tt % 2
                        p.op("dve", lambda e_: e_.tensor_tensor(out=v4(STs), in0=v4(STs), in1=bc(ecl[:, tt, :]), op=ALU.mult),
                             rd=[STsT, eclT], wr=[STsT])
                        p.op("dve", lambda e_: e_.tensor_tensor(out=STs, in0=STs, in1=banks[2 + k].ap[:, 0:256], op=ALU.add),
                             rd=[STsT, banks[2 + k]], wr=[STsT])
                        p.op("act", lambda e_: e_.activation(out=STall[:, tt + 1, :], in_=STs, func=AF.Copy),
                             rd=[STsT], wr=[STaT[tt + 1]])

                    if NT > 1:
                        pre_a(0)
                    for tt in range(NT - 1):
                        if tt + 1 < NT - 1:
                            pre_a(tt + 1)
                        pre_b(tt)

                    def stA(tt):
                        k = tt % 2
                        cl = slice(tt * 128, (tt + 1) * 128)
                        tg = tt // 4
                        pz = banks[6]
                        for kc in range(8):
                            p.op("pe", lambda e_: e_.matmul(pz.ap[:, 0:256], lhsT=hn_sb[:, kc, cl], rhs=wz[:, kc, :],
                                                            start=(kc == 0), stop=(kc == 7)), rd=[WG, hnT[tg]], wr=[pz])
                        p.op("act", lambda e_: e_.activation(out=ez[k], in_=pz.ap[:, 0:256], func=AF.Exp, scale=-1.0),
                             rd=[pz], wr=[ezT[k]])
                        pcb = banks[7]
                        p.op("pe", lambda e_: e_.matmul(pcb.ap[:, 0:128], lhsT=BT[:, cl], rhs=CT[:, cl], start=True, stop=True),
                             rd=[BTt[tg], CTt[tg]], wr=[pcb])
                        p.op("dve", lambda e_: e_.tensor_tensor(out=CBm[k], in0=pcb.ap[:, 0:128], in1=cs("U"), op=ALU.mult),
                             rd=[pcb, cstT], wr=[CBmT[k]])
                        p.op("pool", lambda e_: e_.tensor_tensor(out=Lh4, in0=cs("SL").unsqueeze(1).broadcast_to([128, 4, 128]),
                                                                 in1=da[:, tt, hs].unsqueeze(2).broadcast_to([128, 4, 128]),
                                                                 op=ALU.mult), rd=[cstT, daT], wr=[Lh4T])
                        pE = banks[2 + k]
                        for hl in range(4):
                            p.op("pe", lambda e_: e_.matmul(pE.ap[:, hl * 128:(hl + 1) * 128], lhsT=Lh4[:, hl, :], rhs=cs("U"),
                                                            start=True, stop=True), rd=[Lh4T, cstT], wr=[pE])
                        p.op("act", lambda e_: e_.activation(out=eE4s[k].rearrange("p h t -> p (h t)"), in_=pE.ap, func=AF.Exp),
                             rd=[pE], wr=[eE4Ts[k]])
                        p.op("dve", lambda e_: e_.tensor_scalar(out=ez[k], in0=ez[k], scalar1=1.0, scalar2=None, op0=ALU.add),
                             rd=[ezT[k]], wr=[ezT[k]])
                        p.op("dve", lambda e_: e_.reciprocal(out=ez[k], in_=ez[k]), rd=[ezT[k]], wr=[ezT[k]])
                        p.op("dve", lambda e_: e_.tensor_tensor(out=sz[tt % 3], in0=ez[k], in1=pz.ap[:, 0:256], op=ALU.mult),
                             rd=[ezT[k], pz], wr=[szT[tt % 3]])

                    def stB(tt):
                        k = tt % 2
                        cl = slice(tt * 128, (tt + 1) * 128)
                        tg = tt // 4
                        p.op("dve", lambda e_: e_.tensor_tensor(out=MT4[k], in0=eE4s[k],
                                                                in1=CBm[k].unsqueeze(1).broadcast_to([128, 4, 128]), op=ALU.mult),
                             rd=[eE4Ts[k], CBmT[k]], wr=[MT4T[k]])
                        p.op("pool", lambda e_: e_.tensor_tensor(out=v4(xdt[k]), in0=v4(xg[:, tt, :]), in1=bc(dt[:, tt, hs]),
                                                                 op=ALU.mult), rd=[xgT[tg], dtT], wr=[xdtT[k]])
                        p.op("pool", lambda e_: e_.tensor_tensor(out=v4(xD[k]), in0=v4(xg[:, tt, :]),
                                                                 in1=bc(pr("dsk%d" % o)[:, hs]), op=ALU.mult),
                             rd=[xgT[tg], prmT], wr=[xDT[k]])
                        py = banks[4 + k]
                        for hl in range(4):
                            p.op("pe", lambda e_: e_.matmul(py.ap[:, hl * 64:(hl + 1) * 64], lhsT=MT4[k][:, hl, :],
                                                            rhs=xdt[k][:, hl * 64:(hl + 1) * 64], start=True, stop=False),
                                 rd=[MT4T[k], xdtT[k]], wr=[py])
                            p.op("pe", lambda e_: e_.matmul(py.ap[:, hl * 64:(hl + 1) * 64], lhsT=identb,
                                                            rhs=xD[k][:, hl * 64:(hl + 1) * 64], start=False, stop=True),
                                 rd=[cbT, xDT[k]], wr=[py])
                        p.op("pe", lambda e_: e_.matmul(py.ap[:, 256:512], lhsT=CT[:, cl], rhs=STall[:, tt, :],
                                                        start=True, stop=True), rd=[CTt[tg], STaT[tt]], wr=[py])

                    def stC(tt):
                        k = tt % 2
                        py = banks[4 + k]
                        p.op("dve", lambda e_: e_.tensor_tensor(out=v4(t1[k]), in0=v4(py.ap[:, 256:512]), in1=bc(ecum[:, tt, :]),
                                                                op=ALU.mult), rd=[py, ecumT], wr=[t1T[k]])
                        p.op("dve", lambda e_: e_.tensor_tensor(out=t1[k], in0=t1[k], in1=py.ap[:, 0:256], op=ALU.add),
                             rd=[t1T[k], py], wr=[t1T[k]])
                        p.op("dve", lambda e_: e_.tensor_tensor(out=t1[k], in0=t1[k], in1=sz[tt % 3], op=ALU.mult),
                             rd=[t1T[k], szT[tt % 3]], wr=[t1T[k]])
                        p.op("act", lambda e_: e_.activation(out=t2[k], in_=t1[k], func=AF.Square, accum_out=st[k][:, 0:1]),
                             rd=[t1T[k]], wr=[t2T[k], stT[k]])
                        rstd_chain(st[k][:, 0:1], st[k][:, 1:2], st[k][:, 2:3], 256, stT[k])
                        p.op("act", lambda e_: e_.activation(out=yn[k], in_=t1[k], func=AF.Copy, scale=st[k][:, 2:3]),
                             rd=[t1T[k], stT[k]], wr=[ynT_[k]])

                    def stD(tt):
                        nonlocal n
                        k = tt % 2
                        ptr = banks[7]
                        ptv = ptr.ap.bitcast(BF16)[:, 512:768].rearrange("p (j c) -> p j c", c=128)
                        for j in range(2):
                            p.op("pe", lambda e_: e_.transpose(out=ptv[:, j, :], in_=yn[k][:, j * 128:(j + 1) * 128],
                                                               identity=identb), rd=[ynT_[k], cbT], wr=[ptr])
                        p.op("dve", lambda e_: e_.tensor_tensor(out=ynT[k], in0=ptv,
                                                                in1=pr("snorm%d" % o)[:, 2 * g:2 * g + 2].unsqueeze(2).broadcast_to([128, 2, 128]),
                                                                op=ALU.mult), rd=[ptr, prmT], wr=[ynTT[k]])
                        for hf in range(2):
                            pst = banks[n % 2]
                            n += 1
                            for j in range(2):
                                p.op("pe", lambda e_: e_.matmul(pst.ap, lhsT=ynT[k][:, j, :], rhs=wo[:, j, hf * 512:(hf + 1) * 512],
                                                                start=(j == 0), stop=(j == 1)), rd=[ynTT[k], WO], wr=[pst])
                            add_h(tt, hf, pst, pst.ap)

                    for j in range(-1, NT + 2):
                        if 0 <= j + 1 < NT:
                            stA(j + 1)
                        if 0 <= j < NT:
                            stB(j)
                        if 0 <= j - 1 < NT:
                            stC(j - 1)
                        if 0 <= j - 2 < NT:
                            stD(j - 2)


    out_events = []
    for s_ in range(NSEQ):
        for tt in range(NT):
            p.dma("sp", h_sb[:, tt, :], x_d[s_, tt * 128:(tt + 1) * 128, :], wr=hT[tt])
        with p.phase() as sb:
            memraw = sb("memraw", [128, 2, D], F32)
            memrawT = [Tl(), Tl()]
            for kb in range(2):
                p.dma("sp", memraw[:, kb, :], mem_d[s_, kb * 128:(kb + 1) * 128, :], wr=[memrawT[kb]])
            rmsnorm_T(sb, 2, lambda tt: ([memrawT[tt]], memraw[:, tt, :]), pr("mem_norm"),
                      lambda tt: (memT, mem_sb[:, :, tt * 128:(tt + 1) * 128]), (6, 7))
        ei = oi = 0
        for l, kd in enumerate(kinds):
            if "mix" in flags:
                if kd == "e":
                    even_mixer(l, ei)
                else:
                    ssd_mixer(l, oi)
            if kd == "e":
                ei += 1
            else:
                oi += 1
            if "xa" in flags:
                xattn(l)
            if "mlp" in flags:
                mlp(l)
        with p.phase() as sb:
            gfin = sb("gfin", [128, D], F32)
            gfT = Tl()
            outb = [sb("outb", [128, D], F32) for i in range(2)]
            outT = [Tl(), Tl()]
            dg = sb("dg", [128, 128], F32)
            dgT = Tl()
            for kc in range(8):
                p.op("dve", lambda e: e.tensor_scalar(out=dg, in0=cs("ident"),
                                                      scalar1=pr("norm_final")[:, kc:kc + 1], scalar2=None,
                                                      op0=ALU.mult), rd=[cstT, prmT], wr=[dgT])
                pst = banks[kc % 2]
                p.op("pe", lambda e: e.matmul(pst.ap[:, 0:128], lhsT=cs("ones"), rhs=dg, start=True, stop=True),
                     rd=[cstT, dgT], wr=[pst])
                p.op("act", lambda e: e.activation(out=gfin[:, kc * 128:(kc + 1) * 128], in_=pst.ap[:, 0:128],
                                                   func=AF.Copy), rd=[pst], wr=[gfT])
            junk = sb("junkf", [128, D], BF16)
            junkT = Tl()
            st = sb("stf", [128, 8], F32)
            stT = [Tl(), Tl()]
            for tt in range(NT):
                k = tt % 2
                ap = h_sb[:, tt, :]
                ss, sd, rs = st[:, 4 * k:4 * k + 1], st[:, 4 * k + 1:4 * k + 2], st[:, 4 * k + 2:4 * k + 3]
                p.op("act", lambda e: e.activation(out=junk, in_=ap, func=AF.Square, accum_out=ss),
                     rd=hT[tt], wr=[junkT, stT[k]])
                rstd_chain(ss, sd, rs, D, stT[k])
                p.op("dve", lambda e: e.scalar_tensor_tensor(out=outb[k], in0=ap, scalar=rs, in1=gfin,
                                                             op0=ALU.mult, op1=ALU.mult),
                     rd=list(hT[tt]) + [stT[k], gfT], wr=[outT[k]])
                out_events.append(p.dma("sp", y_d[s_, tt * 128:(tt + 1) * 128, :], outb[k], rd=[outT[k]]))
    for key, val in out_events[-(NDS // 2):]:
        p.need("sp", key, val)
    C.nins = p.nins
    return nc, C


KINDS = ("e", "o", "e", "o")
NCORES = 8


def run(inp, S, NSEQ, kinds, flags=("mix", "xa", "mlp")):
    nc, C = build(S, NSEQ, kinds, flags)
    cst = make_consts()
    prm = make_params(kinds, inp)
    names = ["xa_q", "xa_kv", "xa_o", "mlp_up", "mlp_down"]
    if "e" in kinds:
        names += ["ev_in_proj", "ev_out_proj"]
    if "o" in kinds:
        names += ["ssm_in_proj", "ssm_out_proj"]
    shared = {k: np.ascontiguousarray(np.asarray(inp[k], np.float32)) for k in names}
    x = np.asarray(inp["x"], np.float32)
    mem = np.asarray(inp["mem"], np.float32)
    in_maps = []
    for c in range(NCORES):
        d = dict(shared)
        d["x"] = np.ascontiguousarray(x[c * NSEQ:(c + 1) * NSEQ])
        d["mem"] = np.ascontiguousarray(mem[c * NSEQ:(c + 1) * NSEQ])
        d["cst"] = cst
        d["prm"] = prm
        in_maps.append(d)
    res = run_bass_kernel_spmd(nc, in_maps, core_ids=list(range(NCORES)))
    return np.concatenate([np.asarray(r["y"]) for r in res.results], axis=0)


def kernel(**inputs):
    x = inputs["x"]
    B, S, _ = x.shape
    return run(inputs, S, B // NCORES, KINDS).astype(np.float32)
```
